# Optimizing a Trainium2 kernel written in Bass

```python
import math
import jax
import jax.numpy as jnp
from jax import lax
import numpy as np

D_MODEL = 2048
BATCH = 4
SEQ = 4096
DEPTH = 4

N_META = 16
N_HEADS = 8
HEAD_DIM = 128
V_DIM = 2 * HEAD_DIM
ROT_DIM = HEAD_DIM // 4
ROPE_THETA = 500000.0
Q_BLOCK = 128
ATTN_QK = N_HEADS * 2 * HEAD_DIM
ATTN_V = N_HEADS * V_DIM
LRU_WIDTH = 5 * D_MODEL // 4
LRU_BLOCKS = 16
LRU_BLOCK_W = LRU_WIDTH // LRU_BLOCKS
CONV_W = 4
LRU_C = 8.0
D_FF = ((8 * D_MODEL // 3 + 255) // 256) * 256
N_IN = 2 * ATTN_QK + ATTN_V + 2 * LRU_WIDTH + 2 * D_MODEL
EPS = 1e-6

kernel_name = 'hybrid_diffattn_rglru_macaron_meta'


def rms_norm(x, g):
    xf = x.astype(jnp.float32)
    y = xf * lax.rsqrt(jnp.mean(xf * xf, axis=-1, keepdims=True) + EPS)
    return (y * g.astype(jnp.float32)).astype(x.dtype)


def swiglu(u, w_gate, w_up, w_down):
    return (jax.nn.silu(u @ w_gate) * (u @ w_up)) @ w_down


def rotary_tables(t):
    inv_freq = ROPE_THETA ** (-jnp.arange(0, ROT_DIM, 2, dtype=jnp.float32) / ROT_DIM)
    ang = jnp.arange(t, dtype=jnp.float32)[:, None] * inv_freq[None, :]
    return jnp.cos(ang), jnp.sin(ang)


def apply_partial_rope(x, cos, sin):
    half = ROT_DIM // 2
    xr = x[..., :ROT_DIM].astype(jnp.float32)
    x1, x2 = xr[..., :half], xr[..., half:]
    c = cos[None, :, None, None, :]
    s = sin[None, :, None, None, :]
    rot = jnp.concatenate([x1 * c - x2 * s, x2 * c + x1 * s], axis=-1).astype(x.dtype)
    return jnp.concatenate([rot, x[..., ROT_DIM:]], axis=-1)


def diff_attention(q, k, v, lam, lam_init, subln, cos, sin):
    b, t = q.shape[0], q.shape[1]
    q = apply_partial_rope(q.reshape(b, t, N_HEADS, 2, HEAD_DIM), cos, sin)
    k = apply_partial_rope(k.reshape(b, t, N_HEADS, 2, HEAD_DIM), cos, sin)
    qf = q.astype(jnp.float32) * (HEAD_DIM ** -0.5)
    kf = k.astype(jnp.float32)
    vf = v.reshape(b, t, N_HEADS, V_DIM).astype(jnp.float32)
    starts = [0] + list(range(N_META, t, Q_BLOCK))
    ends = starts[1:] + [t]
    outs = []
    for s, e in zip(starts, ends):
        sc = jnp.einsum('bqhcd,bkhcd->bhcqk', qf[:, s:e], kf[:, :e])
        causal = jnp.arange(e)[None, :] <= jnp.arange(s, e)[:, None]
        p = jax.nn.softmax(jnp.where(causal, sc, -jnp.inf), axis=-1)
        p = p[:, :, 0] - lam * p[:, :, 1]
        outs.append(jnp.einsum('bhqk,bkhe->bqhe', p, vf[:, :e]))
    o = jnp.concatenate(outs, axis=1)
    o = o * lax.rsqrt(jnp.mean(o * o, axis=-1, keepdims=True) + EPS)
    o = o * subln.astype(jnp.float32) * (1.0 - lam_init)
    return o.reshape(b, t, ATTN_V).astype(v.dtype)


def _lru_combine(e1, e2):
    a1, b1 = e1
    a2, b2 = e2
    return a1 * a2, a2 * b1 + b2


def rglru_branch(xr, yr, conv_w, conv_b, gx_w, gx_b, ga_w, ga_b, a_param):
    b, t, _ = xr.shape
    xp = jnp.pad(xr, ((0, 0), (CONV_W - 1, 0), (0, 0)))
    xc = conv_b + sum(xp[:, CONV_W - 1 - j: CONV_W - 1 - j + t] * conv_w[j] for j in range(CONV_W))
    xb = xc.reshape(b, t, LRU_BLOCKS, LRU_BLOCK_W)
    r = jax.nn.sigmoid(jnp.einsum('btnk,nkj->btnj', xb, ga_w).reshape(b, t, LRU_WIDTH) + ga_b)
    i = jax.nn.sigmoid(jnp.einsum('btnk,nkj->btnj', xb, gx_w).reshape(b, t, LRU_WIDTH) + gx_b)
    log_a = LRU_C * r.astype(jnp.float32) * jax.nn.log_sigmoid(a_param.astype(jnp.float32))
    a = jnp.exp(log_a)
    mult = jnp.sqrt(-jnp.expm1(2.0 * log_a))
    bx = mult * (i * xc).astype(jnp.float32)
    _, h = lax.associative_scan(_lru_combine, (a, bx), axis=1)
    return h.astype(xr.dtype) * jax.nn.gelu(yr)


def hybrid_mixer(u, w_in, b_gate, lq1, lk1, lq2, lk2, subln, conv_w, conv_b,
                 gx_w, gx_b, ga_w, ga_b, a_param, w_ba, w_bl, w_o, cos, sin, lam_init):
    z = u @ w_in
    cuts = [ATTN_QK, 2 * ATTN_QK, 2 * ATTN_QK + ATTN_V,
            2 * ATTN_QK + ATTN_V + LRU_WIDTH, 2 * ATTN_QK + ATTN_V + 2 * LRU_WIDTH]
    q, k, v, xr, yr, zg = jnp.split(z, cuts, axis=-1)
    g = jax.nn.sigmoid(zg + b_gate)
    g_attn, g_lru = g[..., :D_MODEL], g[..., D_MODEL:]
    lam = (jnp.exp(jnp.sum(lq1.astype(jnp.float32) * lk1.astype(jnp.float32)))
           - jnp.exp(jnp.sum(lq2.astype(jnp.float32) * lk2.astype(jnp.float32))) + lam_init)
    o_attn = diff_attention(q, k, v, lam, lam_init, subln, cos, sin) @ w_ba
    o_lru = rglru_branch(xr, yr, conv_w, conv_b, gx_w, gx_b, ga_w, ga_b, a_param) @ w_bl
    return (g_attn * o_attn + g_lru * o_lru) @ w_o


def setup_inputs(seed: int = 0) -> dict:
    key = jax.random.key(seed)
    ks = iter(jax.random.split(key, 40))

    def w(shape, fan_in):
        return jax.random.normal(next(ks), shape, jnp.float32) * (fan_in ** -0.5)

    def gain(shape):
        return 1.0 + 0.02 * jax.random.normal(next(ks), shape, jnp.float32)

    def small(shape, scale=0.01):
        return scale * jax.random.normal(next(ks), shape, jnp.float32)

    x = jax.random.normal(next(ks), (BATCH, SEQ, D_MODEL), jnp.float32)
    meta_tokens = jax.random.normal(next(ks), (N_META, D_MODEL), jnp.float32)
    u = jax.random.uniform(next(ks), (DEPTH, LRU_WIDTH), jnp.float32, 0.9, 0.999)
    p = u ** (1.0 / LRU_C)
    lru_a_param = jnp.log(p) - jnp.log1p(-p)
    return {
        'x': x,
        'meta_tokens': meta_tokens,
        'norm_ffn1': gain((DEPTH, D_MODEL)),
        'ffn1_w_gate': w((DEPTH, D_MODEL, D_FF), D_MODEL),
        'ffn1_w_up': w((DEPTH, D_MODEL, D_FF), D_MODEL),
        'ffn1_w_down': w((DEPTH, D_FF, D_MODEL), D_FF),
        'norm_mix': gain((DEPTH, D_MODEL)),
        'w_in': w((DEPTH, D_MODEL, N_IN), D_MODEL),
        'b_gate': small((DEPTH, 2 * D_MODEL)),
        'lambda_q1': small((DEPTH, HEAD_DIM), 0.1),
        'lambda_k1': small((DEPTH, HEAD_DIM), 0.1),
        'lambda_q2': small((DEPTH, HEAD_DIM), 0.1),
        'lambda_k2': small((DEPTH, HEAD_DIM), 0.1),
        'attn_subln': gain((DEPTH, V_DIM)),
        'conv_w': w((DEPTH, CONV_W, LRU_WIDTH), CONV_W),
        'conv_b': small((DEPTH, LRU_WIDTH)),
        'gate_x_w': w((DEPTH, LRU_BLOCKS, LRU_BLOCK_W, LRU_BLOCK_W), LRU_BLOCK_W),
        'gate_x_b': small((DEPTH, LRU_WIDTH)),
        'gate_a_w': w((DEPTH, LRU_BLOCKS, LRU_BLOCK_W, LRU_BLOCK_W), LRU_BLOCK_W),
        'gate_a_b': small((DEPTH, LRU_WIDTH)),
        'lru_a_param': lru_a_param,
        'w_branch_attn': w((DEPTH, ATTN_V, D_MODEL), ATTN_V),
        'w_branch_lru': w((DEPTH, LRU_WIDTH, D_MODEL), LRU_WIDTH),
        'w_out': w((DEPTH, D_MODEL, D_MODEL), D_MODEL),
        'norm_ffn2': gain((DEPTH, D_MODEL)),
        'ffn2_w_gate': w((DEPTH, D_MODEL, D_FF), D_MODEL),
        'ffn2_w_up': w((DEPTH, D_MODEL, D_FF), D_MODEL),
        'ffn2_w_down': w((DEPTH, D_FF, D_MODEL), D_FF),
        'final_norm': gain((D_MODEL,)),
    }


def reference(x, meta_tokens, norm_ffn1, ffn1_w_gate, ffn1_w_up, ffn1_w_down, norm_mix,
              w_in, b_gate, lambda_q1, lambda_k1, lambda_q2, lambda_k2, attn_subln,
              conv_w, conv_b, gate_x_w, gate_x_b, gate_a_w, gate_a_b, lru_a_param,
              w_branch_attn, w_branch_lru, w_out, norm_ffn2, ffn2_w_gate, ffn2_w_up,
              ffn2_w_down, final_norm):
    b = x.shape[0]
    meta = jnp.broadcast_to(meta_tokens.astype(x.dtype)[None], (b, N_META, D_MODEL))
    h = jnp.concatenate([meta, x], axis=1)
    cos, sin = rotary_tables(h.shape[1])
    for l in range(DEPTH):
        lam_init = 0.8 - 0.6 * math.exp(-0.3 * l)
        h = h + 0.5 * swiglu(rms_norm(h, norm_ffn1[l]), ffn1_w_gate[l], ffn1_w_up[l], ffn1_w_down[l])
        h = h + hybrid_mixer(rms_norm(h, norm_mix[l]), w_in[l], b_gate[l],
                             lambda_q1[l], lambda_k1[l], lambda_q2[l], lambda_k2[l], attn_subln[l],
                             conv_w[l], conv_b[l], gate_x_w[l], gate_x_b[l], gate_a_w[l], gate_a_b[l],
                             lru_a_param[l], w_branch_attn[l], w_branch_lru[l], w_out[l],
                             cos, sin, lam_init)
        h = h + 0.5 * swiglu(rms_norm(h, norm_ffn2[l]), ffn2_w_gate[l], ffn2_w_up[l], ffn2_w_down[l])
    return rms_norm(h, final_norm)[:, N_META:]
```

```python
import math
from contextlib import ExitStack

import numpy as np
import concourse.bass as bass
import concourse.mybir as mybir
from concourse.bass_utils import run_bass_kernel_spmd

F32 = mybir.dt.float32
BF16 = mybir.dt.bfloat16
AF = mybir.ActivationFunctionType
ALU = mybir.AluOpType

D = 2048
KD = 16
FF = 5632
KF = 44
NH = 8
HD = 128
VD = 256
ROT = 32
LW = 2560
KL = 20
NIN = 15360
NMETA = 16
EPS = 1e-6
THETA = 500000.0

ENGS = ("pe", "act", "dve", "pool", "sp")


class Res:
    __slots__ = ("w", "r")

    def __init__(self):
        self.w = []
        self.r = []


class Sched:
    def __init__(self, nc):
        self.nc = nc
        self.ops = {e: [] for e in ENGS}
        self.cnt = {}
        self.handles = {}
        self.waited = {e: {} for e in ENGS}
        self.res = {}
        self.stack = ExitStack()
        for e in ("pe", "act", "dve", "pool"):
            self.new_sem("c_" + e)

    def new_sem(self, name):
        if name not in self.handles:
            self.handles[name] = self.stack.enter_context(self.nc.semaphore(name))
            self.cnt[name] = 0
        return name

    def R(self, *key):
        r = self.res.get(key)
        if r is None:
            r = self.res[key] = Res()
        return r

    def _deps(self, eng, reads, writes):
        need = {}
        for r in reads:
            for (s, v) in r.w:
                need[s] = max(need.get(s, 0), v)
        for w in writes:
            for (s, v) in w.w:
                need[s] = max(need.get(s, 0), v)
            for (s, v) in w.r:
                need[s] = max(need.get(s, 0), v)
        waits = []
        for s, v in need.items():
            if not s.startswith("c_"):
                v = self.cnt[s]
            if eng == "pe" and s == "c_pe":
                continue
            if self.waited[eng].get(s, 0) >= v:
                continue
            self.waited[eng][s] = v
            waits.append((s, v))
        return waits

    def _commit(self, tok, reads, writes):
        for r in reads:
            r.r.append(tok)
            if len(r.r) > 24:
                m = {}
                for (s, v) in r.r:
                    m[s] = max(m.get(s, 0), v)
                r.r = list(m.items())
        for w in writes:
            w.w = [tok]
            w.r = []

    def op(self, eng, fn, reads=(), writes=()):
        waits = self._deps(eng, reads, writes)
        s = "c_" + eng
        self.cnt[s] += 1
        tok = (s, self.cnt[s])
        self.ops[eng].append((waits, fn, s, 1))
        self._commit(tok, reads, writes)

    def dma(self, q, fn, sem, reads=(), writes=(), accumulate_writers=False):
        waits = self._deps(q, reads, writes)
        self.new_sem(sem)
        self.cnt[sem] += 16
        tok = (sem, self.cnt[sem])
        self.ops[q].append((waits, fn, sem, 16))
        if accumulate_writers:
            for r in reads:
                r.r.append(tok)
            for w in writes:
                w.w.append(tok)
        else:
            self._commit(tok, reads, writes)

    def barrier(self):
        for e in ENGS:
            waits = []
            for s, v in self.cnt.items():
                if v == 0 or (e == "pe" and s == "c_pe"):
                    continue
                if s == "c_" + e:
                    continue
                if self.waited[e].get(s, 0) >= v:
                    continue
                self.waited[e][s] = v
                waits.append((s, v))
            if waits:
                self.ops[e].append((waits, None, None, 0))

    def emit(self):
        nc = self.nc
        H = self.handles

        def run(eng, lst):
            for (waits, fn, s, inc) in lst:
                for (ws, wv) in waits:
                    eng.wait_ge(H[ws], wv)
                if fn is not None:
                    ins = fn(eng)
                    ins.then_inc(H[s], inc)

        with nc.Block() as block:
            @block.tensor
            def _(e):
                run(e, self.ops["pe"])

            @block.scalar
            def _(e):
                run(e, self.ops["act"])

            @block.vector
            def _(e):
                run(e, self.ops["dve"])

            @block.gpsimd
            def _(e):
                run(e, self.ops["pool"])

            @block.sync
            def _(e):
                run(e, self.ops["sp"])


def tiles_of(T, step):
    out = []
    s = 0
    while s < T:
        out.append((s, min(step, T - s)))
        s += step
    return out


def super_tiles(T):
    tt = tiles_of(T, 512)
    groups = []
    i = 0
    while i < len(tt):
        g = tt[i:i + 2]
        i += 2
        groups.append(g)
    if len(groups) > 1 and sum(n for _, n in groups[-1]) < 128:
        last = groups.pop()
        groups[-1] = groups[-1] + last
    return groups


def lam_init_of(l):
    return 0.8 - 0.6 * math.exp(-0.3 * l)


GATE_PAIRS = sorted({(ci, co) for b in range(4) for ci in (b, b + 1) for co in (b, b + 1)})
GATE_NBRS = {co: [ci for (ci, c2) in GATE_PAIRS if c2 == co] for co in range(5)}


def build(T, DEPTH, debug=False):
    nc = bass.Bass("TRN2", target_bir_lowering=False)
    S = Sched(nc)
    TT = tiles_of(T, 512)
    TB = tiles_of(T, 128)
    NB = len(TB)
    ST = super_tiles(T)
    STMAX = max(sum(n for _, n in g) for g in ST)

    def din(name, shape, dt=F32):
        return nc.dram_tensor(name, list(shape), dt, kind="ExternalInput").ap()

    dbg_kind = "ExternalOutput" if debug else "Internal"

    def dscr(name, shape, dt):
        return nc.dram_tensor(name, list(shape), dt, kind=dbg_kind).ap()

    h0T = din("h0T", [D, T])
    norm_ffn1 = din("norm_ffn1", [DEPTH, D])
    ffn1_wg = din("ffn1_w_gate", [DEPTH, D, FF])
    ffn1_wu = din("ffn1_w_up", [DEPTH, D, FF])
    ffn1_wd = din("ffn1_w_down", [DEPTH, FF, D])
    norm_mix = din("norm_mix", [DEPTH, D])
    w_in = din("w_in", [DEPTH, D, NIN])
    b_gate = din("b_gate", [DEPTH, 2 * D])
    lam_q1 = din("lambda_q1", [DEPTH, HD])
    lam_k1 = din("lambda_k1", [DEPTH, HD])
    lam_q2 = din("lambda_q2", [DEPTH, HD])
    lam_k2 = din("lambda_k2", [DEPTH, HD])
    attn_subln = din("attn_subln", [DEPTH, VD])
    conv_w = din("conv_w", [DEPTH, 4, LW])
    conv_b = din("conv_b", [DEPTH, LW])
    gate_x_w = din("gate_x_w", [DEPTH, 16, 160, 160])
    gate_x_b = din("gate_x_b", [DEPTH, LW])
    gate_a_w = din("gate_a_w", [DEPTH, 16, 160, 160])
    gate_a_b = din("gate_a_b", [DEPTH, LW])
    lru_a = din("lru_a_param", [DEPTH, LW])
    w_ba = din("w_branch_attn", [DEPTH, D, D])
    w_bl = din("w_branch_lru", [DEPTH, LW, D])
    w_o = din("w_out", [DEPTH, D, D])
    norm_ffn2 = din("norm_ffn2", [DEPTH, D])
    ffn2_wg = din("ffn2_w_gate", [DEPTH, D, FF])
    ffn2_wu = din("ffn2_w_up", [DEPTH, D, FF])
    ffn2_wd = din("ffn2_w_down", [DEPTH, FF, D])
    final_norm = din("final_norm", [1, D])
    c_cos = din("c_cos", [NB * 128, 64])
    c_sin = din("c_sin", [NB * 128, 64])
    c_mask = din("c_mask", [128, 128], BF16)
    c_ident = din("c_ident", [128, 128], BF16)
    c_laminit = din("c_laminit", [DEPTH, 2])

    outT = nc.dram_tensor("outT", [D, T], F32, kind="ExternalOutput").ap()

    hT = dscr("hT", [D, T], F32)
    qT_d = dscr("qT_d", [16, 128, T], BF16)
    kT_d = dscr("kT_d", [16, 128, T], BF16)
    v_d = dscr("v_d", [NB * 128, D], BF16)
    xrT_d = dscr("xrT_d", [LW, T], F32)
    yrT_d = dscr("yrT_d", [LW, T], F32)
    gT_d = dscr("gT_d", [2 * D, T], BF16)
    aoT_d = dscr("aoT_d", [D, T], BF16)
    hyT_d = dscr("hyT_d", [LW, T], BF16)

    with S.stack:
        es = S.stack

        uid = [0]

        def sb(name, shape, dt, st=None):
            uid[0] += 1
            return (st or es).enter_context(nc.sbuf_tensor(f"{name}_{uid[0]}", list(shape), dt))

        def ps(name, shape, dt, st):
            uid[0] += 1
            return st.enter_context(nc.psum_tensor(f"{name}_{uid[0]}", list(shape), dt))

        ones_bf = sb("ones_bf", [128, 128], BF16)
        ident = sb("ident", [128, 128], BF16)
        maskt = sb("maskt", [128, 128], BF16)
        eps_t = sb("eps_t", [128, 1], F32)
        one_t = sb("one_t", [128, 1], F32)
        cos_t = sb("cos_t", [128, NB, 64], F32)
        sin_t = sb("sin_t", [128, NB, 64], F32)
        R_const = S.R("const")
        S.op("dve", lambda e: e.memset(ones_bf[:], 1.0), writes=[R_const])
        S.op("dve", lambda e: e.memset(eps_t[:], EPS), writes=[R_const])
        S.op("dve", lambda e: e.memset(one_t[:], 1.0), writes=[R_const])
        S.dma("sp", lambda e: e.dma_start(out=ident[:], in_=c_ident[:, :]), "ld_c", writes=[R_const],
              accumulate_writers=True)
        S.dma("sp", lambda e: e.dma_start(out=maskt[:], in_=c_mask[:, :]), "ld_c", writes=[R_const],
              accumulate_writers=True)
        S.dma("sp", lambda e: e.dma_start(out=cos_t[:], in_=c_cos.rearrange("(b p) c -> p b c", p=128)),
              "ld_c", writes=[R_const], accumulate_writers=True)
        S.dma("sp", lambda e: e.dma_start(out=sin_t[:], in_=c_sin.rearrange("(b p) c -> p b c", p=128)),
              "ld_c", writes=[R_const], accumulate_writers=True)

        def Rh(kc, ts):
            return S.R("h", kc, ts)

        def Rhr(hin, kc, ts):
            return [Rh(kc, ts)] if hin is hT else []

        def load_cols(dst, src_row, n, sem, res, q="sp"):
            S.dma(q, lambda e: e.dma_start(out=dst, in_=src_row.rearrange("(c p) -> p c", p=128),
                                           allow_slow_non_contiguous=True), sem, writes=[res])

        def wload(dst, W2d, kcn, col0, ncols, sem, res):
            Wv = W2d.rearrange("(kc p) n -> p kc n", p=128)
            first = True
            for k0 in range(0, kcn, 4):
                k1 = min(kcn, k0 + 4)
                S.dma("pool", lambda e, k0=k0, k1=k1: e.dma_start(
                    out=dst[:, k0:k1, 0:ncols], in_=Wv[:, k0:k1, col0:col0 + ncols]), sem,
                    writes=[res], accumulate_writers=not first)
                first = False

        def rms_to_uT(st, stoks, gain_row, uT, R_uT, tag, out_dram=None, hin=None):
            if hin is None:
                hin = hT
            with ExitStack() as ls:
                gcol = sb(f"gcol_{tag}", [128, KD], F32, ls)
                hb = [sb(f"hb{i}_{tag}", [128, 512], F32, ls) for i in range(3)]
                sq = [sb(f"sq{i}_{tag}", [128, 512], BF16, ls) for i in range(2)]
                rt = sb(f"rt_{tag}", [128, 512], F32, ls)
                rstd = sb(f"rstd_{tag}", [128, 512], F32, ls)
                ob = [sb(f"ob{i}_{tag}", [128, 512], F32, ls) for i in range(2)] if out_dram is not None else None
                pss = ps(f"pss_{tag}", [128, 512], F32, ls)
                R_g = S.R("gcol", tag)
                load_cols(gcol[:, :], gain_row, KD, "ld_small", R_g)
                R_hb = [S.R("hb", tag, i) for i in range(3)]
                R_sq = [S.R("sq", tag, i) for i in range(2)]
                R_pss = S.R("pss", tag)
                R_rt = S.R("rt", tag)
                R_rstd = S.R("rstd", tag)
                R_ob = [S.R("ob", tag, i) for i in range(2)]
                n_hb = 0
                off = 0
                for (ts, tn) in stoks:
                    for kc in range(KD):
                        b = n_hb % 3
                        n_hb += 1
                        S.dma("sp", lambda e, b=b, kc=kc, ts=ts, tn=tn: e.dma_start(
                            out=hb[b][:, :tn], in_=hin[kc * 128:(kc + 1) * 128, ts:ts + tn]),
                            f"ld_hb{b}", reads=Rhr(hin, kc, ts), writes=[R_hb[b]])
                        s2 = kc % 2
                        S.op("act", lambda e, b=b, s2=s2, tn=tn: e.activation(
                            out=sq[s2][:, :tn], in_=hb[b][:, :tn], func=AF.Square),
                            reads=[R_hb[b]], writes=[R_sq[s2]])
                        S.op("pe", lambda e, s2=s2, tn=tn, kc=kc: e.matmul(
                            pss[:, :tn], ones_bf[:, :], sq[s2][:, :tn], start=(kc == 0), stop=(kc == KD - 1)),
                            reads=[R_sq[s2], R_const], writes=[R_pss])
                    S.op("act", lambda e, tn=tn: e.activation(
                        out=rt[:, :tn], in_=pss[:, :tn], func=AF.Sqrt, bias=eps_t[:, :], scale=1.0 / D),
                        reads=[R_pss, R_const], writes=[R_rt])
                    S.op("dve", lambda e, tn=tn: e.reciprocal(out=rstd[:, :tn], in_=rt[:, :tn]),
                         reads=[R_rt], writes=[R_rstd])
                    for kc in range(KD):
                        b = n_hb % 3
                        n_hb += 1
                        S.dma("sp", lambda e, b=b, kc=kc, ts=ts, tn=tn: e.dma_start(
                            out=hb[b][:, :tn], in_=hin[kc * 128:(kc + 1) * 128, ts:ts + tn]),
                            f"ld_hb{b}", reads=Rhr(hin, kc, ts), writes=[R_hb[b]])
                        if out_dram is None:
                            S.op("dve", lambda e, b=b, kc=kc, tn=tn, off=off: e.scalar_tensor_tensor(
                                out=uT[:, kc, off:off + tn], in0=hb[b][:, :tn], scalar=gcol[:, kc:kc + 1],
                                in1=rstd[:, :tn], op0=ALU.mult, op1=ALU.mult),
                                reads=[R_hb[b], R_g, R_rstd], writes=[R_uT])
                        else:
                            o2 = kc % 2
                            S.op("dve", lambda e, b=b, kc=kc, tn=tn, o2=o2: e.scalar_tensor_tensor(
                                out=ob[o2][:, :tn], in0=hb[b][:, :tn], scalar=gcol[:, kc:kc + 1],
                                in1=rstd[:, :tn], op0=ALU.mult, op1=ALU.mult),
                                reads=[R_hb[b], R_g, R_rstd], writes=[R_ob[o2]])
                            S.dma("sp", lambda e, kc=kc, ts=ts, tn=tn, o2=o2: e.dma_start(
                                out=out_dram[kc * 128:(kc + 1) * 128, ts:ts + tn], in_=ob[o2][:, :tn]),
                                f"st_ob{o2}", reads=[R_ob[o2]], writes=[S.R("out", kc, ts)])
                    off += tn
                S.barrier()

        def ffn(l, gain, wg, wu, wd, tag, hin=None):
            if hin is None:
                hin = hT
            for si, stoks in enumerate(ST):
                ffn_body(l, gain, wg, wu, wd, tag, hin, si, stoks)

        def ffn_body(l, gain, wg, wu, wd, tag, hin, si, stoks):
            if True:
                ntok = sum(n for _, n in stoks)
                with ExitStack() as ls:
                    aT = sb(f"aT_{tag}", [128, KF, STMAX], BF16, ls)
                    R_aT = S.R("aT", tag, si)
                    with ExitStack() as l1:
                        uT = sb(f"uT_{tag}", [128, KD, STMAX], BF16, l1)
                        R_uT = S.R("uT", tag, si)
                        rms_to_uT(l1, stoks, gain[l, :], uT, R_uT, f"{tag}n", hin=hin)
                        wgt = [sb(f"wg{i}_{tag}", [128, KD, 256], BF16, l1) for i in range(2)]
                        wut = [sb(f"wu{i}_{tag}", [128, KD, 256], BF16, l1) for i in range(2)]
                        sg = [sb(f"sg{i}_{tag}", [128, 512], BF16, l1) for i in range(2)]
                        psg = [ps(f"psg{i}_{tag}", [128, 512], F32, l1) for i in range(2)]
                        psu = [ps(f"psu{i}_{tag}", [128, 512], F32, l1) for i in range(2)]
                        R_w = [S.R("wgu", tag, i) for i in range(2)]
                        R_sg = [S.R("sg", tag, i) for i in range(2)]
                        R_pg = [S.R("psg", tag, i) for i in range(2)]
                        R_pu = [S.R("psu", tag, i) for i in range(2)]

                        def ldw(m2):
                            s = m2 % 2
                            wload(wgt[s], wg[l], KD, m2 * 256, 256, f"ld_w{s}", R_w[s])
                            Wv = wu[l].rearrange("(kc p) n -> p kc n", p=128)
                            for k0 in range(0, KD, 4):
                                S.dma("pool", lambda e, k0=k0, s=s, m2=m2: e.dma_start(
                                    out=wut[s][:, k0:k0 + 4, :], in_=Wv[:, k0:k0 + 4, m2 * 256:(m2 + 1) * 256]),
                                    f"ld_w{s}", writes=[R_w[s]], accumulate_writers=True)

                        ldw(0)
                        ldw(1)
                        it = 0
                        for m2 in range(KF // 2):
                            s = m2 % 2
                            off = 0
                            for (ts, tn) in stoks:
                                for mm in range(2):
                                    m = 2 * m2 + mm
                                    p = it % 2
                                    it += 1
                                    for kc in range(KD):
                                        S.op("pe", lambda e, p=p, s=s, kc=kc, mm=mm, off=off, tn=tn: e.matmul(
                                            psg[p][:, :tn], wgt[s][:, kc, mm * 128:(mm + 1) * 128],
                                            uT[:, kc, off:off + tn], start=(kc == 0), stop=(kc == KD - 1)),
                                            reads=[R_w[s], R_uT], writes=[R_pg[p]])
                                    for kc in range(KD):
                                        S.op("pe", lambda e, p=p, s=s, kc=kc, mm=mm, off=off, tn=tn: e.matmul(
                                            psu[p][:, :tn], wut[s][:, kc, mm * 128:(mm + 1) * 128],
                                            uT[:, kc, off:off + tn], start=(kc == 0), stop=(kc == KD - 1)),
                                            reads=[R_w[s], R_uT], writes=[R_pu[p]])
                                    S.op("act", lambda e, p=p, tn=tn: e.activation(
                                        out=sg[p][:, :tn], in_=psg[p][:, :tn], func=AF.Silu),
                                        reads=[R_pg[p]], writes=[R_sg[p]])
                                    S.op("dve", lambda e, p=p, m=m, off=off, tn=tn: e.tensor_tensor(
                                        out=aT[:, m, off:off + tn], in0=psu[p][:, :tn], in1=sg[p][:, :tn],
                                        op=ALU.mult), reads=[R_pu[p], R_sg[p]], writes=[R_aT])
                                off += tn
                            if m2 + 2 < KF // 2:
                                ldw(m2 + 2)
                        S.barrier()
                    with ExitStack() as l2:
                        wdt = [sb(f"wd{i}_{tag}", [128, KF, 256], BF16, l2) for i in range(2)]
                        hr = [sb(f"hr{i}_{tag}", [128, 512], F32, l2) for i in range(2)]
                        hn = [sb(f"hn{i}_{tag}", [128, 512], F32, l2) for i in range(2)]
                        pso = [ps(f"pso{i}_{tag}", [128, 512], F32, l2) for i in range(2)]
                        R_wd = [S.R("wd", tag, i) for i in range(2)]
                        R_hr = [S.R("hr", tag, i) for i in range(2)]
                        R_hn = [S.R("hn", tag, i) for i in range(2)]
                        R_po = [S.R("pso", tag, i) for i in range(2)]

                        def ldd(d2):
                            s = d2 % 2
                            wload(wdt[s], wd[l], KF, d2 * 256, 256, f"ld_w{s}", R_wd[s])

                        ldd(0)
                        ldd(1)
                        it = 0
                        for d2 in range(KD // 2):
                            s = d2 % 2
                            off = 0
                            for (ts, tn) in stoks:
                                for dd in range(2):
                                    d = 2 * d2 + dd
                                    p = it % 2
                                    it += 1
                                    S.dma("sp", lambda e, p=p, d=d, ts=ts, tn=tn: e.dma_start(
                                        out=hr[p][:, :tn], in_=hin[d * 128:(d + 1) * 128, ts:ts + tn]),
                                        f"ld_hr{p}", reads=Rhr(hin, d, ts), writes=[R_hr[p]])
                                    for m in range(KF):
                                        S.op("pe", lambda e, p=p, s=s, m=m, dd=dd, off=off, tn=tn: e.matmul(
                                            pso[p][:, :tn], wdt[s][:, m, dd * 128:(dd + 1) * 128],
                                            aT[:, m, off:off + tn], start=(m == 0), stop=(m == KF - 1)),
                                            reads=[R_wd[s], R_aT], writes=[R_po[p]])
                                    S.op("dve", lambda e, p=p, tn=tn: e.scalar_tensor_tensor(
                                        out=hn[p][:, :tn], in0=pso[p][:, :tn], scalar=0.5, in1=hr[p][:, :tn],
                                        op0=ALU.mult, op1=ALU.add), reads=[R_po[p], R_hr[p]], writes=[R_hn[p]])
                                    S.dma("sp", lambda e, p=p, d=d, ts=ts, tn=tn: e.dma_start(
                                        out=hT[d * 128:(d + 1) * 128, ts:ts + tn], in_=hn[p][:, :tn]),
                                        f"st_hn{p}", reads=[R_hn[p]], writes=[Rh(d, ts)])
                                off += tn
                            if d2 + 2 < KD // 2:
                                ldd(d2 + 2)
                        S.barrier()

        def mixer_proj(l):
            for si, stoks in enumerate(ST):
                mixer_proj_body(l, si, stoks)

        def mixer_proj_body(l, si, stoks):
            if True:
                t0 = stoks[0][0]
                ntok = sum(n for _, n in stoks)
                blocks = tiles_of(ntok, 128)
                with ExitStack() as ls:
                    uT = sb("uT_m", [128, KD, STMAX], BF16, ls)
                    R_uT = S.R("uT_m", si)
                    rms_to_uT(ls, stoks, norm_mix[l, :], uT, R_uT, "mn")
                    wt = [sb(f"win{i}", [128, KD, 512], BF16, ls) for i in range(2)]
                    R_w = [S.R("win", i) for i in range(2)]
                    bg = sb("bgcol", [128, 32], F32, ls)
                    R_bg = S.R("bgcol")
                    load_cols(bg[:, :], b_gate[l, :], 32, "ld_small", R_bg)
                    qk_sb = [sb(f"qk_sb{i}", [128, 512], BF16, ls) for i in range(2)]
                    t1 = [sb(f"rt1_{i}", [128, 4, 16], F32, ls) for i in range(2)]
                    t2 = [sb(f"rt2_{i}", [128, 4, 16], F32, ls) for i in range(2)]
                    x12 = [sb(f"x12_{i}", [128, 4, 32], F32, ls) for i in range(2)]
                    stage = sb("qk_stage", [128, 4, STMAX], BF16, ls)
                    vst = [sb(f"vst{i}", [128, 512], BF16, ls) for i in range(2)]
                    fst = [sb(f"fst{i}", [128, 512], F32, ls) for i in range(2)]
                    gst = [sb(f"gst{i}", [128, 512], BF16, ls) for i in range(2)]
                    pm = [ps(f"pm{i}", [128, 512], F32, ls) for i in range(3)]
                    pt = [ps(f"ptr{i}", [128, 512], BF16, ls) for i in range(2)]
                    R_qk = [S.R("qk_sb", i) for i in range(2)]
                    R_t = [S.R("ropet", i) for i in range(2)]
                    R_stage = S.R("qk_stage")
                    R_vst = [S.R("vst", i) for i in range(2)]
                    R_fst = [S.R("fst", i) for i in range(2)]
                    R_gst = [S.R("gst", i) for i in range(2)]
                    R_pm = [S.R("pm", i) for i in range(3)]
                    R_pt = [S.R("ptr", i) for i in range(2)]
                    NG = NIN // 512

                    def ldw(g):
                        wload(wt[g % 2], w_in[l], KD, g * 512, 512, f"ld_w{g % 2}", R_w[g % 2])

                    import os
                    GL = [int(x) for x in os.environ.get("K_GL", ",".join(str(i) for i in range(NG))).split(",")]
                    norope = bool(os.environ.get("K_NOROPE"))
                    notr = bool(os.environ.get("K_NOTR"))
                    for gi_ in range(min(2, len(GL))):
                        wload(wt[gi_ % 2], w_in[l], KD, GL[gi_] * 512, 512, f"ld_w{gi_ % 2}", R_w[gi_ % 2])
                    it = 0
                    itq = 0
                    for gi_, g in enumerate(GL):
                        s = gi_ % 2
                        if g < 12:
                            for (bs, bn) in blocks:
                                p = it % 3
                                it += 1
                                gb = (t0 + bs) // 128
                                for kc in range(KD):
                                    S.op("pe", lambda e, p=p, s=s, kc=kc, bs=bs, bn=bn: e.matmul(
                                        pm[p][:bn, :], uT[:, kc, bs:bs + bn], wt[s][:, kc, :],
                                        start=(kc == 0), stop=(kc == KD - 1)),
                                        reads=[R_w[s], R_uT], writes=[R_pm[p]])
                                if g < 8:
                                    q2 = itq % 2
                                    itq += 1
                                    S.op("act", lambda e, p=p, q2=q2, bn=bn: e.activation(
                                        out=qk_sb[q2][:bn, :], in_=pm[p][:bn, :], func=AF.Copy),
                                        reads=[R_pm[p]], writes=[R_qk[q2]])
                                    pv = pm[p][:, :].rearrange("p (h d) -> p h d", h=4)
                                    qv = qk_sb[q2][:, :].rearrange("p (h d) -> p h d", h=4)
                                    cv = cos_t[:, gb, :].rearrange("p (h d) -> p h d", h=4)
                                    sv = sin_t[:, gb, :].rearrange("p (h d) -> p h d", h=4)
                                    if not norope:
                                        xr32 = x12[q2]
                                        S.op("act", lambda e, bn=bn, pv=pv, xr32=xr32: e.activation(
                                            out=xr32[:bn], in_=pv[:bn, :, 0:32], func=AF.Copy),
                                            reads=[R_pm[p]], writes=[R_t[q2]])
                                        S.op("dve", lambda e, bn=bn, xr32=xr32, cv=cv, q2=q2: e.tensor_tensor(
                                            out=t1[q2][:bn], in0=xr32[:bn, :, 0:16], in1=cv[:bn], op=ALU.mult),
                                            reads=[R_t[q2], R_const], writes=[R_t[q2]])
                                        S.op("dve", lambda e, bn=bn, xr32=xr32, sv=sv, q2=q2: e.tensor_tensor(
                                            out=t2[q2][:bn], in0=xr32[:bn, :, 16:32], in1=sv[:bn], op=ALU.mult),
                                            reads=[R_t[q2], R_const], writes=[R_t[q2]])
                                        S.op("dve", lambda e, bn=bn, qv=qv, q2=q2: e.tensor_tensor(
                                            out=qv[:bn, :, 0:16], in0=t1[q2][:bn], in1=t2[q2][:bn], op=ALU.subtract),
                                            reads=[R_t[q2]], writes=[R_qk[q2]])
                                        S.op("dve", lambda e, bn=bn, xr32=xr32, cv=cv, q2=q2: e.tensor_tensor(
                                            out=t1[q2][:bn], in0=xr32[:bn, :, 16:32], in1=cv[:bn], op=ALU.mult),
                                            reads=[R_t[q2], R_const, R_qk[q2]], writes=[R_t[q2]])
                                        S.op("dve", lambda e, bn=bn, xr32=xr32, sv=sv, q2=q2: e.tensor_tensor(
                                            out=t2[q2][:bn], in0=xr32[:bn, :, 0:16], in1=sv[:bn], op=ALU.mult),
                                            reads=[R_t[q2], R_const], writes=[R_t[q2]])
                                        S.op("dve", lambda e, bn=bn, qv=qv, q2=q2: e.tensor_tensor(
                                            out=qv[:bn, :, 16:32], in0=t1[q2][:bn], in1=t2[q2][:bn], op=ALU.add),
                                            reads=[R_t[q2]], writes=[R_qk[q2]])
                                    if not notr:
                                        for hc in range(4):
                                            S.op("pe", lambda e, q2=q2, hc=hc, bn=bn: e.transpose(
                                                pt[q2][:, hc * 128:hc * 128 + bn], qk_sb[q2][:bn, hc * 128:(hc + 1) * 128],
                                                ident[:bn, :bn]), reads=[R_qk[q2], R_const], writes=[R_pt[q2]])
                                        ptv = pt[q2][:, :].rearrange("p (h t) -> p h t", h=4)
                                        S.op("act", lambda e, ptv=ptv, bs=bs, bn=bn: e.activation(
                                            out=stage[:, :, bs:bs + bn], in_=ptv[:, :, :bn], func=AF.Copy),
                                            reads=[R_pt[q2]], writes=[R_stage])
                                else:
                                    v2 = it % 2
                                    S.op("act", lambda e, p=p, v2=v2, bn=bn: e.activation(
                                        out=vst[v2][:bn, :], in_=pm[p][:bn, :], func=AF.Copy),
                                        reads=[R_pm[p]], writes=[R_vst[v2]])
                                    S.dma("sp", lambda e, v2=v2, bs=bs, bn=bn, g=g: e.dma_start(
                                        out=v_d[t0 + bs:t0 + bs + bn, (g - 8) * 512:(g - 7) * 512], in_=vst[v2][:bn, :]),
                                        f"st_v{v2}", reads=[R_vst[v2]], writes=[S.R("v_d", g, t0 + bs)])
                            if g < 8:
                                dst = qT_d if g < 4 else kT_d
                                i0 = (g % 4) * 4
                                S.dma("sp", lambda e, dst=dst, i0=i0: e.dma_start(
                                    out=dst[i0:i0 + 4, :, t0:t0 + ntok].rearrange("h p t -> p h t"),
                                    in_=stage[:, :, 0:ntok]), "st_stage", reads=[R_stage],
                                    writes=[S.R("qkT_d", g, si)])
                        else:
                            off = 0
                            for (ts, tn) in stoks:
                                for mm in range(4):
                                    col = g * 512 + mm * 128
                                    p = it % 3
                                    it += 1
                                    for kc in range(KD):
                                        S.op("pe", lambda e, p=p, s=s, kc=kc, mm=mm, off=off, tn=tn: e.matmul(
                                            pm[p][:, :tn], wt[s][:, kc, mm * 128:(mm + 1) * 128],
                                            uT[:, kc, off:off + tn], start=(kc == 0), stop=(kc == KD - 1)),
                                            reads=[R_w[s], R_uT], writes=[R_pm[p]])
                                    f2 = it % 2
                                    if col < 6144 + 2 * LW:
                                        c0 = col - 6144
                                        dstT = xrT_d if c0 < LW else yrT_d
                                        r0 = c0 % LW
                                        S.op("act", lambda e, p=p, f2=f2, tn=tn: e.activation(
                                            out=fst[f2][:, :tn], in_=pm[p][:, :tn], func=AF.Copy),
                                            reads=[R_pm[p]], writes=[R_fst[f2]])
                                        S.dma("sp", lambda e, f2=f2, dstT=dstT, r0=r0, ts=ts, tn=tn: e.dma_start(
                                            out=dstT[r0:r0 + 128, ts:ts + tn], in_=fst[f2][:, :tn]),
                                            f"st_f{f2}", reads=[R_fst[f2]], writes=[S.R("xy_d", col, ts)])
                                    else:
                                        c0 = col - (6144 + 2 * LW)
                                        S.op("act", lambda e, p=p, f2=f2, tn=tn, c0=c0: e.activation(
                                            out=gst[f2][:, :tn], in_=pm[p][:, :tn], func=AF.Sigmoid,
                                            bias=bg[:, c0 // 128:c0 // 128 + 1]),
                                            reads=[R_pm[p], R_bg], writes=[R_gst[f2]])
                                        S.dma("sp", lambda e, f2=f2, c0=c0, ts=ts, tn=tn: e.dma_start(
                                            out=gT_d[c0:c0 + 128, ts:ts + tn], in_=gst[f2][:, :tn]),
                                            f"st_g{f2}", reads=[R_gst[f2]], writes=[S.R("g_d", col, ts)])
                                off += tn
                        if gi_ + 2 < len(GL):
                            wload(wt[gi_ % 2], w_in[l], KD, GL[gi_ + 2] * 512, 512, f"ld_w{gi_ % 2}", R_w[gi_ % 2])
                    S.barrier()

        def attention(l):
            with ExitStack() as ls:
                qT2 = [sb(f"qT2_{i}", [128, 2, T], BF16, ls) for i in range(2)]
                kT2 = [sb(f"kT2_{i}", [128, 2, T], BF16, ls) for i in range(2)]
                vx = [sb(f"vx_{i}", [128, NB, VD + 1], BF16, ls) for i in range(2)]
                pT = [sb(f"pT_{i}", [128, 512], BF16, ls) for i in range(3)]
                oc = [sb(f"oc_{i}", [128, 4, VD], F32, ls) for i in range(2)]
                rl = sb("rl", [128, 8], F32, ls)
                osb = sb("osb", [128, VD], F32, ls)
                junk = sb("junk", [128, VD], F32, ls)
                ss = sb("ss", [128, 4], F32, ls)
                on = sb("on", [128, VD], BF16, ls)
                ostage = [sb(f"ostage{i}", [128, 2, 512], BF16, ls) for i in range(2)]
                lamv = sb("lamv", [128, 8], F32, ls)
                lamc = sb("lamc", [128, 2], F32, ls)
                subl = sb("subl", [128, VD], F32, ls)
                pS = [ps(f"pS{i}", [128, 512], F32, ls) for i in range(2)]
                acc = [ps(f"acc{i}", [128, 512], F32, ls) for i in range(4)]
                ptr = ps("ptr_a", [128, 1024], BF16, ls)
                plam = ps("plam", [128, 512], F32, ls)
                R_q = [S.R("qT2", i) for i in range(2)]
                R_v = [S.R("vx", i) for i in range(2)]
                R_pT = [S.R("pT", i) for i in range(3)]
                R_oc = [S.R("oc", i) for i in range(2)]
                R_pS = [S.R("pS", i) for i in range(2)]
                R_acc = [S.R("acc", i) for i in range(4)]
                R_fin = S.R("fin")
                R_on = S.R("on")
                R_ptr = S.R("ptr_a")
                R_ost = [S.R("ostage", i) for i in range(2)]
                R_lam = S.R("lam")
                R_subl = S.R("subl")
                for j, src in enumerate((lam_q1, lam_k1, lam_q2, lam_k2)):
                    S.dma("sp", lambda e, j=j, src=src: e.dma_start(
                        out=lamv[:, j:j + 1], in_=src[l, :].rearrange("(p o) -> p o", o=1)),
                        "ld_small", writes=[R_lam], accumulate_writers=(j > 0))
                S.dma("sp", lambda e: e.dma_start(out=lamc[:, :], in_=c_laminit[l, :].partition_broadcast(128)),
                      "ld_small", writes=[R_lam], accumulate_writers=True)
                S.dma("sp", lambda e: e.dma_start(out=subl[:, :], in_=attn_subln[l, :].partition_broadcast(128)),
                      "ld_small", writes=[R_subl])
                S.op("dve", lambda e: e.tensor_tensor(out=lamv[:, 4:5], in0=lamv[:, 0:1], in1=lamv[:, 1:2],
                                                      op=ALU.mult), reads=[R_lam], writes=[R_lam])
                S.op("dve", lambda e: e.tensor_tensor(out=lamv[:, 5:6], in0=lamv[:, 2:3], in1=lamv[:, 3:4],
                                                      op=ALU.mult), reads=[R_lam], writes=[R_lam])
                onesf = sb("onesf", [128, 128], F32, ls)
                S.op("dve", lambda e: e.memset(onesf[:], 1.0), writes=[R_lam])
                S.op("pe", lambda e: e.matmul(plam[:, 0:2], onesf[:, :], lamv[:, 4:6], start=True, stop=True),
                     reads=[R_lam], writes=[S.R("plam")])
                S.op("act", lambda e: e.activation(out=lamv[:, 6:8], in_=plam[:, 0:2], func=AF.Exp),
                     reads=[S.R("plam")], writes=[R_lam])
                S.op("dve", lambda e: e.tensor_tensor(out=lamv[:, 4:5], in0=lamv[:, 7:8], in1=lamv[:, 6:7],
                                                      op=ALU.subtract), reads=[R_lam], writes=[R_lam])
                S.op("dve", lambda e: e.tensor_tensor(out=lamv[:, 4:5], in0=lamv[:, 4:5], in1=lamc[:, 0:1],
                                                      op=ALU.subtract), reads=[R_lam], writes=[R_lam])
                S.op("dve", lambda e: e.tensor_scalar(out=subl[:, :], in0=subl[:, :], scalar1=lamc[:, 1:2],
                                                      scalar2=None, op0=ALU.mult),
                     reads=[R_lam, R_subl], writes=[R_subl])
                for i in range(2):
                    S.op("dve", lambda e, i=i: e.memset(vx[i][:, :, VD:VD + 1], 1.0), writes=[R_v[i]])

                def load_head(h):
                    b = h % 2
                    S.dma("sp", lambda e, b=b, h=h: e.dma_start(
                        out=qT2[b][:, :, :], in_=qT_d[2 * h:2 * h + 2, :, :].rearrange("c p t -> p c t")),
                        f"ld_q{b}", writes=[R_q[b]])
                    S.dma("sp", lambda e, b=b, h=h: e.dma_start(
                        out=kT2[b][:, :, :], in_=kT_d[2 * h:2 * h + 2, :, :].rearrange("c p t -> p c t")),
                        f"ld_q{b}", writes=[R_q[b]], accumulate_writers=True)
                    for j0 in range(0, NB, 8):
                        j1 = min(NB, j0 + 8)
                        S.dma("sp", lambda e, b=b, h=h, j0=j0, j1=j1: e.dma_start(
                            out=vx[b][:, j0:j1, 0:VD],
                            in_=v_d[j0 * 128:j1 * 128, h * VD:(h + 1) * VD].rearrange("(j p) e -> p j e", p=128)),
                            f"ld_v{b}", reads=[], writes=[R_v[b]], accumulate_writers=True)

                load_head(0)
                ip = 0
                io = 0
                scale = HD ** -0.5
                for h in range(NH):
                    b = h % 2
                    if h + 1 < NH:
                        load_head(h + 1)
                    for (qs, qn) in TT:
                        nsb = (qn + 127) // 128
                        last_blk = (qs + qn - 1) // 128
                        for c in range(2):
                            for j in range(last_blk + 1):
                                ks, kn = TB[j]
                                q_lo = max(qs, ks) - qs
                                w = qn - q_lo
                                p2 = ip % 2
                                p3 = ip % 3
                                ip += 1
                                S.op("pe", lambda e, b=b, c=c, ks=ks, kn=kn, qs=qs, qn=qn, q_lo=q_lo, p2=p2: e.matmul(
                                    pS[p2][:kn, q_lo:qn], kT2[b][:, c, ks:ks + kn], qT2[b][:, c, qs + q_lo:qs + qn],
                                    start=True, stop=True), reads=[R_q[b]], writes=[R_pS[p2]])
                                S.op("act", lambda e, kn=kn, q_lo=q_lo, qn=qn, p2=p2, p3=p3: e.activation(
                                    out=pT[p3][:kn, q_lo:qn], in_=pS[p2][:kn, q_lo:qn], func=AF.Exp, scale=scale),
                                    reads=[R_pS[p2]], writes=[R_pT[p3]])
                                if ks >= qs:
                                    mw = min(128, w)
                                    S.op("dve", lambda e, kn=kn, q_lo=q_lo, mw=mw, p3=p3: e.tensor_tensor(
                                        out=pT[p3][:kn, q_lo:q_lo + mw], in0=pT[p3][:kn, q_lo:q_lo + mw],
                                        in1=maskt[:kn, :mw], op=ALU.mult),
                                        reads=[R_pT[p3], R_const], writes=[R_pT[p3]])
                                for sbi in range(nsb):
                                    s0 = sbi * 128
                                    if s0 < q_lo:
                                        continue
                                    sn = min(128, qn - s0)
                                    diag = (qs + s0) // 128
                                    S.op("pe", lambda e, b=b, kn=kn, s0=s0, sn=sn, j=j, sbi=sbi, p3=p3, diag=diag: e.matmul(
                                        acc[sbi][:sn, 0:VD + 1], pT[p3][:kn, s0:s0 + sn], vx[b][:kn, j, :],
                                        start=(j == 0), stop=(j == diag)),
                                        reads=[R_pT[p3], R_v[b]], writes=[R_acc[sbi]])
                            for sbi in range(nsb):
                                sn = min(128, qn - sbi * 128)
                                S.op("dve", lambda e, sbi=sbi, sn=sn, c=c: e.reciprocal(
                                    out=rl[:sn, c * 4 + sbi:c * 4 + sbi + 1], in_=acc[sbi][:sn, VD:VD + 1]),
                                    reads=[R_acc[sbi]], writes=[R_fin])
                                S.op("dve", lambda e, sbi=sbi, sn=sn, c=c: e.tensor_scalar(
                                    out=oc[c][:sn, sbi, :], in0=acc[sbi][:sn, 0:VD],
                                    scalar1=rl[:sn, c * 4 + sbi:c * 4 + sbi + 1], scalar2=None, op0=ALU.mult),
                                    reads=[R_acc[sbi], R_fin], writes=[R_oc[c]])
                        o2 = io % 2
                        io += 1
                        for sbi in range(nsb):
                            s0 = sbi * 128
                            sn = min(128, qn - s0)
                            S.op("dve", lambda e, sbi=sbi, sn=sn: e.scalar_tensor_tensor(
                                out=osb[:sn, :], in0=oc[1][:sn, sbi, :], scalar=lamv[:sn, 4:5], in1=oc[0][:sn, sbi, :],
                                op0=ALU.mult, op1=ALU.add), reads=[R_oc[0], R_oc[1], R_lam], writes=[R_fin])
                            S.op("dve", lambda e: e.memset(ss[:, 0:1], 0.0), writes=[S.R("ss")])
                            S.op("act", lambda e, sn=sn, sbi=sbi: e.activation(
                                out=junk[:sn, :], in_=osb[:sn, :], func=AF.Square, accum_out=ss[:sn, 0:1]),
                                reads=[R_fin], writes=[S.R("ss")])
                            S.op("act", lambda e, sn=sn: e.activation(
                                out=ss[:sn, 1:2], in_=ss[:sn, 0:1], func=AF.Sqrt, bias=eps_t[:sn, :], scale=1.0 / VD),
                                reads=[S.R("ss"), R_const], writes=[S.R("ss")])
                            S.op("dve", lambda e, sn=sn: e.reciprocal(out=ss[:sn, 2:3], in_=ss[:sn, 1:2]),
                                 reads=[S.R("ss")], writes=[S.R("ss")])
                            S.op("dve", lambda e, sn=sn: e.scalar_tensor_tensor(
                                out=on[:sn, :], in0=osb[:sn, :], scalar=ss[:sn, 2:3], in1=subl[:sn, :],
                                op0=ALU.mult, op1=ALU.mult), reads=[R_fin, S.R("ss"), R_subl], writes=[R_on])
                            for e2 in range(2):
                                S.op("pe", lambda e, e2=e2, sn=sn: e.transpose(
                                    ptr[:, e2 * 128:e2 * 128 + sn], on[:sn, e2 * 128:(e2 + 1) * 128], ident[:sn, :sn]),
                                    reads=[R_on, R_const], writes=[R_ptr])
                            ptv = ptr[:, 0:256].rearrange("p (a t) -> p a t", a=2)
                            S.op("act", lambda e, ptv=ptv, s0=s0, sn=sn, o2=o2: e.activation(
                                out=ostage[o2][:, :, s0:s0 + sn], in_=ptv[:, :, :sn], func=AF.Copy),
                                reads=[R_ptr], writes=[R_ost[o2]])
                        S.dma("sp", lambda e, h=h, qs=qs, qn=qn, o2=o2: e.dma_start(
                            out=aoT_d[h * VD:(h + 1) * VD, qs:qs + qn].rearrange("(a p) t -> p a t", p=128),
                            in_=ostage[o2][:, :, 0:qn]), f"st_o{o2}", reads=[R_ost[o2]],
                            writes=[S.R("aoT_d", h, qs)])
                S.barrier()

        def lru(l):
            with ExitStack() as ls:
                NP = len(GATE_PAIRS)
                wexp = sb("wexp", [128, 2 * 4 * NP, 128], BF16, ls)
                wstg = sb("wstg", [128, 2 * 4 * NP, 128], F32, ls)
                cw = sb("cw", [128, 4, KL], F32, ls)
                cb = sb("cb", [128, KL], F32, ls)
                gab = sb("gab", [128, KL], F32, ls)
                gxb = sb("gxb", [128, KL], F32, ls)
                ca = sb("ca", [128, KL], F32, ls)
                carry = sb("carry", [128, KL], F32, ls)
                xin = [sb(f"xin{i}", [128, 5, 3 + 512], F32, ls) for i in range(2)]
                xc = sb("xc", [128, 5, 512], F32, ls)
                xcb = sb("xcb", [128, 5, 512], BF16, ls)
                NTMP = 2
                tr = [sb(f"tr{i}", [128, 512], F32, ls) for i in range(NTMP)]
                ti = [sb(f"ti{i}", [128, 512], F32, ls) for i in range(NTMP)]
                ta = [sb(f"ta{i}", [128, 512], F32, ls) for i in range(NTMP)]
                tm = [sb(f"tm{i}", [128, 512], F32, ls) for i in range(NTMP)]
                tbx = [sb(f"tbx{i}", [128, 512], F32, ls) for i in range(NTMP)]
                th = [sb(f"th{i}", [128, 512], F32, ls) for i in range(NTMP)]
                ty = [sb(f"ty{i}", [128, 512], F32, ls) for i in range(NTMP)]
                tg = [sb(f"tg{i}", [128, 512], F32, ls) for i in range(NTMP)]
                thy = [sb(f"thy{i}", [128, 512], BF16, ls) for i in range(NTMP)]
                pg = [ps(f"pgate{i}", [128, 512], F32, ls) for i in range(4)]
                R_w = S.R("wexp")
                R_par = S.R("lrupar")
                R_carry = S.R("carry")
                R_xin = [S.R("xin", i) for i in range(2)]
                R_xc = S.R("xc")
                R_xcb = S.R("xcb")
                R_pg = [S.R("pgate", i) for i in range(4)]
                R_tmp = [S.R("lrutmp", i) for i in range(NTMP)]
                R_ty = [S.R("lruty", i) for i in range(NTMP)]
                R_thy = [S.R("lruthy", i) for i in range(NTMP)]
                for j in range(4):
                    S.dma("sp", lambda e, j=j: e.dma_start(
                        out=cw[:, j, :], in_=conv_w[l, j, :].rearrange("(c p) -> p c", p=128),
                        allow_slow_non_contiguous=True), "ld_small", writes=[R_par], accumulate_writers=(j > 0))
                for dst, src in ((cb, conv_b), (gab, gate_a_b), (gxb, gate_x_b), (ca, lru_a)):
                    S.dma("sp", lambda e, dst=dst, src=src: e.dma_start(
                        out=dst[:, :], in_=src[l, :].rearrange("(c p) -> p c", p=128),
                        allow_slow_non_contiguous=True), "ld_small", writes=[R_par], accumulate_writers=True)
                S.op("act", lambda e: e.activation(out=ca[:, :], in_=ca[:, :], func=AF.Exp, scale=-1.0),
                     reads=[R_par], writes=[R_par])
                S.op("act", lambda e: e.activation(out=ca[:, :], in_=ca[:, :], func=AF.Ln, bias=one_t[:, :], scale=1.0),
                     reads=[R_par, R_const], writes=[R_par])
                S.op("dve", lambda e: e.tensor_scalar(out=ca[:, :], in0=ca[:, :], scalar1=-8.0, scalar2=None,
                                                      op0=ALU.mult), reads=[R_par], writes=[R_par])
                S.op("dve", lambda e: e.memset(carry[:, :], 0.0), writes=[R_carry])
                S.op("dve", lambda e: e.memset(wstg[:], 0.0), writes=[R_w])
                first = True
                for gi, gw in enumerate((gate_a_w, gate_x_w)):
                    for grp in range(4):
                        for bl in range(4):
                            n = grp * 4 + bl
                            base = 160 * bl
                            for (ci, r0, r1) in ((bl, base, 128 * (bl + 1)), (bl + 1, 128 * (bl + 1), base + 160)):
                                for (co, c0, c1) in ((bl, base, 128 * (bl + 1)), (bl + 1, 128 * (bl + 1), base + 160)):
                                    if r1 <= r0 or c1 <= c0:
                                        continue
                                    idx = (gi * 4 + grp) * NP + GATE_PAIRS.index((ci, co))
                                    S.dma("sp", lambda e, idx=idx, gw=gw, n=n, r0=r0, r1=r1, c0=c0, c1=c1, ci=ci, co=co, base=base: e.dma_start(
                                        out=wstg[r0 - 128 * ci:r1 - 128 * ci, idx, c0 - 128 * co:c1 - 128 * co],
                                        in_=gw[l, n, r0 - base:r1 - base, c0 - base:c1 - base]),
                                        "ld_gw", writes=[R_w], accumulate_writers=not first)
                                    first = False
                S.op("dve", lambda e: e.tensor_copy(out=wexp[:], in_=wstg[:]), reads=[R_w], writes=[R_w])
                for i in range(2):
                    S.op("dve", lambda e, i=i: e.memset(xin[i][:, :, 0:3], 0.0), writes=[R_xin[i]])
                it = 0
                ic = 0
                for (ts, tn) in TT:
                    for grp in range(4):
                        x2 = it % 2
                        it += 1
                        rows = xrT_d[grp * 640:(grp + 1) * 640, :].rearrange("(c p) t -> p c t", p=128)
                        if ts == 0:
                            S.dma("sp", lambda e, x2=x2, rows=rows, tn=tn: e.dma_start(
                                out=xin[x2][:, :, 3:3 + tn], in_=rows[:, :, 0:tn]), f"ld_xin{x2}", writes=[R_xin[x2]])
                        else:
                            S.dma("sp", lambda e, x2=x2, rows=rows, ts=ts, tn=tn: e.dma_start(
                                out=xin[x2][:, :, 0:3 + tn], in_=rows[:, :, ts - 3:ts + tn]), f"ld_xin{x2}",
                                writes=[R_xin[x2]])
                        for ci in range(5):
                            ch = grp * 5 + ci
                            S.op("dve", lambda e, x2=x2, ci=ci, ch=ch, tn=tn: e.tensor_scalar(
                                out=xc[:, ci, :tn], in0=xin[x2][:, ci, 3:3 + tn], scalar1=cw[:, 0, ch:ch + 1],
                                scalar2=cb[:, ch:ch + 1], op0=ALU.mult, op1=ALU.add),
                                reads=[R_xin[x2], R_par], writes=[R_xc])
                            for j in range(1, 4):
                                S.op("dve", lambda e, x2=x2, ci=ci, ch=ch, tn=tn, j=j: e.scalar_tensor_tensor(
                                    out=xc[:, ci, :tn], in0=xin[x2][:, ci, 3 - j:3 - j + tn], scalar=cw[:, j, ch:ch + 1],
                                    in1=xc[:, ci, :tn], op0=ALU.mult, op1=ALU.add),
                                    reads=[R_xin[x2], R_par, R_xc], writes=[R_xc])
                        S.op("act", lambda e, tn=tn: e.activation(out=xcb[:, :, :tn], in_=xc[:, :, :tn], func=AF.Copy),
                             reads=[R_xc], writes=[R_xcb])
                        for co in range(5):
                            ch = grp * 5 + co
                            k = ic % NTMP
                            pa = (2 * ic) % 4
                            px = (2 * ic + 1) % 4
                            ic += 1
                            nb = GATE_NBRS[co]
                            for gi, pp in ((0, pa), (1, px)):
                                for n_i, ci in enumerate(nb):
                                    idx = (gi * 4 + grp) * NP + GATE_PAIRS.index((ci, co))
                                    S.op("pe", lambda e, pp=pp, idx=idx, ci=ci, tn=tn, n_i=n_i, nb=nb: e.matmul(
                                        pg[pp][:, :tn], wexp[:, idx, :], xcb[:, ci, :tn],
                                        start=(n_i == 0), stop=(n_i == len(nb) - 1)),
                                        reads=[R_w, R_xcb], writes=[R_pg[pp]])
                            S.dma("sp", lambda e, k=k, ch=ch, ts=ts, tn=tn: e.dma_start(
                                out=ty[k][:, :tn], in_=yrT_d[ch * 128:(ch + 1) * 128, ts:ts + tn]),
                                f"ld_ty{k}", writes=[R_ty[k]])
                            S.op("act", lambda e, k=k, pa=pa, ch=ch, tn=tn: e.activation(
                                out=tr[k][:, :tn], in_=pg[pa][:, :tn], func=AF.Sigmoid, bias=gab[:, ch:ch + 1]),
                                reads=[R_pg[pa], R_par], writes=[R_tmp[k]])
                            S.op("act", lambda e, k=k, px=px, ch=ch, tn=tn: e.activation(
                                out=ti[k][:, :tn], in_=pg[px][:, :tn], func=AF.Sigmoid, bias=gxb[:, ch:ch + 1]),
                                reads=[R_pg[px], R_par], writes=[R_tmp[k]])
                            S.op("act", lambda e, k=k, ch=ch, tn=tn: e.activation(
                                out=ta[k][:, :tn], in_=tr[k][:, :tn], func=AF.Exp, scale=ca[:, ch:ch + 1]),
                                reads=[R_tmp[k], R_par], writes=[R_tmp[k]])
                            S.op("dve", lambda e, k=k, tn=tn: e.tensor_tensor(
                                out=tm[k][:, :tn], in0=ta[k][:, :tn], in1=ta[k][:, :tn], op=ALU.mult),
                                reads=[R_tmp[k]], writes=[R_tmp[k]])
                            S.op("dve", lambda e, k=k, tn=tn: e.tensor_scalar(
                                out=tm[k][:, :tn], in0=tm[k][:, :tn], scalar1=-1.0, scalar2=1.0,
                                op0=ALU.mult, op1=ALU.add), reads=[R_tmp[k]], writes=[R_tmp[k]])
                            S.op("act", lambda e, k=k, tn=tn: e.activation(
                                out=tm[k][:, :tn], in_=tm[k][:, :tn], func=AF.Sqrt),
                                reads=[R_tmp[k]], writes=[R_tmp[k]])
                            S.op("dve", lambda e, k=k, co=co, tn=tn: e.tensor_tensor(
                                out=tbx[k][:, :tn], in0=ti[k][:, :tn], in1=xc[:, co, :tn], op=ALU.mult),
                                reads=[R_tmp[k], R_xc], writes=[R_tmp[k]])
                            S.op("dve", lambda e, k=k, tn=tn: e.tensor_tensor(
                                out=tbx[k][:, :tn], in0=tbx[k][:, :tn], in1=tm[k][:, :tn], op=ALU.mult),
                                reads=[R_tmp[k]], writes=[R_tmp[k]])
                            S.op("dve", lambda e, k=k, ch=ch, tn=tn: e.tensor_tensor_scan(
                                out=th[k][:, :tn], data0=ta[k][:, :tn], data1=tbx[k][:, :tn],
                                initial=carry[:, ch:ch + 1], op0=ALU.mult, op1=ALU.add),
                                reads=[R_tmp[k], R_carry], writes=[R_tmp[k]])
                            S.op("dve", lambda e, k=k, ch=ch, tn=tn: e.tensor_copy(
                                out=carry[:, ch:ch + 1], in_=th[k][:, tn - 1:tn]),
                                reads=[R_tmp[k]], writes=[R_carry])
                            S.op("dve", lambda e, k=k, tn=tn: e.tensor_tensor(
                                out=tg[k][:, :tn], in0=ty[k][:, :tn], in1=ty[k][:, :tn], op=ALU.mult),
                                reads=[R_ty[k], R_thy[k]], writes=[R_tmp[k]])
                            S.op("dve", lambda e, k=k, tn=tn: e.tensor_scalar(
                                out=tg[k][:, :tn], in0=tg[k][:, :tn], scalar1=0.044715, scalar2=1.0,
                                op0=ALU.mult, op1=ALU.add), reads=[R_tmp[k]], writes=[R_tmp[k]])
                            S.op("dve", lambda e, k=k, tn=tn: e.tensor_tensor(
                                out=tg[k][:, :tn], in0=tg[k][:, :tn], in1=ty[k][:, :tn], op=ALU.mult),
                                reads=[R_tmp[k], R_ty[k]], writes=[R_tmp[k]])
                            S.op("act", lambda e, k=k, tn=tn: e.activation(
                                out=tg[k][:, :tn], in_=tg[k][:, :tn], func=AF.Sigmoid, scale=1.5957691216057308),
                                reads=[R_tmp[k]], writes=[R_tmp[k]])
                            S.op("dve", lambda e, k=k, tn=tn: e.tensor_tensor(
                                out=tg[k][:, :tn], in0=tg[k][:, :tn], in1=ty[k][:, :tn], op=ALU.mult),
                                reads=[R_tmp[k], R_ty[k]], writes=[R_tmp[k]])
                            S.op("dve", lambda e, k=k, tn=tn: e.tensor_tensor(
                                out=thy[k][:, :tn], in0=tg[k][:, :tn], in1=th[k][:, :tn], op=ALU.mult),
                                reads=[R_tmp[k]], writes=[R_thy[k]])
                            S.dma("sp", lambda e, k=k, ch=ch, ts=ts, tn=tn: e.dma_start(
                                out=hyT_d[ch * 128:(ch + 1) * 128, ts:ts + tn], in_=thy[k][:, :tn]),
                                f"st_hy{k}", reads=[R_thy[k]], writes=[S.R("hyT_d", ch, ts)])
                S.barrier()

        def mixer_out(l):
            for si, stoks in enumerate(ST):
                mixer_out_body(l, si, stoks)

        def mixer_out_body(l, si, stoks):
            if True:
                t0 = stoks[0][0]
                ntok = sum(n for _, n in stoks)
                with ExitStack() as ls:
                    aoT = sb("aoT", [128, KD, STMAX], BF16, ls)
                    hyT = sb("hyT", [128, KL, STMAX], BF16, ls)
                    mT = sb("mT", [128, KD, STMAX], BF16, ls)
                    wa = [sb(f"wba{i}", [128, KD, 256], BF16, ls) for i in range(2)]
                    wl = [sb(f"wbl{i}", [128, KL, 256], BF16, ls) for i in range(2)]
                    gA = [sb(f"gA{i}", [128, 512], BF16, ls) for i in range(2)]
                    gL = [sb(f"gL{i}", [128, 512], BF16, ls) for i in range(2)]
                    m1 = [sb(f"m1_{i}", [128, 512], F32, ls) for i in range(2)]
                    m2 = [sb(f"m2_{i}", [128, 512], F32, ls) for i in range(2)]
                    hr = [sb(f"hrm{i}", [128, 512], F32, ls) for i in range(2)]
                    hn = [sb(f"hnm{i}", [128, 512], F32, ls) for i in range(2)]
                    pa = [ps(f"pba{i}", [128, 512], F32, ls) for i in range(2)]
                    pl = [ps(f"pbl{i}", [128, 512], F32, ls) for i in range(2)]
                    po = [ps(f"pwo{i}", [128, 512], F32, ls) for i in range(2)]
                    R_in = S.R("m4in")
                    R_mT = S.R("mT")
                    R_w = [S.R("wbr", i) for i in range(2)]
                    R_g = [S.R("gAL", i) for i in range(2)]
                    R_m = [S.R("m12", i) for i in range(2)]
                    R_hr = [S.R("hrm", i) for i in range(2)]
                    R_hn = [S.R("hnm", i) for i in range(2)]
                    R_pa = [S.R("pba", i) for i in range(2)]
                    R_pl = [S.R("pbl", i) for i in range(2)]
                    R_po = [S.R("pwo", i) for i in range(2)]
                    for c0 in range(0, KD, 4):
                        S.dma("sp", lambda e, c0=c0: e.dma_start(
                            out=aoT[:, c0:c0 + 4, 0:ntok],
                            in_=aoT_d[c0 * 128:(c0 + 4) * 128, t0:t0 + ntok].rearrange("(c p) t -> p c t", p=128)),
                            "ld_m4", writes=[R_in], accumulate_writers=(c0 > 0))
                    for c0 in range(0, KL, 4):
                        S.dma("sp", lambda e, c0=c0: e.dma_start(
                            out=hyT[:, c0:c0 + 4, 0:ntok],
                            in_=hyT_d[c0 * 128:(c0 + 4) * 128, t0:t0 + ntok].rearrange("(c p) t -> p c t", p=128)),
                            "ld_m4", writes=[R_in], accumulate_writers=True)

                    def ldw(dg):
                        s = dg % 2
                        wload(wa[s], w_ba[l], KD, dg * 256, 256, f"ld_w{s}", R_w[s])
                        Wv = w_bl[l].rearrange("(kc p) n -> p kc n", p=128)
                        for k0 in range(0, KL, 4):
                            S.dma("pool", lambda e, k0=k0, s=s, dg=dg: e.dma_start(
                                out=wl[s][:, k0:k0 + 4, :], in_=Wv[:, k0:k0 + 4, dg * 256:(dg + 1) * 256]),
                                f"ld_w{s}", writes=[R_w[s]], accumulate_writers=True)

                    ldw(0)
                    ldw(1)
                    it = 0
                    for dg in range(KD // 2):
                        s = dg % 2
                        off = 0
                        for (ts, tn) in stoks:
                            for dd in range(2):
                                d = 2 * dg + dd
                                p = it % 2
                                it += 1
                                S.dma("sp", lambda e, p=p, d=d, ts=ts, tn=tn: e.dma_start(
                                    out=gA[p][:, :tn], in_=gT_d[d * 128:(d + 1) * 128, ts:ts + tn]),
                                    f"ld_g{p}", writes=[R_g[p]])
                                S.dma("sp", lambda e, p=p, d=d, ts=ts, tn=tn: e.dma_start(
                                    out=gL[p][:, :tn], in_=gT_d[D + d * 128:D + (d + 1) * 128, ts:ts + tn]),
                                    f"ld_g{p}", writes=[R_g[p]], accumulate_writers=True)
                                for kc in range(KD):
                                    S.op("pe", lambda e, p=p, s=s, kc=kc, dd=dd, off=off, tn=tn: e.matmul(
                                        pa[p][:, :tn], wa[s][:, kc, dd * 128:(dd + 1) * 128], aoT[:, kc, off:off + tn],
                                        start=(kc == 0), stop=(kc == KD - 1)), reads=[R_w[s], R_in], writes=[R_pa[p]])
                                for kc in range(KL):
                                    S.op("pe", lambda e, p=p, s=s, kc=kc, dd=dd, off=off, tn=tn: e.matmul(
                                        pl[p][:, :tn], wl[s][:, kc, dd * 128:(dd + 1) * 128], hyT[:, kc, off:off + tn],
                                        start=(kc == 0), stop=(kc == KL - 1)), reads=[R_w[s], R_in], writes=[R_pl[p]])
                                S.op("dve", lambda e, p=p, tn=tn: e.tensor_tensor(
                                    out=m1[p][:, :tn], in0=pa[p][:, :tn], in1=gA[p][:, :tn], op=ALU.mult),
                                    reads=[R_pa[p], R_g[p]], writes=[R_m[p]])
                                S.op("dve", lambda e, p=p, tn=tn: e.tensor_tensor(
                                    out=m2[p][:, :tn], in0=pl[p][:, :tn], in1=gL[p][:, :tn], op=ALU.mult),
                                    reads=[R_pl[p], R_g[p]], writes=[R_m[p]])
                                S.op("dve", lambda e, p=p, d=d, off=off, tn=tn: e.tensor_tensor(
                                    out=mT[:, d, off:off + tn], in0=m1[p][:, :tn], in1=m2[p][:, :tn], op=ALU.add),
                                    reads=[R_m[p]], writes=[R_mT])
                            off += tn
                        if dg + 2 < KD // 2:
                            ldw(dg + 2)
                    def ldo(dg):
                        s = dg % 2
                        wload(wa[s], w_o[l], KD, dg * 256, 256, f"ld_w{s}", R_w[s])

                    ldo(0)
                    ldo(1)
                    it = 0
                    for dg in range(KD // 2):
                        s = dg % 2
                        off = 0
                        for (ts, tn) in stoks:
                            for dd in range(2):
                                d = 2 * dg + dd
                                p = it % 2
                                it += 1
                                S.dma("sp", lambda e, p=p, d=d, ts=ts, tn=tn: e.dma_start(
                                    out=hr[p][:, :tn], in_=hT[d * 128:(d + 1) * 128, ts:ts + tn]),
                                    f"ld_hr{p}", reads=[Rh(d, ts)], writes=[R_hr[p]])
                                for kc in range(KD):
                                    S.op("pe", lambda e, p=p, s=s, kc=kc, dd=dd, off=off, tn=tn: e.matmul(
                                        po[p][:, :tn], wa[s][:, kc, dd * 128:(dd + 1) * 128], mT[:, kc, off:off + tn],
                                        start=(kc == 0), stop=(kc == KD - 1)), reads=[R_w[s], R_mT], writes=[R_po[p]])
                                S.op("dve", lambda e, p=p, tn=tn: e.tensor_tensor(
                                    out=hn[p][:, :tn], in0=po[p][:, :tn], in1=hr[p][:, :tn], op=ALU.add),
                                    reads=[R_po[p], R_hr[p]], writes=[R_hn[p]])
                                S.dma("sp", lambda e, p=p, d=d, ts=ts, tn=tn: e.dma_start(
                                    out=hT[d * 128:(d + 1) * 128, ts:ts + tn], in_=hn[p][:, :tn]),
                                    f"st_hn{p}", reads=[R_hn[p]], writes=[Rh(d, ts)])
                            off += tn
                        if dg + 2 < KD // 2:
                            ldo(dg + 2)
                    S.barrier()

        S.barrier()
        import os
        PH = os.environ.get("K_PHASES", "f1,proj,attn,lru,out,f2").split(",")
        wrote_h = False
        for l in range(DEPTH):
            if "f1" in PH:
                ffn(l, norm_ffn1, ffn1_wg, ffn1_wu, ffn1_wd, "f1", hin=(h0T if l == 0 else hT))
                wrote_h = True
            if "proj" in PH:
                mixer_proj(l)
            if "attn" in PH:
                attention(l)
            if "lru" in PH:
                lru(l)
            if "out" in PH:
                mixer_out(l)
            if "f2" in PH:
                ffn(l, norm_ffn2, ffn2_wg, ffn2_wu, ffn2_wd, "f2")
        for si, stoks in enumerate(ST):
            with ExitStack() as ls:
                rms_to_uT(ls, stoks, final_norm[0, :], None, None, "fin", out_dram=outT,
                          hin=(hT if wrote_h else h0T))
        S.barrier()
        S.emit()
    return nc


def const_tables(T):
    NB = (T + 127) // 128
    inv_freq = (THETA ** (-np.arange(0, ROT, 2, dtype=np.float32) / ROT)).astype(np.float32)
    ang = np.arange(NB * 128, dtype=np.float32)[:, None] * inv_freq[None, :]
    cos = np.tile(np.cos(ang).astype(np.float32), (1, 4))
    sin = np.tile(np.sin(ang).astype(np.float32), (1, 4))
    mask = np.triu(np.ones((128, 128), np.float32))
    ident = np.eye(128, dtype=np.float32)
    return cos, sin, mask, ident


_NC_CACHE = {}


def run_layers(h0T_list, inputs, T, DEPTH, debug=False):
    key = (T, DEPTH, debug)
    if key not in _NC_CACHE:
        _NC_CACHE[key] = build(T, DEPTH, debug)
    nc = _NC_CACHE[key]
    cos, sin, mask, ident = const_tables(T)
    import ml_dtypes
    lam = np.array([[lam_init_of(l), 1.0 - lam_init_of(l)] for l in range(DEPTH)], np.float32)
    shared = {k: np.ascontiguousarray(v) for k, v in inputs.items() if k not in ("x", "meta_tokens")}
    shared["final_norm"] = np.ascontiguousarray(inputs["final_norm"]).reshape(1, D)
    shared.update(c_cos=cos, c_sin=sin, c_mask=mask.astype(ml_dtypes.bfloat16),
                  c_ident=ident.astype(ml_dtypes.bfloat16), c_laminit=lam)
    in_maps = []
    for h0T in h0T_list:
        m = dict(shared)
        m["h0T"] = h0T
        in_maps.append(m)
    res = run_bass_kernel_spmd(nc, in_maps, core_ids=list(range(len(in_maps))))
    return res


def kernel(**inputs):
    x = np.asarray(inputs["x"])
    meta = np.asarray(inputs["meta_tokens"])
    B, SEQ, _ = x.shape
    T = SEQ + NMETA
    DEPTH = np.asarray(inputs["w_in"]).shape[0]
    h0T_list = [np.ascontiguousarray(np.concatenate([meta, x[b]], axis=0).T) for b in range(B)]
    res = run_layers(h0T_list, {k: np.asarray(v) for k, v in inputs.items()}, T, DEPTH)
    out = np.stack([np.ascontiguousarray(r["outT"].T[NMETA:]) for r in res.results], axis=0)
    return out.astype(np.float32)
```

```python
import math
from contextlib import ExitStack

import numpy as np
import concourse.bass as bass
import concourse.mybir as mybir
from concourse.bass_utils import run_bass_kernel_spmd

F32 = mybir.dt.float32
BF16 = mybir.dt.bfloat16
AF = mybir.ActivationFunctionType
ALU = mybir.AluOpType

D = 2048
KD = 16
FF = 5632
KF = 44
NH = 8
HD = 128
VD = 256
ROT = 32
LW = 2560
KL = 20
NIN = 15360
NMETA = 16
EPS = 1e-6
THETA = 500000.0

ENGS = ("pe", "act", "dve", "pool", "sp")


class Res:
    __slots__ = ("w", "r")

    def __init__(self):
        self.w = []
        self.r = []


class Sched:
    def __init__(self, nc):
        self.nc = nc
        self.ops = {e: [] for e in ENGS}
        self.cnt = {}
        self.handles = {}
        self.waited = {e: {} for e in ENGS}
        self.res = {}
        self.stack = ExitStack()
        for e in ("pe", "act", "dve", "pool"):
            self.new_sem("c_" + e)

    def new_sem(self, name):
        if name not in self.handles:
            self.handles[name] = self.stack.enter_context(self.nc.semaphore(name))
            self.cnt[name] = 0
        return name

    def R(self, *key):
        r = self.res.get(key)
        if r is None:
            r = self.res[key] = Res()
        return r

    def _deps(self, eng, reads, writes):
        need = {}
        for r in reads:
            for (s, v) in r.w:
                need[s] = max(need.get(s, 0), v)
        for w in writes:
            for (s, v) in w.w:
                need[s] = max(need.get(s, 0), v)
            for (s, v) in w.r:
                need[s] = max(need.get(s, 0), v)
        waits = []
        for s, v in need.items():
            if not s.startswith("c_"):
                v = self.cnt[s]
            if eng == "pe" and s == "c_pe":
                continue
            if self.waited[eng].get(s, 0) >= v:
                continue
            self.waited[eng][s] = v
            waits.append((s, v))
        return waits

    def _commit(self, tok, reads, writes):
        for r in reads:
            r.r.append(tok)
            if len(r.r) > 24:
                m = {}
                for (s, v) in r.r:
                    m[s] = max(m.get(s, 0), v)
                r.r = list(m.items())
        for w in writes:
            w.w = [tok]
            w.r = []

    def op(self, eng, fn, reads=(), writes=()):
        waits = self._deps(eng, reads, writes)
        s = "c_" + eng
        self.cnt[s] += 1
        tok = (s, self.cnt[s])
        self.ops[eng].append((waits, fn, s, 1))
        self._commit(tok, reads, writes)

    def dma(self, q, fn, sem, reads=(), writes=(), accumulate_writers=False):
        waits = self._deps(q, reads, writes)
        self.new_sem(sem)
        self.cnt[sem] += 16
        tok = (sem, self.cnt[sem])
        self.ops[q].append((waits, fn, sem, 16))
        if accumulate_writers:
            for r in reads:
                r.r.append(tok)
            for w in writes:
                w.w.append(tok)
        else:
            self._commit(tok, reads, writes)

    def barrier(self):
        for e in ENGS:
            waits = []
            for s, v in self.cnt.items():
                if v == 0 or (e == "pe" and s == "c_pe"):
                    continue
                if s == "c_" + e:
                    continue
                if self.waited[e].get(s, 0) >= v:
                    continue
                self.waited[e][s] = v
                waits.append((s, v))
            if waits:
                self.ops[e].append((waits, None, None, 0))

    def emit(self):
        nc = self.nc
        H = self.handles

        def run(eng, lst):
            for (waits, fn, s, inc) in lst:
                for (ws, wv) in waits:
                    eng.wait_ge(H[ws], wv)
                if fn is not None:
                    ins = fn(eng)
                    ins.then_inc(H[s], inc)

        with nc.Block() as block:
            @block.tensor
            def _(e):
                run(e, self.ops["pe"])

            @block.scalar
            def _(e):
                run(e, self.ops["act"])

            @block.vector
            def _(e):
                run(e, self.ops["dve"])

            @block.gpsimd
            def _(e):
                run(e, self.ops["pool"])

            @block.sync
            def _(e):
                run(e, self.ops["sp"])


def tiles_of(T, step):
    out = []
    s = 0
    while s < T:
        out.append((s, min(step, T - s)))
        s += step
    return out


def super_tiles(T):
    tt = tiles_of(T, 512)
    groups = []
    i = 0
    while i < len(tt):
        g = tt[i:i + 2]
        i += 2
        groups.append(g)
    if len(groups) > 1 and sum(n for _, n in groups[-1]) < 128:
        last = groups.pop()
        groups[-1] = groups[-1] + last
    return groups


def lam_init_of(l):
    return 0.8 - 0.6 * math.exp(-0.3 * l)


GATE_PAIRS = sorted({(ci, co) for b in range(4) for ci in (b, b + 1) for co in (b, b + 1)})
GATE_NBRS = {co: [ci for (ci, c2) in GATE_PAIRS if c2 == co] for co in range(5)}


def build(T, DEPTH, debug=False):
    nc = bass.Bass("TRN2", target_bir_lowering=False)
    S = Sched(nc)
    TT = tiles_of(T, 512)
    TB = tiles_of(T, 128)
    NB = len(TB)
    ST = super_tiles(T)
    STMAX = max(sum(n for _, n in g) for g in ST)

    def din(name, shape, dt=F32):
        return nc.dram_tensor(name, list(shape), dt, kind="ExternalInput").ap()

    dbg_kind = "ExternalOutput" if debug else "Internal"

    def dscr(name, shape, dt):
        return nc.dram_tensor(name, list(shape), dt, kind=dbg_kind).ap()

    h0T = din("h0T", [D, T])
    norm_ffn1 = din("norm_ffn1", [DEPTH, D])
    ffn1_wg = din("ffn1_w_gate", [DEPTH, D, FF])
    ffn1_wu = din("ffn1_w_up", [DEPTH, D, FF])
    ffn1_wd = din("ffn1_w_down", [DEPTH, FF, D])
    norm_mix = din("norm_mix", [DEPTH, D])
    w_in = din("w_in", [DEPTH, D, NIN])
    b_gate = din("b_gate", [DEPTH, 2 * D])
    lam_q1 = din("lambda_q1", [DEPTH, HD])
    lam_k1 = din("lambda_k1", [DEPTH, HD])
    lam_q2 = din("lambda_q2", [DEPTH, HD])
    lam_k2 = din("lambda_k2", [DEPTH, HD])
    attn_subln = din("attn_subln", [DEPTH, VD])
    conv_w = din("conv_w", [DEPTH, 4, LW])
    conv_b = din("conv_b", [DEPTH, LW])
    gate_x_w = din("gate_x_w", [DEPTH, 16, 160, 160])
    gate_x_b = din("gate_x_b", [DEPTH, LW])
    gate_a_w = din("gate_a_w", [DEPTH, 16, 160, 160])
    gate_a_b = din("gate_a_b", [DEPTH, LW])
    lru_a = din("lru_a_param", [DEPTH, LW])
    w_ba = din("w_branch_attn", [DEPTH, D, D])
    w_bl = din("w_branch_lru", [DEPTH, LW, D])
    w_o = din("w_out", [DEPTH, D, D])
    norm_ffn2 = din("norm_ffn2", [DEPTH, D])
    ffn2_wg = din("ffn2_w_gate", [DEPTH, D, FF])
    ffn2_wu = din("ffn2_w_up", [DEPTH, D, FF])
    ffn2_wd = din("ffn2_w_down", [DEPTH, FF, D])
    final_norm = din("final_norm", [1, D])
    c_cos = din("c_cos", [NB * 128, 64])
    c_sin = din("c_sin", [NB * 128, 64])
    c_mask = din("c_mask", [128, 128], BF16)
    c_ident = din("c_ident", [128, 128], BF16)
    c_laminit = din("c_laminit", [DEPTH, 2])

    outT = nc.dram_tensor("outT", [D, T], F32, kind="ExternalOutput").ap()

    hT = dscr("hT", [D, T], F32)
    qT_d = dscr("qT_d", [16, 128, T], BF16)
    kT_d = dscr("kT_d", [16, 128, T], BF16)
    v_d = dscr("v_d", [NB * 128, D], BF16)
    xrT_d = dscr("xrT_d", [LW, T], F32)
    yrT_d = dscr("yrT_d", [LW, T], F32)
    gT_d = dscr("gT_d", [2 * D, T], BF16)
    aoT_d = dscr("aoT_d", [D, T], BF16)
    hyT_d = dscr("hyT_d", [LW, T], BF16)

    WSP = {"ffn1_wg": (ffn1_wg, D, FF, 256), "ffn1_wu": (ffn1_wu, D, FF, 256), "ffn1_wd": (ffn1_wd, FF, D, 256),
           "w_in": (w_in, D, NIN, 512), "w_ba": (w_ba, D, D, 256), "w_bl": (w_bl, LW, D, 256),
           "w_o": (w_o, D, D, 256),
           "ffn2_wg": (ffn2_wg, D, FF, 256), "ffn2_wu": (ffn2_wu, D, FF, 256), "ffn2_wd": (ffn2_wd, FF, D, 256)}
    wbuf = {}
    for l_ in range(DEPTH):
        for nm, (_, K_, N_, tc_) in WSP.items():
            wbuf[(nm, l_)] = nc.dram_tensor(f"wb_{nm}_{l_}", [N_ // tc_, 128, (K_ // 128) * tc_], BF16).ap()

    with S.stack:
        es = S.stack

        uid = [0]

        def sb(name, shape, dt, st=None):
            uid[0] += 1
            return (st or es).enter_context(nc.sbuf_tensor(f"{name}_{uid[0]}", list(shape), dt))

        def ps(name, shape, dt, st):
            uid[0] += 1
            return st.enter_context(nc.psum_tensor(f"{name}_{uid[0]}", list(shape), dt))

        ones_bf = sb("ones_bf", [128, 128], BF16)
        ident = sb("ident", [128, 128], BF16)
        maskt = sb("maskt", [128, 128], BF16)
        eps_t = sb("eps_t", [128, 1], F32)
        one_t = sb("one_t", [128, 1], F32)
        cos_t = sb("cos_t", [128, NB, 64], F32)
        sin_t = sb("sin_t", [128, NB, 64], F32)
        R_const = S.R("const")
        S.op("dve", lambda e: e.memset(ones_bf[:], 1.0), writes=[R_const])
        S.op("dve", lambda e: e.memset(eps_t[:], EPS), writes=[R_const])
        S.op("dve", lambda e: e.memset(one_t[:], 1.0), writes=[R_const])
        S.dma("sp", lambda e: e.dma_start(out=ident[:], in_=c_ident[:, :]), "ld_c", writes=[R_const],
              accumulate_writers=True)
        S.dma("sp", lambda e: e.dma_start(out=maskt[:], in_=c_mask[:, :]), "ld_c", writes=[R_const],
              accumulate_writers=True)
        S.dma("sp", lambda e: e.dma_start(out=cos_t[:], in_=c_cos.rearrange("(b p) c -> p b c", p=128)),
              "ld_c", writes=[R_const], accumulate_writers=True)
        S.dma("sp", lambda e: e.dma_start(out=sin_t[:], in_=c_sin.rearrange("(b p) c -> p b c", p=128)),
              "ld_c", writes=[R_const], accumulate_writers=True)

        def Rh(kc, ts):
            return S.R("h", kc, ts)

        def Rhr(hin, kc, ts):
            return [Rh(kc, ts)] if hin is hT else []

        def load_cols(dst, src_row, n, sem, res, q="sp"):
            S.dma(q, lambda e: e.dma_start(out=dst, in_=src_row.rearrange("(c p) -> p c", p=128),
                                           allow_slow_non_contiguous=True), sem, writes=[res])

        def wload(dst, W2d, kcn, col0, ncols, sem, res):
            Wv = W2d.rearrange("(kc p) n -> p kc n", p=128)
            first = True
            for k0 in range(0, kcn, 4):
                k1 = min(kcn, k0 + 4)
                S.dma("pool", lambda e, k0=k0, k1=k1: e.dma_start(
                    out=dst[:, k0:k1, 0:ncols], in_=Wv[:, k0:k1, col0:col0 + ncols]), sem,
                    writes=[res], accumulate_writers=not first)
                first = False

        def precast_all():
            for l_ in range(DEPTH):
                for nm, (W3, K_, N_, tc_) in WSP.items():
                    kcn = K_ // 128
                    Wv = W3[l_].rearrange("(kc p) n -> p kc n", p=128)
                    for t in range(N_ // tc_):
                        dv = wbuf[(nm, l_)][t].rearrange("p (kc c) -> p kc c", c=tc_)
                        for k0 in range(0, kcn, 4):
                            k1 = min(kcn, k0 + 4)
                            S.dma("pool", lambda e, dv=dv, Wv=Wv, k0=k0, k1=k1, t=t, tc_=tc_: e.dma_start(
                                out=dv[:, k0:k1, :], in_=Wv[:, k0:k1, t * tc_:(t + 1) * tc_]),
                                f"pc_{nm}_{l_}", writes=[S.R("wb", nm, l_)], accumulate_writers=True)

        def wtile(dst, nm, l_, t, sem, res, acc=False):
            tc_ = WSP[nm][3]
            S.dma("sp", lambda e: e.dma_start(
                out=dst[:, :, :], in_=wbuf[(nm, l_)][t].rearrange("p (kc c) -> p kc c", c=tc_)),
                sem, reads=[S.R("wb", nm, l_)], writes=[res], accumulate_writers=acc)

        def rms_to_uT(st, stoks, gain_row, uT, R_uT, tag, out_dram=None, hin=None):
            if hin is None:
                hin = hT
            with ExitStack() as ls:
                gcol = sb(f"gcol_{tag}", [128, KD], F32, ls)
                hb = [sb(f"hb{i}_{tag}", [128, 512], F32, ls) for i in range(3)]
                sq = [sb(f"sq{i}_{tag}", [128, 512], BF16, ls) for i in range(2)]
                rt = sb(f"rt_{tag}", [128, 512], F32, ls)
                rstd = sb(f"rstd_{tag}", [128, 512], F32, ls)
                ob = [sb(f"ob{i}_{tag}", [128, 512], F32, ls) for i in range(2)] if out_dram is not None else None
                pss = ps(f"pss_{tag}", [128, 512], F32, ls)
                R_g = S.R("gcol", tag)
                load_cols(gcol[:, :], gain_row, KD, "ld_small", R_g)
                R_hb = [S.R("hb", tag, i) for i in range(3)]
                R_sq = [S.R("sq", tag, i) for i in range(2)]
                R_pss = S.R("pss", tag)
                R_rt = S.R("rt", tag)
                R_rstd = S.R("rstd", tag)
                R_ob = [S.R("ob", tag, i) for i in range(2)]
                n_hb = 0
                off = 0
                for (ts, tn) in stoks:
                    for kc in range(KD):
                        b = n_hb % 3
                        n_hb += 1
                        S.dma("sp", lambda e, b=b, kc=kc, ts=ts, tn=tn: e.dma_start(
                            out=hb[b][:, :tn], in_=hin[kc * 128:(kc + 1) * 128, ts:ts + tn]),
                            f"ld_hb{b}", reads=Rhr(hin, kc, ts), writes=[R_hb[b]])
                        s2 = kc % 2
                        S.op("act", lambda e, b=b, s2=s2, tn=tn: e.activation(
                            out=sq[s2][:, :tn], in_=hb[b][:, :tn], func=AF.Square),
                            reads=[R_hb[b]], writes=[R_sq[s2]])
                        S.op("pe", lambda e, s2=s2, tn=tn, kc=kc: e.matmul(
                            pss[:, :tn], ones_bf[:, :], sq[s2][:, :tn], start=(kc == 0), stop=(kc == KD - 1)),
                            reads=[R_sq[s2], R_const], writes=[R_pss])
                    S.op("act", lambda e, tn=tn: e.activation(
                        out=rt[:, :tn], in_=pss[:, :tn], func=AF.Sqrt, bias=eps_t[:, :], scale=1.0 / D),
                        reads=[R_pss, R_const], writes=[R_rt])
                    S.op("dve", lambda e, tn=tn: e.reciprocal(out=rstd[:, :tn], in_=rt[:, :tn]),
                         reads=[R_rt], writes=[R_rstd])
                    for kc in range(KD):
                        b = n_hb % 3
                        n_hb += 1
                        S.dma("sp", lambda e, b=b, kc=kc, ts=ts, tn=tn: e.dma_start(
                            out=hb[b][:, :tn], in_=hin[kc * 128:(kc + 1) * 128, ts:ts + tn]),
                            f"ld_hb{b}", reads=Rhr(hin, kc, ts), writes=[R_hb[b]])
                        if out_dram is None:
                            S.op("dve", lambda e, b=b, kc=kc, tn=tn, off=off: e.scalar_tensor_tensor(
                                out=uT[:, kc, off:off + tn], in0=hb[b][:, :tn], scalar=gcol[:, kc:kc + 1],
                                in1=rstd[:, :tn], op0=ALU.mult, op1=ALU.mult),
                                reads=[R_hb[b], R_g, R_rstd], writes=[R_uT])
                        else:
                            o2 = kc % 2
                            S.op("dve", lambda e, b=b, kc=kc, tn=tn, o2=o2: e.scalar_tensor_tensor(
                                out=ob[o2][:, :tn], in0=hb[b][:, :tn], scalar=gcol[:, kc:kc + 1],
                                in1=rstd[:, :tn], op0=ALU.mult, op1=ALU.mult),
                                reads=[R_hb[b], R_g, R_rstd], writes=[R_ob[o2]])
                            S.dma("sp", lambda e, kc=kc, ts=ts, tn=tn, o2=o2: e.dma_start(
                                out=out_dram[kc * 128:(kc + 1) * 128, ts:ts + tn], in_=ob[o2][:, :tn]),
                                f"st_ob{o2}", reads=[R_ob[o2]], writes=[S.R("out", kc, ts)])
                    off += tn
                S.barrier()

        def ffn(l, gain, pfx, tag, hin=None):
            if hin is None:
                hin = hT
            for si, stoks in enumerate(ST):
                ffn_body(l, gain, pfx, tag, hin, si, stoks)

        def ffn_body(l, gain, pfx, tag, hin, si, stoks):
            if True:
                ntok = sum(n for _, n in stoks)
                with ExitStack() as ls:
                    aT = sb(f"aT_{tag}", [128, KF, STMAX], BF16, ls)
                    R_aT = S.R("aT", tag, si)
                    with ExitStack() as l1:
                        uT = sb(f"uT_{tag}", [128, KD, STMAX], BF16, l1)
                        R_uT = S.R("uT", tag, si)
                        rms_to_uT(l1, stoks, gain[l, :], uT, R_uT, f"{tag}n", hin=hin)
                        wgt = [sb(f"wg{i}_{tag}", [128, KD, 256], BF16, l1) for i in range(2)]
                        wut = [sb(f"wu{i}_{tag}", [128, KD, 256], BF16, l1) for i in range(2)]
                        sg = [sb(f"sg{i}_{tag}", [128, 512], BF16, l1) for i in range(2)]
                        psg = [ps(f"psg{i}_{tag}", [128, 512], F32, l1) for i in range(2)]
                        psu = [ps(f"psu{i}_{tag}", [128, 512], F32, l1) for i in range(2)]
                        R_w = [S.R("wgu", tag, i) for i in range(2)]
                        R_sg = [S.R("sg", tag, i) for i in range(2)]
                        R_pg = [S.R("psg", tag, i) for i in range(2)]
                        R_pu = [S.R("psu", tag, i) for i in range(2)]

                        def ldw(m2):
                            s = m2 % 2
                            wtile(wgt[s], pfx + "_wg", l, m2, f"ld_w{s}", R_w[s])
                            wtile(wut[s], pfx + "_wu", l, m2, f"ld_w{s}", R_w[s], acc=True)

                        ldw(0)
                        ldw(1)
                        it = 0
                        for m2 in range(KF // 2):
                            s = m2 % 2
                            off = 0
                            for (ts, tn) in stoks:
                                for mm in range(2):
                                    m = 2 * m2 + mm
                                    p = it % 2
                                    it += 1
                                    for kc in range(KD):
                                        S.op("pe", lambda e, p=p, s=s, kc=kc, mm=mm, off=off, tn=tn: e.matmul(
                                            psg[p][:, :tn], wgt[s][:, kc, mm * 128:(mm + 1) * 128],
                                            uT[:, kc, off:off + tn], start=(kc == 0), stop=(kc == KD - 1)),
                                            reads=[R_w[s], R_uT], writes=[R_pg[p]])
                                    for kc in range(KD):
                                        S.op("pe", lambda e, p=p, s=s, kc=kc, mm=mm, off=off, tn=tn: e.matmul(
                                            psu[p][:, :tn], wut[s][:, kc, mm * 128:(mm + 1) * 128],
                                            uT[:, kc, off:off + tn], start=(kc == 0), stop=(kc == KD - 1)),
                                            reads=[R_w[s], R_uT], writes=[R_pu[p]])
                                    S.op("act", lambda e, p=p, tn=tn: e.activation(
                                        out=sg[p][:, :tn], in_=psg[p][:, :tn], func=AF.Silu),
                                        reads=[R_pg[p]], writes=[R_sg[p]])
                                    S.op("dve", lambda e, p=p, m=m, off=off, tn=tn: e.tensor_tensor(
                                        out=aT[:, m, off:off + tn], in0=psu[p][:, :tn], in1=sg[p][:, :tn],
                                        op=ALU.mult), reads=[R_pu[p], R_sg[p]], writes=[R_aT])
                                off += tn
                            if m2 + 2 < KF // 2:
                                ldw(m2 + 2)
                        S.barrier()
                    with ExitStack() as l2:
                        wdt = [sb(f"wd{i}_{tag}", [128, KF, 256], BF16, l2) for i in range(2)]
                        hr = [sb(f"hr{i}_{tag}", [128, 512], F32, l2) for i in range(2)]
                        hn = [sb(f"hn{i}_{tag}", [128, 512], F32, l2) for i in range(2)]
                        pso = [ps(f"pso{i}_{tag}", [128, 512], F32, l2) for i in range(2)]
                        R_wd = [S.R("wd", tag, i) for i in range(2)]
                        R_hr = [S.R("hr", tag, i) for i in range(2)]
                        R_hn = [S.R("hn", tag, i) for i in range(2)]
                        R_po = [S.R("pso", tag, i) for i in range(2)]

                        def ldd(d2):
                            s = d2 % 2
                            wtile(wdt[s], pfx + "_wd", l, d2, f"ld_w{s}", R_wd[s])

                        ldd(0)
                        ldd(1)
                        it = 0
                        for d2 in range(KD // 2):
                            s = d2 % 2
                            off = 0
                            for (ts, tn) in stoks:
                                for dd in range(2):
                                    d = 2 * d2 + dd
                                    p = it % 2
                                    it += 1
                                    S.dma("sp", lambda e, p=p, d=d, ts=ts, tn=tn: e.dma_start(
                                        out=hr[p][:, :tn], in_=hin[d * 128:(d + 1) * 128, ts:ts + tn]),
                                        f"ld_hr{p}", reads=Rhr(hin, d, ts), writes=[R_hr[p]])
                                    for m in range(KF):
                                        S.op("pe", lambda e, p=p, s=s, m=m, dd=dd, off=off, tn=tn: e.matmul(
                                            pso[p][:, :tn], wdt[s][:, m, dd * 128:(dd + 1) * 128],
                                            aT[:, m, off:off + tn], start=(m == 0), stop=(m == KF - 1)),
                                            reads=[R_wd[s], R_aT], writes=[R_po[p]])
                                    S.op("dve", lambda e, p=p, tn=tn: e.scalar_tensor_tensor(
                                        out=hn[p][:, :tn], in0=pso[p][:, :tn], scalar=0.5, in1=hr[p][:, :tn],
                                        op0=ALU.mult, op1=ALU.add), reads=[R_po[p], R_hr[p]], writes=[R_hn[p]])
                                    S.dma("sp", lambda e, p=p, d=d, ts=ts, tn=tn: e.dma_start(
                                        out=hT[d * 128:(d + 1) * 128, ts:ts + tn], in_=hn[p][:, :tn]),
                                        f"st_hn{p}", reads=[R_hn[p]], writes=[Rh(d, ts)])
                                off += tn
                            if d2 + 2 < KD // 2:
                                ldd(d2 + 2)
                        S.barrier()

        def mixer_proj(l):
            for si, stoks in enumerate(ST):
                mixer_proj_body(l, si, stoks)

        def mixer_proj_body(l, si, stoks):
            if True:
                t0 = stoks[0][0]
                ntok = sum(n for _, n in stoks)
                blocks = tiles_of(ntok, 128)
                with ExitStack() as ls:
                    uT = sb("uT_m", [128, KD, STMAX], BF16, ls)
                    R_uT = S.R("uT_m", si)
                    rms_to_uT(ls, stoks, norm_mix[l, :], uT, R_uT, "mn")
                    wt = [sb(f"win{i}", [128, KD, 512], BF16, ls) for i in range(2)]
                    R_w = [S.R("win", i) for i in range(2)]
                    bg = sb("bgcol", [128, 32], F32, ls)
                    R_bg = S.R("bgcol")
                    load_cols(bg[:, :], b_gate[l, :], 32, "ld_small", R_bg)
                    qk_sb = [sb(f"qk_sb{i}", [128, 512], BF16, ls) for i in range(2)]
                    t1 = [sb(f"rt1_{i}", [128, 4, 16], F32, ls) for i in range(2)]
                    t2 = [sb(f"rt2_{i}", [128, 4, 16], F32, ls) for i in range(2)]
                    x12 = [sb(f"x12_{i}", [128, 4, 32], F32, ls) for i in range(2)]
                    stage = sb("qk_stage", [128, 4, STMAX], BF16, ls)
                    vst = [sb(f"vst{i}", [128, 512], BF16, ls) for i in range(2)]
                    fst = [sb(f"fst{i}", [128, 512], F32, ls) for i in range(2)]
                    gst = [sb(f"gst{i}", [128, 512], BF16, ls) for i in range(2)]
                    pm = [ps(f"pm{i}", [128, 512], F32, ls) for i in range(3)]
                    pt = [ps(f"ptr{i}", [128, 512], BF16, ls) for i in range(2)]
                    R_qk = [S.R("qk_sb", i) for i in range(2)]
                    R_t = [S.R("ropet", i) for i in range(2)]
                    R_stage = S.R("qk_stage")
                    R_vst = [S.R("vst", i) for i in range(2)]
                    R_fst = [S.R("fst", i) for i in range(2)]
                    R_gst = [S.R("gst", i) for i in range(2)]
                    R_pm = [S.R("pm", i) for i in range(3)]
                    R_pt = [S.R("ptr", i) for i in range(2)]
                    NG = NIN // 512

                    def ldw(g):
                        wload(wt[g % 2], w_in[l], KD, g * 512, 512, f"ld_w{g % 2}", R_w[g % 2])

                    import os
                    GL = [int(x) for x in os.environ.get("K_GL", ",".join(str(i) for i in range(NG))).split(",")]
                    norope = bool(os.environ.get("K_NOROPE"))
                    notr = bool(os.environ.get("K_NOTR"))
                    for gi_ in range(min(2, len(GL))):
                        wtile(wt[gi_ % 2], "w_in", l, GL[gi_], f"ld_w{gi_ % 2}", R_w[gi_ % 2])
                    it = 0
                    itq = 0
                    for gi_, g in enumerate(GL):
                        s = gi_ % 2
                        if g < 12:
                            for (bs, bn) in blocks:
                                p = it % 3
                                it += 1
                                gb = (t0 + bs) // 128
                                for kc in range(KD):
                                    S.op("pe", lambda e, p=p, s=s, kc=kc, bs=bs, bn=bn: e.matmul(
                                        pm[p][:bn, :], uT[:, kc, bs:bs + bn], wt[s][:, kc, :],
                                        start=(kc == 0), stop=(kc == KD - 1)),
                                        reads=[R_w[s], R_uT], writes=[R_pm[p]])
                                if g < 8:
                                    q2 = itq % 2
                                    itq += 1
                                    S.op("act", lambda e, p=p, q2=q2, bn=bn: e.activation(
                                        out=qk_sb[q2][:bn, :], in_=pm[p][:bn, :], func=AF.Copy),
                                        reads=[R_pm[p]], writes=[R_qk[q2]])
                                    pv = pm[p][:, :].rearrange("p (h d) -> p h d", h=4)
                                    qv = qk_sb[q2][:, :].rearrange("p (h d) -> p h d", h=4)
                                    cv = cos_t[:, gb, :].rearrange("p (h d) -> p h d", h=4)
                                    sv = sin_t[:, gb, :].rearrange("p (h d) -> p h d", h=4)
                                    if not norope:
                                        xr32 = x12[q2]
                                        S.op("act", lambda e, bn=bn, pv=pv, xr32=xr32: e.activation(
                                            out=xr32[:bn], in_=pv[:bn, :, 0:32], func=AF.Copy),
                                            reads=[R_pm[p]], writes=[R_t[q2]])
                                        S.op("dve", lambda e, bn=bn, xr32=xr32, cv=cv, q2=q2: e.tensor_tensor(
                                            out=t1[q2][:bn], in0=xr32[:bn, :, 0:16], in1=cv[:bn], op=ALU.mult),
                                            reads=[R_t[q2], R_const], writes=[R_t[q2]])
                                        S.op("dve", lambda e, bn=bn, xr32=xr32, sv=sv, q2=q2: e.tensor_tensor(
                                            out=t2[q2][:bn], in0=xr32[:bn, :, 16:32], in1=sv[:bn], op=ALU.mult),
                                            reads=[R_t[q2], R_const], writes=[R_t[q2]])
                                        S.op("dve", lambda e, bn=bn, qv=qv, q2=q2: e.tensor_tensor(
                                            out=qv[:bn, :, 0:16], in0=t1[q2][:bn], in1=t2[q2][:bn], op=ALU.subtract),
                                            reads=[R_t[q2]], writes=[R_qk[q2]])
                                        S.op("dve", lambda e, bn=bn, xr32=xr32, cv=cv, q2=q2: e.tensor_tensor(
                                            out=t1[q2][:bn], in0=xr32[:bn, :, 16:32], in1=cv[:bn], op=ALU.mult),
                                            reads=[R_t[q2], R_const, R_qk[q2]], writes=[R_t[q2]])
                                        S.op("dve", lambda e, bn=bn, xr32=xr32, sv=sv, q2=q2: e.tensor_tensor(
                                            out=t2[q2][:bn], in0=xr32[:bn, :, 0:16], in1=sv[:bn], op=ALU.mult),
                                            reads=[R_t[q2], R_const], writes=[R_t[q2]])
                                        S.op("dve", lambda e, bn=bn, qv=qv, q2=q2: e.tensor_tensor(
                                            out=qv[:bn, :, 16:32], in0=t1[q2][:bn], in1=t2[q2][:bn], op=ALU.add),
                                            reads=[R_t[q2]], writes=[R_qk[q2]])
                                    if not notr:
                                        for hc in range(4):
                                            S.op("pe", lambda e, q2=q2, hc=hc, bn=bn: e.transpose(
                                                pt[q2][:, hc * 128:hc * 128 + bn], qk_sb[q2][:bn, hc * 128:(hc + 1) * 128],
                                                ident[:bn, :bn]), reads=[R_qk[q2], R_const], writes=[R_pt[q2]])
                                        ptv = pt[q2][:, :].rearrange("p (h t) -> p h t", h=4)
                                        S.op("act", lambda e, ptv=ptv, bs=bs, bn=bn: e.activation(
                                            out=stage[:, :, bs:bs + bn], in_=ptv[:, :, :bn], func=AF.Copy),
                                            reads=[R_pt[q2]], writes=[R_stage])
                                else:
                                    v2 = it % 2
                                    S.op("act", lambda e, p=p, v2=v2, bn=bn: e.activation(
                                        out=vst[v2][:bn, :], in_=pm[p][:bn, :], func=AF.Copy),
                                        reads=[R_pm[p]], writes=[R_vst[v2]])
                                    S.dma("sp", lambda e, v2=v2, bs=bs, bn=bn, g=g: e.dma_start(
                                        out=v_d[t0 + bs:t0 + bs + bn, (g - 8) * 512:(g - 7) * 512], in_=vst[v2][:bn, :]),
                                        f"st_v{v2}", reads=[R_vst[v2]], writes=[S.R("v_d", g, t0 + bs)])
                            if g < 8:
                                dst = qT_d if g < 4 else kT_d
                                i0 = (g % 4) * 4
                                S.dma("sp", lambda e, dst=dst, i0=i0: e.dma_start(
                                    out=dst[i0:i0 + 4, :, t0:t0 + ntok].rearrange("h p t -> p h t"),
                                    in_=stage[:, :, 0:ntok]), "st_stage", reads=[R_stage],
                                    writes=[S.R("qkT_d", g, si)])
                        else:
                            off = 0
                            for (ts, tn) in stoks:
                                for mm in range(4):
                                    col = g * 512 + mm * 128
                                    p = it % 3
                                    it += 1
                                    for kc in range(KD):
                                        S.op("pe", lambda e, p=p, s=s, kc=kc, mm=mm, off=off, tn=tn: e.matmul(
                                            pm[p][:, :tn], wt[s][:, kc, mm * 128:(mm + 1) * 128],
                                            uT[:, kc, off:off + tn], start=(kc == 0), stop=(kc == KD - 1)),
                                            reads=[R_w[s], R_uT], writes=[R_pm[p]])
                                    f2 = it % 2
                                    if col < 6144 + 2 * LW:
                                        c0 = col - 6144
                                        dstT = xrT_d if c0 < LW else yrT_d
                                        r0 = c0 % LW
                                        S.op("act", lambda e, p=p, f2=f2, tn=tn: e.activation(
                                            out=fst[f2][:, :tn], in_=pm[p][:, :tn], func=AF.Copy),
                                            reads=[R_pm[p]], writes=[R_fst[f2]])
                                        S.dma("sp", lambda e, f2=f2, dstT=dstT, r0=r0, ts=ts, tn=tn: e.dma_start(
                                            out=dstT[r0:r0 + 128, ts:ts + tn], in_=fst[f2][:, :tn]),
                                            f"st_f{f2}", reads=[R_fst[f2]], writes=[S.R("xy_d", col, ts)])
                                    else:
                                        c0 = col - (6144 + 2 * LW)
                                        S.op("act", lambda e, p=p, f2=f2, tn=tn, c0=c0: e.activation(
                                            out=gst[f2][:, :tn], in_=pm[p][:, :tn], func=AF.Sigmoid,
                                            bias=bg[:, c0 // 128:c0 // 128 + 1]),
                                            reads=[R_pm[p], R_bg], writes=[R_gst[f2]])
                                        S.dma("sp", lambda e, f2=f2, c0=c0, ts=ts, tn=tn: e.dma_start(
                                            out=gT_d[c0:c0 + 128, ts:ts + tn], in_=gst[f2][:, :tn]),
                                            f"st_g{f2}", reads=[R_gst[f2]], writes=[S.R("g_d", col, ts)])
                                off += tn
                        if gi_ + 2 < len(GL):
                            wtile(wt[gi_ % 2], "w_in", l, GL[gi_ + 2], f"ld_w{gi_ % 2}", R_w[gi_ % 2])
                    S.barrier()

        def attention(l):
            with ExitStack() as ls:
                qT2 = [sb(f"qT2_{i}", [128, 2, T], BF16, ls) for i in range(2)]
                kT2 = [sb(f"kT2_{i}", [128, 2, T], BF16, ls) for i in range(2)]
                vx = [sb(f"vx_{i}", [128, NB, VD + 1], BF16, ls) for i in range(2)]
                pT = [sb(f"pT_{i}", [128, 512], BF16, ls) for i in range(3)]
                oc = [sb(f"oc_{i}", [128, 4, VD], F32, ls) for i in range(2)]
                rl = sb("rl", [128, 8], F32, ls)
                osb = sb("osb", [128, VD], F32, ls)
                junk = sb("junk", [128, VD], F32, ls)
                ss = sb("ss", [128, 4], F32, ls)
                on = sb("on", [128, VD], BF16, ls)
                ostage = [sb(f"ostage{i}", [128, 2, 512], BF16, ls) for i in range(2)]
                lamv = sb("lamv", [128, 8], F32, ls)
                lamc = sb("lamc", [128, 2], F32, ls)
                subl = sb("subl", [128, VD], F32, ls)
                pS = [ps(f"pS{i}", [128, 512], F32, ls) for i in range(2)]
                acc = [ps(f"acc{i}", [128, 512], F32, ls) for i in range(4)]
                ptr = ps("ptr_a", [128, 1024], BF16, ls)
                plam = ps("plam", [128, 512], F32, ls)
                R_q = [S.R("qT2", i) for i in range(2)]
                R_v = [S.R("vx", i) for i in range(2)]
                R_pT = [S.R("pT", i) for i in range(3)]
                R_oc = [S.R("oc", i) for i in range(2)]
                R_pS = [S.R("pS", i) for i in range(2)]
                R_acc = [S.R("acc", i) for i in range(4)]
                R_fin = S.R("fin")
                R_on = S.R("on")
                R_ptr = S.R("ptr_a")
                R_ost = [S.R("ostage", i) for i in range(2)]
                R_lam = S.R("lam")
                R_subl = S.R("subl")
                for j, src in enumerate((lam_q1, lam_k1, lam_q2, lam_k2)):
                    S.dma("sp", lambda e, j=j, src=src: e.dma_start(
                        out=lamv[:, j:j + 1], in_=src[l, :].rearrange("(p o) -> p o", o=1)),
                        "ld_small", writes=[R_lam], accumulate_writers=(j > 0))
                S.dma("sp", lambda e: e.dma_start(out=lamc[:, :], in_=c_laminit[l, :].partition_broadcast(128)),
                      "ld_small", writes=[R_lam], accumulate_writers=True)
                S.dma("sp", lambda e: e.dma_start(out=subl[:, :], in_=attn_subln[l, :].partition_broadcast(128)),
                      "ld_small", writes=[R_subl])
                S.op("dve", lambda e: e.tensor_tensor(out=lamv[:, 4:5], in0=lamv[:, 0:1], in1=lamv[:, 1:2],
                                                      op=ALU.mult), reads=[R_lam], writes=[R_lam])
                S.op("dve", lambda e: e.tensor_tensor(out=lamv[:, 5:6], in0=lamv[:, 2:3], in1=lamv[:, 3:4],
                                                      op=ALU.mult), reads=[R_lam], writes=[R_lam])
                onesf = sb("onesf", [128, 128], F32, ls)
                S.op("dve", lambda e: e.memset(onesf[:], 1.0), writes=[R_lam])
                S.op("pe", lambda e: e.matmul(plam[:, 0:2], onesf[:, :], lamv[:, 4:6], start=True, stop=True),
                     reads=[R_lam], writes=[S.R("plam")])
                S.op("act", lambda e: e.activation(out=lamv[:, 6:8], in_=plam[:, 0:2], func=AF.Exp),
                     reads=[S.R("plam")], writes=[R_lam])
                S.op("dve", lambda e: e.tensor_tensor(out=lamv[:, 4:5], in0=lamv[:, 7:8], in1=lamv[:, 6:7],
                                                      op=ALU.subtract), reads=[R_lam], writes=[R_lam])
                S.op("dve", lambda e: e.tensor_tensor(out=lamv[:, 4:5], in0=lamv[:, 4:5], in1=lamc[:, 0:1],
                                                      op=ALU.subtract), reads=[R_lam], writes=[R_lam])
                S.op("dve", lambda e: e.tensor_scalar(out=subl[:, :], in0=subl[:, :], scalar1=lamc[:, 1:2],
                                                      scalar2=None, op0=ALU.mult),
                     reads=[R_lam, R_subl], writes=[R_subl])
                for i in range(2):
                    S.op("dve", lambda e, i=i: e.memset(vx[i][:, :, VD:VD + 1], 1.0), writes=[R_v[i]])

                def load_head(h):
                    b = h % 2
                    S.dma("sp", lambda e, b=b, h=h: e.dma_start(
                        out=qT2[b][:, :, :], in_=qT_d[2 * h:2 * h + 2, :, :].rearrange("c p t -> p c t")),
                        f"ld_q{b}", writes=[R_q[b]])
                    S.dma("sp", lambda e, b=b, h=h: e.dma_start(
                        out=kT2[b][:, :, :], in_=kT_d[2 * h:2 * h + 2, :, :].rearrange("c p t -> p c t")),
                        f"ld_q{b}", writes=[R_q[b]], accumulate_writers=True)
                    for j0 in range(0, NB, 8):
                        j1 = min(NB, j0 + 8)
                        S.dma("sp", lambda e, b=b, h=h, j0=j0, j1=j1: e.dma_start(
                            out=vx[b][:, j0:j1, 0:VD],
                            in_=v_d[j0 * 128:j1 * 128, h * VD:(h + 1) * VD].rearrange("(j p) e -> p j e", p=128)),
                            f"ld_v{b}", reads=[], writes=[R_v[b]], accumulate_writers=True)

                load_head(0)
                ip = 0
                io = 0
                scale = HD ** -0.5
                for h in range(NH):
                    b = h % 2
                    if h + 1 < NH:
                        load_head(h + 1)
                    for (qs, qn) in TT:
                        nsb = (qn + 127) // 128
                        last_blk = (qs + qn - 1) // 128
                        for c in range(2):
                            units = []
                            for j in range(last_blk + 1):
                                ks, kn = TB[j]
                                q_lo = max(qs, ks) - qs
                                units.append((j, ks, kn, q_lo, ip % 2, ip % 3))
                                ip += 1

                            def emit_S(u, b=b, c=c, qs=qs, qn=qn):
                                (j, ks, kn, q_lo, p2, p3) = u
                                S.op("pe", lambda e: e.matmul(
                                    pS[p2][:kn, q_lo:qn], kT2[b][:, c, ks:ks + kn], qT2[b][:, c, qs + q_lo:qs + qn],
                                    start=True, stop=True), reads=[R_q[b]], writes=[R_pS[p2]])

                            emit_S(units[0])
                            for ui, u in enumerate(units):
                                (j, ks, kn, q_lo, p2, p3) = u
                                w = qn - q_lo
                                if ui + 1 < len(units):
                                    emit_S(units[ui + 1])
                                S.op("act", lambda e, kn=kn, q_lo=q_lo, qn=qn, p2=p2, p3=p3: e.activation(
                                    out=pT[p3][:kn, q_lo:qn], in_=pS[p2][:kn, q_lo:qn], func=AF.Exp, scale=scale),
                                    reads=[R_pS[p2]], writes=[R_pT[p3]])
                                if ks >= qs:
                                    mw = min(128, w)
                                    S.op("dve", lambda e, kn=kn, q_lo=q_lo, mw=mw, p3=p3: e.tensor_tensor(
                                        out=pT[p3][:kn, q_lo:q_lo + mw], in0=pT[p3][:kn, q_lo:q_lo + mw],
                                        in1=maskt[:kn, :mw], op=ALU.mult),
                                        reads=[R_pT[p3], R_const], writes=[R_pT[p3]])
                                for sbi in range(nsb):
                                    s0 = sbi * 128
                                    if s0 < q_lo:
                                        continue
                                    sn = min(128, qn - s0)
                                    diag = (qs + s0) // 128
                                    S.op("pe", lambda e, b=b, kn=kn, s0=s0, sn=sn, j=j, sbi=sbi, p3=p3, diag=diag: e.matmul(
                                        acc[sbi][:sn, 0:VD + 1], pT[p3][:kn, s0:s0 + sn], vx[b][:kn, j, :],
                                        start=(j == 0), stop=(j == diag)),
                                        reads=[R_pT[p3], R_v[b]], writes=[R_acc[sbi]])
                            for sbi in range(nsb):
                                sn = min(128, qn - sbi * 128)
                                S.op("dve", lambda e, sbi=sbi, sn=sn, c=c: e.reciprocal(
                                    out=rl[:sn, c * 4 + sbi:c * 4 + sbi + 1], in_=acc[sbi][:sn, VD:VD + 1]),
                                    reads=[R_acc[sbi]], writes=[R_fin])
                                S.op("dve", lambda e, sbi=sbi, sn=sn, c=c: e.tensor_scalar(
                                    out=oc[c][:sn, sbi, :], in0=acc[sbi][:sn, 0:VD],
                                    scalar1=rl[:sn, c * 4 + sbi:c * 4 + sbi + 1], scalar2=None, op0=ALU.mult),
                                    reads=[R_acc[sbi], R_fin], writes=[R_oc[c]])
                        o2 = io % 2
                        io += 1
                        for sbi in range(nsb):
                            s0 = sbi * 128
                            sn = min(128, qn - s0)
                            S.op("dve", lambda e, sbi=sbi, sn=sn: e.scalar_tensor_tensor(
                                out=osb[:sn, :], in0=oc[1][:sn, sbi, :], scalar=lamv[:sn, 4:5], in1=oc[0][:sn, sbi, :],
                                op0=ALU.mult, op1=ALU.add), reads=[R_oc[0], R_oc[1], R_lam], writes=[R_fin])
                            S.op("dve", lambda e: e.memset(ss[:, 0:1], 0.0), writes=[S.R("ss")])
                            S.op("act", lambda e, sn=sn, sbi=sbi: e.activation(
                                out=junk[:sn, :], in_=osb[:sn, :], func=AF.Square, accum_out=ss[:sn, 0:1]),
                                reads=[R_fin], writes=[S.R("ss")])
                            S.op("act", lambda e, sn=sn: e.activation(
                                out=ss[:sn, 1:2], in_=ss[:sn, 0:1], func=AF.Sqrt, bias=eps_t[:sn, :], scale=1.0 / VD),
                                reads=[S.R("ss"), R_const], writes=[S.R("ss")])
                            S.op("dve", lambda e, sn=sn: e.reciprocal(out=ss[:sn, 2:3], in_=ss[:sn, 1:2]),
                                 reads=[S.R("ss")], writes=[S.R("ss")])
                            S.op("dve", lambda e, sn=sn: e.scalar_tensor_tensor(
                                out=on[:sn, :], in0=osb[:sn, :], scalar=ss[:sn, 2:3], in1=subl[:sn, :],
                                op0=ALU.mult, op1=ALU.mult), reads=[R_fin, S.R("ss"), R_subl], writes=[R_on])
                            for e2 in range(2):
                                S.op("pe", lambda e, e2=e2, sn=sn: e.transpose(
                                    ptr[:, e2 * 128:e2 * 128 + sn], on[:sn, e2 * 128:(e2 + 1) * 128], ident[:sn, :sn]),
                                    reads=[R_on, R_const], writes=[R_ptr])
                            ptv = ptr[:, 0:256].rearrange("p (a t) -> p a t", a=2)
                            S.op("act", lambda e, ptv=ptv, s0=s0, sn=sn, o2=o2: e.activation(
                                out=ostage[o2][:, :, s0:s0 + sn], in_=ptv[:, :, :sn], func=AF.Copy),
                                reads=[R_ptr], writes=[R_ost[o2]])
                        S.dma("sp", lambda e, h=h, qs=qs, qn=qn, o2=o2: e.dma_start(
                            out=aoT_d[h * VD:(h + 1) * VD, qs:qs + qn].rearrange("(a p) t -> p a t", p=128),
                            in_=ostage[o2][:, :, 0:qn]), f"st_o{o2}", reads=[R_ost[o2]],
                            writes=[S.R("aoT_d", h, qs)])
                S.barrier()

        def lru(l):
            with ExitStack() as ls:
                NP = len(GATE_PAIRS)
                wexp = sb("wexp", [128, 2 * 4 * NP, 128], BF16, ls)
                wstg = sb("wstg", [128, NP, 128], F32, ls)
                cw = sb("cw", [128, 4, KL], F32, ls)
                cb = sb("cb", [128, KL], F32, ls)
                gab = sb("gab", [128, KL], F32, ls)
                gxb = sb("gxb", [128, KL], F32, ls)
                ca = sb("ca", [128, KL], F32, ls)
                carry = sb("carry", [128, KL], F32, ls)
                xin = [sb(f"xin{i}", [128, 5, 3 + 512], F32, ls) for i in range(2)]
                xc = sb("xc", [128, 5, 512], F32, ls)
                xcb = sb("xcb", [128, 5, 512], BF16, ls)
                tr = sb("tr", [128, 5, 512], F32, ls)
                ti = sb("ti", [128, 5, 512], F32, ls)
                ta = sb("ta", [128, 5, 512], F32, ls)
                th = sb("th", [128, 5, 512], F32, ls)
                ty = sb("ty", [128, 5, 512], F32, ls)
                tg = sb("tg", [128, 5, 512], F32, ls)
                thy = sb("thy", [128, 5, 512], BF16, ls)
                pg = [ps(f"pgate{i}", [128, 512], F32, ls) for i in range(4)]
                R_w = S.R("wexp")
                R_ws = S.R("wstg")
                R_par = S.R("lrupar")
                R_carry = S.R("carry")
                R_xin = [S.R("xin", i) for i in range(2)]
                R_xc = S.R("xc")
                R_xcb = S.R("xcb")
                R_pg = [S.R("pgate", i) for i in range(4)]
                R_tr, R_ti, R_ta, R_th, R_ty, R_tg, R_thy = (S.R("lt", n) for n in ("r", "i", "a", "h", "y", "g", "hy"))
                for j in range(4):
                    S.dma("sp", lambda e, j=j: e.dma_start(
                        out=cw[:, j, :], in_=conv_w[l, j, :].rearrange("(c p) -> p c", p=128),
                        allow_slow_non_contiguous=True), "ld_small", writes=[R_par], accumulate_writers=(j > 0))
                for dst, src in ((cb, conv_b), (gab, gate_a_b), (gxb, gate_x_b), (ca, lru_a)):
                    S.dma("sp", lambda e, dst=dst, src=src: e.dma_start(
                        out=dst[:, :], in_=src[l, :].rearrange("(c p) -> p c", p=128),
                        allow_slow_non_contiguous=True), "ld_small", writes=[R_par], accumulate_writers=True)
                S.op("act", lambda e: e.activation(out=ca[:, :], in_=ca[:, :], func=AF.Exp, scale=-1.0),
                     reads=[R_par], writes=[R_par])
                S.op("act", lambda e: e.activation(out=ca[:, :], in_=ca[:, :], func=AF.Ln, bias=one_t[:, :], scale=1.0),
                     reads=[R_par, R_const], writes=[R_par])
                S.op("dve", lambda e: e.tensor_scalar(out=ca[:, :], in0=ca[:, :], scalar1=-8.0, scalar2=None,
                                                      op0=ALU.mult), reads=[R_par], writes=[R_par])
                S.op("dve", lambda e: e.memset(carry[:, :], 0.0), writes=[R_carry])
                for gi, gw in enumerate((gate_a_w, gate_x_w)):
                    for grp in range(4):
                        S.op("dve", lambda e: e.memset(wstg[:], 0.0), writes=[R_ws])
                        for bl in range(4):
                            n = grp * 4 + bl
                            base = 160 * bl
                            for (ci, r0, r1) in ((bl, base, 128 * (bl + 1)), (bl + 1, 128 * (bl + 1), base + 160)):
                                for (co, c0, c1) in ((bl, base, 128 * (bl + 1)), (bl + 1, 128 * (bl + 1), base + 160)):
                                    idx = GATE_PAIRS.index((ci, co))
                                    S.dma("sp", lambda e, idx=idx, gw=gw, n=n, r0=r0, r1=r1, c0=c0, c1=c1, ci=ci, co=co, base=base: e.dma_start(
                                        out=wstg[r0 - 128 * ci:r1 - 128 * ci, idx, c0 - 128 * co:c1 - 128 * co],
                                        in_=gw[l, n, r0 - base:r1 - base, c0 - base:c1 - base]),
                                        "ld_gw", reads=[R_ws], writes=[R_ws], accumulate_writers=True)
                        i0 = (gi * 4 + grp) * NP
                        S.op("dve", lambda e, i0=i0: e.tensor_copy(out=wexp[:, i0:i0 + NP, :], in_=wstg[:]),
                             reads=[R_ws], writes=[R_w, R_ws])
                for i in range(2):
                    S.op("dve", lambda e, i=i: e.memset(xin[i][:, :, 0:3], 0.0), writes=[R_xin[i]])
                it = 0
                ic = 0
                for (ts, tn) in TT:
                    for grp in range(4):
                        x2 = it % 2
                        it += 1
                        g5 = grp * 5
                        rows = xrT_d[grp * 640:(grp + 1) * 640, :].rearrange("(c p) t -> p c t", p=128)
                        yrows = yrT_d[grp * 640:(grp + 1) * 640, :].rearrange("(c p) t -> p c t", p=128)
                        hrows = hyT_d[grp * 640:(grp + 1) * 640, :].rearrange("(c p) t -> p c t", p=128)
                        if ts == 0:
                            S.dma("sp", lambda e, x2=x2, rows=rows, tn=tn: e.dma_start(
                                out=xin[x2][:, :, 3:3 + tn], in_=rows[:, :, 0:tn]), f"ld_xin{x2}", writes=[R_xin[x2]])
                        else:
                            S.dma("sp", lambda e, x2=x2, rows=rows, ts=ts, tn=tn: e.dma_start(
                                out=xin[x2][:, :, 0:3 + tn], in_=rows[:, :, ts - 3:ts + tn]), f"ld_xin{x2}",
                                writes=[R_xin[x2]])
                        S.dma("sp", lambda e, yrows=yrows, ts=ts, tn=tn: e.dma_start(
                            out=ty[:, :, :tn], in_=yrows[:, :, ts:ts + tn]), "ld_ty0", writes=[R_ty])
                        for ci in range(5):
                            ch = g5 + ci
                            S.op("dve", lambda e, x2=x2, ci=ci, ch=ch, tn=tn: e.tensor_scalar(
                                out=xc[:, ci, :tn], in0=xin[x2][:, ci, 3:3 + tn], scalar1=cw[:, 0, ch:ch + 1],
                                scalar2=cb[:, ch:ch + 1], op0=ALU.mult, op1=ALU.add),
                                reads=[R_xin[x2], R_par], writes=[R_xc])
                            for j in range(1, 4):
                                S.op("dve", lambda e, x2=x2, ci=ci, ch=ch, tn=tn, j=j: e.scalar_tensor_tensor(
                                    out=xc[:, ci, :tn], in0=xin[x2][:, ci, 3 - j:3 - j + tn], scalar=cw[:, j, ch:ch + 1],
                                    in1=xc[:, ci, :tn], op0=ALU.mult, op1=ALU.add),
                                    reads=[R_xin[x2], R_par, R_xc], writes=[R_xc])
                        S.op("act", lambda e, tn=tn: e.activation(out=xcb[:, :, :tn], in_=xc[:, :, :tn], func=AF.Copy),
                             reads=[R_xc], writes=[R_xcb])
                        S.op("dve", lambda e, tn=tn: e.tensor_tensor(
                            out=tg[:, :, :tn], in0=ty[:, :, :tn], in1=ty[:, :, :tn], op=ALU.mult),
                            reads=[R_ty], writes=[R_tg])
                        S.op("dve", lambda e, tn=tn: e.tensor_scalar(
                            out=tg[:, :, :tn], in0=tg[:, :, :tn], scalar1=0.044715, scalar2=1.0,
                            op0=ALU.mult, op1=ALU.add), reads=[R_tg], writes=[R_tg])
                        S.op("dve", lambda e, tn=tn: e.tensor_tensor(
                            out=tg[:, :, :tn], in0=tg[:, :, :tn], in1=ty[:, :, :tn], op=ALU.mult),
                            reads=[R_tg, R_ty], writes=[R_tg])
                        for co in range(5):
                            ch = g5 + co
                            pa = (2 * ic) % 4
                            px = (2 * ic + 1) % 4
                            ic += 1
                            nb = GATE_NBRS[co]
                            for gi, pp in ((0, pa), (1, px)):
                                for n_i, ci in enumerate(nb):
                                    idx = (gi * 4 + grp) * NP + GATE_PAIRS.index((ci, co))
                                    S.op("pe", lambda e, pp=pp, idx=idx, ci=ci, tn=tn, n_i=n_i, nb=nb: e.matmul(
                                        pg[pp][:, :tn], wexp[:, idx, :], xcb[:, ci, :tn],
                                        start=(n_i == 0), stop=(n_i == len(nb) - 1)),
                                        reads=[R_w, R_xcb], writes=[R_pg[pp]])
                            S.op("act", lambda e, co=co, pa=pa, ch=ch, tn=tn: e.activation(
                                out=tr[:, co, :tn], in_=pg[pa][:, :tn], func=AF.Sigmoid, bias=gab[:, ch:ch + 1]),
                                reads=[R_pg[pa], R_par], writes=[R_tr])
                            S.op("act", lambda e, co=co, px=px, ch=ch, tn=tn: e.activation(
                                out=ti[:, co, :tn], in_=pg[px][:, :tn], func=AF.Sigmoid, bias=gxb[:, ch:ch + 1]),
                                reads=[R_pg[px], R_par], writes=[R_ti])
                            S.op("act", lambda e, co=co, ch=ch, tn=tn: e.activation(
                                out=ta[:, co, :tn], in_=tr[:, co, :tn], func=AF.Exp, scale=ca[:, ch:ch + 1]),
                                reads=[R_tr, R_par], writes=[R_ta])
                        S.op("act", lambda e, tn=tn: e.activation(
                            out=tg[:, :, :tn], in_=tg[:, :, :tn], func=AF.Sigmoid, scale=1.5957691216057308),
                            reads=[R_tg], writes=[R_tg])
                        S.op("dve", lambda e, tn=tn: e.tensor_tensor(
                            out=ti[:, :, :tn], in0=ti[:, :, :tn], in1=xc[:, :, :tn], op=ALU.mult),
                            reads=[R_ti, R_xc], writes=[R_ti])
                        S.op("dve", lambda e, tn=tn: e.tensor_tensor(
                            out=tr[:, :, :tn], in0=ta[:, :, :tn], in1=ta[:, :, :tn], op=ALU.mult),
                            reads=[R_ta, R_tr], writes=[R_tr])
                        S.op("dve", lambda e, tn=tn: e.tensor_scalar(
                            out=tr[:, :, :tn], in0=tr[:, :, :tn], scalar1=-1.0, scalar2=1.0,
                            op0=ALU.mult, op1=ALU.add), reads=[R_tr], writes=[R_tr])
                        S.op("act", lambda e, tn=tn: e.activation(
                            out=tr[:, :, :tn], in_=tr[:, :, :tn], func=AF.Sqrt), reads=[R_tr], writes=[R_tr])
                        S.op("dve", lambda e, tn=tn: e.tensor_tensor(
                            out=tg[:, :, :tn], in0=tg[:, :, :tn], in1=ty[:, :, :tn], op=ALU.mult),
                            reads=[R_tg, R_ty], writes=[R_tg])
                        S.op("dve", lambda e, tn=tn: e.tensor_tensor(
                            out=ti[:, :, :tn], in0=ti[:, :, :tn], in1=tr[:, :, :tn], op=ALU.mult),
                            reads=[R_ti, R_tr], writes=[R_ti])
                        for co in range(5):
                            ch = g5 + co
                            S.op("dve", lambda e, co=co, ch=ch, tn=tn: e.tensor_tensor_scan(
                                out=th[:, co, :tn], data0=ta[:, co, :tn], data1=ti[:, co, :tn],
                                initial=carry[:, ch:ch + 1], op0=ALU.mult, op1=ALU.add),
                                reads=[R_ta, R_ti, R_carry], writes=[R_th])
                        for co in range(5):
                            ch = g5 + co
                            S.op("dve", lambda e, co=co, ch=ch, tn=tn: e.tensor_copy(
                                out=carry[:, ch:ch + 1], in_=th[:, co, tn - 1:tn]),
                                reads=[R_th], writes=[R_carry])
                        S.op("dve", lambda e, tn=tn: e.tensor_tensor(
                            out=thy[:, :, :tn], in0=tg[:, :, :tn], in1=th[:, :, :tn], op=ALU.mult),
                            reads=[R_tg, R_th], writes=[R_thy])
                        S.dma("sp", lambda e, hrows=hrows, ts=ts, tn=tn: e.dma_start(
                            out=hrows[:, :, ts:ts + tn], in_=thy[:, :, :tn]),
                            "st_hy0", reads=[R_thy], writes=[S.R("hyT_d", grp, ts)])
                S.barrier()

        def mixer_out(l):
            for si, stoks in enumerate(ST):
                mixer_out_body(l, si, stoks)

        def mixer_out_body(l, si, stoks):
            if True:
                t0 = stoks[0][0]
                ntok = sum(n for _, n in stoks)
                with ExitStack() as ls:
                    aoT = sb("aoT", [128, KD, STMAX], BF16, ls)
                    hyT = sb("hyT", [128, KL, STMAX], BF16, ls)
                    mT = sb("mT", [128, KD, STMAX], BF16, ls)
                    wa = [sb(f"wba{i}", [128, KD, 256], BF16, ls) for i in range(2)]
                    wl = [sb(f"wbl{i}", [128, KL, 256], BF16, ls) for i in range(2)]
                    gA = [sb(f"gA{i}", [128, 512], BF16, ls) for i in range(2)]
                    gL = [sb(f"gL{i}", [128, 512], BF16, ls) for i in range(2)]
                    m1 = [sb(f"m1_{i}", [128, 512], F32, ls) for i in range(2)]
                    m2 = [sb(f"m2_{i}", [128, 512], F32, ls) for i in range(2)]
                    hr = [sb(f"hrm{i}", [128, 512], F32, ls) for i in range(2)]
                    hn = [sb(f"hnm{i}", [128, 512], F32, ls) for i in range(2)]
                    pa = [ps(f"pba{i}", [128, 512], F32, ls) for i in range(2)]
                    pl = [ps(f"pbl{i}", [128, 512], F32, ls) for i in range(2)]
                    po = [ps(f"pwo{i}", [128, 512], F32, ls) for i in range(2)]
                    R_in = S.R("m4in")
                    R_mT = S.R("mT")
                    R_w = [S.R("wbr", i) for i in range(2)]
                    R_g = [S.R("gAL", i) for i in range(2)]
                    R_m = [S.R("m12", i) for i in range(2)]
                    R_hr = [S.R("hrm", i) for i in range(2)]
                    R_hn = [S.R("hnm", i) for i in range(2)]
                    R_pa = [S.R("pba", i) for i in range(2)]
                    R_pl = [S.R("pbl", i) for i in range(2)]
                    R_po = [S.R("pwo", i) for i in range(2)]
                    for c0 in range(0, KD, 4):
                        S.dma("sp", lambda e, c0=c0: e.dma_start(
                            out=aoT[:, c0:c0 + 4, 0:ntok],
                            in_=aoT_d[c0 * 128:(c0 + 4) * 128, t0:t0 + ntok].rearrange("(c p) t -> p c t", p=128)),
                            "ld_m4", writes=[R_in], accumulate_writers=(c0 > 0))
                    for c0 in range(0, KL, 4):
                        S.dma("sp", lambda e, c0=c0: e.dma_start(
                            out=hyT[:, c0:c0 + 4, 0:ntok],
                            in_=hyT_d[c0 * 128:(c0 + 4) * 128, t0:t0 + ntok].rearrange("(c p) t -> p c t", p=128)),
                            "ld_m4", writes=[R_in], accumulate_writers=True)

                    def ldw(dg):
                        s = dg % 2
                        wtile(wa[s], "w_ba", l, dg, f"ld_w{s}", R_w[s])
                        wtile(wl[s], "w_bl", l, dg, f"ld_w{s}", R_w[s], acc=True)

                    ldw(0)
                    ldw(1)
                    it = 0
                    for dg in range(KD // 2):
                        s = dg % 2
                        off = 0
                        for (ts, tn) in stoks:
                            for dd in range(2):
                                d = 2 * dg + dd
                                p = it % 2
                                it += 1
                                S.dma("sp", lambda e, p=p, d=d, ts=ts, tn=tn: e.dma_start(
                                    out=gA[p][:, :tn], in_=gT_d[d * 128:(d + 1) * 128, ts:ts + tn]),
                                    f"ld_g{p}", writes=[R_g[p]])
                                S.dma("sp", lambda e, p=p, d=d, ts=ts, tn=tn: e.dma_start(
                                    out=gL[p][:, :tn], in_=gT_d[D + d * 128:D + (d + 1) * 128, ts:ts + tn]),
                                    f"ld_g{p}", writes=[R_g[p]], accumulate_writers=True)
                                for kc in range(KD):
                                    S.op("pe", lambda e, p=p, s=s, kc=kc, dd=dd, off=off, tn=tn: e.matmul(
                                        pa[p][:, :tn], wa[s][:, kc, dd * 128:(dd + 1) * 128], aoT[:, kc, off:off + tn],
                                        start=(kc == 0), stop=(kc == KD - 1)), reads=[R_w[s], R_in], writes=[R_pa[p]])
                                for kc in range(KL):
                                    S.op("pe", lambda e, p=p, s=s, kc=kc, dd=dd, off=off, tn=tn: e.matmul(
                                        pl[p][:, :tn], wl[s][:, kc, dd * 128:(dd + 1) * 128], hyT[:, kc, off:off + tn],
                                        start=(kc == 0), stop=(kc == KL - 1)), reads=[R_w[s], R_in], writes=[R_pl[p]])
                                S.op("dve", lambda e, p=p, tn=tn: e.tensor_tensor(
                                    out=m1[p][:, :tn], in0=pa[p][:, :tn], in1=gA[p][:, :tn], op=ALU.mult),
                                    reads=[R_pa[p], R_g[p]], writes=[R_m[p]])
                                S.op("dve", lambda e, p=p, tn=tn: e.tensor_tensor(
                                    out=m2[p][:, :tn], in0=pl[p][:, :tn], in1=gL[p][:, :tn], op=ALU.mult),
                                    reads=[R_pl[p], R_g[p]], writes=[R_m[p]])
                                S.op("dve", lambda e, p=p, d=d, off=off, tn=tn: e.tensor_tensor(
                                    out=mT[:, d, off:off + tn], in0=m1[p][:, :tn], in1=m2[p][:, :tn], op=ALU.add),
                                    reads=[R_m[p]], writes=[R_mT])
                            off += tn
                        if dg + 2 < KD // 2:
                            ldw(dg + 2)
                    def ldo(dg):
                        s = dg % 2
                        wtile(wa[s], "w_o", l, dg, f"ld_w{s}", R_w[s])

                    ldo(0)
                    ldo(1)
                    it = 0
                    for dg in range(KD // 2):
                        s = dg % 2
                        off = 0
                        for (ts, tn) in stoks:
                            for dd in range(2):
                                d = 2 * dg + dd
                                p = it % 2
                                it += 1
                                S.dma("sp", lambda e, p=p, d=d, ts=ts, tn=tn: e.dma_start(
                                    out=hr[p][:, :tn], in_=hT[d * 128:(d + 1) * 128, ts:ts + tn]),
                                    f"ld_hr{p}", reads=[Rh(d, ts)], writes=[R_hr[p]])
                                for kc in range(KD):
                                    S.op("pe", lambda e, p=p, s=s, kc=kc, dd=dd, off=off, tn=tn: e.matmul(
                                        po[p][:, :tn], wa[s][:, kc, dd * 128:(dd + 1) * 128], mT[:, kc, off:off + tn],
                                        start=(kc == 0), stop=(kc == KD - 1)), reads=[R_w[s], R_mT], writes=[R_po[p]])
                                S.op("dve", lambda e, p=p, tn=tn: e.tensor_tensor(
                                    out=hn[p][:, :tn], in0=po[p][:, :tn], in1=hr[p][:, :tn], op=ALU.add),
                                    reads=[R_po[p], R_hr[p]], writes=[R_hn[p]])
                                S.dma("sp", lambda e, p=p, d=d, ts=ts, tn=tn: e.dma_start(
                                    out=hT[d * 128:(d + 1) * 128, ts:ts + tn], in_=hn[p][:, :tn]),
                                    f"st_hn{p}", reads=[R_hn[p]], writes=[Rh(d, ts)])
                            off += tn
                        if dg + 2 < KD // 2:
                            ldo(dg + 2)
                    S.barrier()

        precast_all()
        import os
        PH = os.environ.get("K_PHASES", "f1,proj,attn,lru,out,f2").split(",")
        wrote_h = False
        for l in range(DEPTH):
            if "f1" in PH:
                ffn(l, norm_ffn1, "ffn1", "f1", hin=(h0T if l == 0 else hT))
                wrote_h = True
            if "proj" in PH:
                mixer_proj(l)
            if "attn" in PH:
                attention(l)
            if "lru" in PH:
                lru(l)
            if "out" in PH:
                mixer_out(l)
            if "f2" in PH:
                ffn(l, norm_ffn2, "ffn2", "f2")
        for si, stoks in enumerate(ST):
            with ExitStack() as ls:
                rms_to_uT(ls, stoks, final_norm[0, :], None, None, "fin", out_dram=outT,
                          hin=(hT if wrote_h else h0T))
        S.barrier()
        S.emit()
    return nc


def const_tables(T):
    NB = (T + 127) // 128
    inv_freq = (THETA ** (-np.arange(0, ROT, 2, dtype=np.float32) / ROT)).astype(np.float32)
    ang = np.arange(NB * 128, dtype=np.float32)[:, None] * inv_freq[None, :]
    cos = np.tile(np.cos(ang).astype(np.float32), (1, 4))
    sin = np.tile(np.sin(ang).astype(np.float32), (1, 4))
    mask = np.triu(np.ones((128, 128), np.float32))
    ident = np.eye(128, dtype=np.float32)
    return cos, sin, mask, ident


_NC_CACHE = {}


def run_layers(h0T_list, inputs, T, DEPTH, debug=False):
    key = (T, DEPTH, debug)
    if key not in _NC_CACHE:
        _NC_CACHE[key] = build(T, DEPTH, debug)
    nc = _NC_CACHE[key]
    cos, sin, mask, ident = const_tables(T)
    import ml_dtypes
    lam = np.array([[lam_init_of(l), 1.0 - lam_init_of(l)] for l in range(DEPTH)], np.float32)
    shared = {k: np.ascontiguousarray(v) for k, v in inputs.items() if k not in ("x", "meta_tokens")}
    shared["final_norm"] = np.ascontiguousarray(inputs["final_norm"]).reshape(1, D)
    shared.update(c_cos=cos, c_sin=sin, c_mask=mask.astype(ml_dtypes.bfloat16),
                  c_ident=ident.astype(ml_dtypes.bfloat16), c_laminit=lam)
    in_maps = []
    for h0T in h0T_list:
        m = dict(shared)
        m["h0T"] = h0T
        in_maps.append(m)
    res = run_bass_kernel_spmd(nc, in_maps, core_ids=list(range(len(in_maps))))
    return res


def kernel(**inputs):
    x = np.asarray(inputs["x"])
    meta = np.asarray(inputs["meta_tokens"])
    B, SEQ, _ = x.shape
    T = SEQ + NMETA
    DEPTH = np.asarray(inputs["w_in"]).shape[0]
    h0T_list = [np.ascontiguousarray(np.concatenate([meta, x[b]], axis=0).T) for b in range(B)]
    res = run_layers(h0T_list, {k: np.asarray(v) for k, v in inputs.items()}, T, DEPTH)
    out = np.stack([np.ascontiguousarray(r["outT"].T[NMETA:]) for r in res.results], axis=0)
    return out.astype(np.float32)
```

```python
import math
from contextlib import ExitStack

import numpy as np
import concourse.bass as bass
import concourse.mybir as mybir
from concourse.bass_utils import run_bass_kernel_spmd

F32 = mybir.dt.float32
BF16 = mybir.dt.bfloat16
AF = mybir.ActivationFunctionType
ALU = mybir.AluOpType

D = 2048
KD = 16
FF = 5632
KF = 44
NH = 8
HD = 128
VD = 256
ROT = 32
LW = 2560
KL = 20
NIN = 15360
NMETA = 16
EPS = 1e-6
THETA = 500000.0

ENGS = ("pe", "act", "dve", "pool", "sp")


class Res:
    __slots__ = ("w", "r")

    def __init__(self):
        self.w = []
        self.r = []


class Sched:
    def __init__(self, nc):
        self.nc = nc
        self.ops = {e: [] for e in ENGS}
        self.cnt = {}
        self.handles = {}
        self.waited = {e: {} for e in ENGS}
        self.res = {}
        self.stack = ExitStack()
        for e in ("pe", "act", "dve", "pool"):
            self.new_sem("c_" + e)

    def new_sem(self, name):
        if name not in self.handles:
            self.handles[name] = self.stack.enter_context(self.nc.semaphore(name))
            self.cnt[name] = 0
        return name

    def R(self, *key):
        r = self.res.get(key)
        if r is None:
            r = self.res[key] = Res()
        return r

    def _deps(self, eng, reads, writes):
        need = {}
        for r in reads:
            for (s, v) in r.w:
                need[s] = max(need.get(s, 0), v)
        for w in writes:
            for (s, v) in w.w:
                need[s] = max(need.get(s, 0), v)
            for (s, v) in w.r:
                need[s] = max(need.get(s, 0), v)
        waits = []
        for s, v in need.items():
            if not s.startswith("c_"):
                v = self.cnt[s]
            if eng == "pe" and s == "c_pe":
                continue
            if self.waited[eng].get(s, 0) >= v:
                continue
            self.waited[eng][s] = v
            waits.append((s, v))
        return waits

    def _commit(self, tok, reads, writes):
        for r in reads:
            r.r.append(tok)
            if len(r.r) > 24:
                m = {}
                for (s, v) in r.r:
                    m[s] = max(m.get(s, 0), v)
                r.r = list(m.items())
        for w in writes:
            w.w = [tok]
            w.r = []

    def op(self, eng, fn, reads=(), writes=()):
        waits = self._deps(eng, reads, writes)
        s = "c_" + eng
        self.cnt[s] += 1
        tok = (s, self.cnt[s])
        self.ops[eng].append((waits, fn, s, 1))
        self._commit(tok, reads, writes)

    def dma(self, q, fn, sem, reads=(), writes=(), accumulate_writers=False):
        waits = self._deps(q, reads, writes)
        self.new_sem(sem)
        self.cnt[sem] += 16
        tok = (sem, self.cnt[sem])
        self.ops[q].append((waits, fn, sem, 16))
        if accumulate_writers:
            for r in reads:
                r.r.append(tok)
            for w in writes:
                w.w.append(tok)
        else:
            self._commit(tok, reads, writes)

    def barrier(self):
        for e in ENGS:
            waits = []
            for s, v in self.cnt.items():
                if v == 0 or (e == "pe" and s == "c_pe"):
                    continue
                if s == "c_" + e:
                    continue
                if self.waited[e].get(s, 0) >= v:
                    continue
                self.waited[e][s] = v
                waits.append((s, v))
            if waits:
                self.ops[e].append((waits, None, None, 0))

    def emit(self):
        nc = self.nc
        H = self.handles

        def run(eng, lst):
            for (waits, fn, s, inc) in lst:
                for (ws, wv) in waits:
                    eng.wait_ge(H[ws], wv)
                if fn is not None:
                    ins = fn(eng)
                    ins.then_inc(H[s], inc)

        with nc.Block() as block:
            @block.tensor
            def _(e):
                run(e, self.ops["pe"])

            @block.scalar
            def _(e):
                run(e, self.ops["act"])

            @block.vector
            def _(e):
                run(e, self.ops["dve"])

            @block.gpsimd
            def _(e):
                run(e, self.ops["pool"])

            @block.sync
            def _(e):
                run(e, self.ops["sp"])


def tiles_of(T, step):
    out = []
    s = 0
    while s < T:
        out.append((s, min(step, T - s)))
        s += step
    return out


def super_tiles(T):
    tt = tiles_of(T, 512)
    groups = []
    i = 0
    while i < len(tt):
        g = tt[i:i + 2]
        i += 2
        groups.append(g)
    if len(groups) > 1 and sum(n for _, n in groups[-1]) < 128:
        last = groups.pop()
        groups[-1] = groups[-1] + last
    return groups


def lam_init_of(l):
    return 0.8 - 0.6 * math.exp(-0.3 * l)


GATE_PAIRS = sorted({(ci, co) for b in range(4) for ci in (b, b + 1) for co in (b, b + 1)})
GATE_NBRS = {co: [ci for (ci, c2) in GATE_PAIRS if c2 == co] for co in range(5)}


def build(T, DEPTH, debug=False):
    nc = bass.Bass("TRN2", target_bir_lowering=False)
    S = Sched(nc)
    TT = tiles_of(T, 512)
    TB = tiles_of(T, 128)
    NB = len(TB)
    ST = super_tiles(T)
    STMAX = max(sum(n for _, n in g) for g in ST)

    def din(name, shape, dt=F32):
        return nc.dram_tensor(name, list(shape), dt, kind="ExternalInput").ap()

    dbg_kind = "ExternalOutput" if debug else "Internal"

    def dscr(name, shape, dt):
        return nc.dram_tensor(name, list(shape), dt, kind=dbg_kind).ap()

    h0T = din("h0T", [D, T])
    norm_ffn1 = din("norm_ffn1", [DEPTH, D])
    ffn1_wg = din("ffn1_w_gate", [DEPTH, D, FF])
    ffn1_wu = din("ffn1_w_up", [DEPTH, D, FF])
    ffn1_wd = din("ffn1_w_down", [DEPTH, FF, D])
    norm_mix = din("norm_mix", [DEPTH, D])
    w_in = din("w_in", [DEPTH, D, NIN])
    b_gate = din("b_gate", [DEPTH, 2 * D])
    lam_q1 = din("lambda_q1", [DEPTH, HD])
    lam_k1 = din("lambda_k1", [DEPTH, HD])
    lam_q2 = din("lambda_q2", [DEPTH, HD])
    lam_k2 = din("lambda_k2", [DEPTH, HD])
    attn_subln = din("attn_subln", [DEPTH, VD])
    conv_w = din("conv_w", [DEPTH, 4, LW])
    conv_b = din("conv_b", [DEPTH, LW])
    gate_x_w = din("gate_x_w", [DEPTH, 16, 160, 160])
    gate_x_b = din("gate_x_b", [DEPTH, LW])
    gate_a_w = din("gate_a_w", [DEPTH, 16, 160, 160])
    gate_a_b = din("gate_a_b", [DEPTH, LW])
    lru_a = din("lru_a_param", [DEPTH, LW])
    w_ba = din("w_branch_attn", [DEPTH, D, D])
    w_bl = din("w_branch_lru", [DEPTH, LW, D])
    w_o = din("w_out", [DEPTH, D, D])
    norm_ffn2 = din("norm_ffn2", [DEPTH, D])
    ffn2_wg = din("ffn2_w_gate", [DEPTH, D, FF])
    ffn2_wu = din("ffn2_w_up", [DEPTH, D, FF])
    ffn2_wd = din("ffn2_w_down", [DEPTH, FF, D])
    final_norm = din("final_norm", [1, D])
    c_cos = din("c_cos", [NB * 128, 64])
    c_sin = din("c_sin", [NB * 128, 64])
    c_mask = din("c_mask", [128, 128], BF16)
    c_ident = din("c_ident", [128, 128], BF16)
    c_laminit = din("c_laminit", [DEPTH, 2])

    outT = nc.dram_tensor("outT", [D, T], F32, kind="ExternalOutput").ap()

    hT = dscr("hT", [D, T], F32)
    qT_d = dscr("qT_d", [16, 128, T], BF16)
    kT_d = dscr("kT_d", [16, 128, T], BF16)
    v_d = dscr("v_d", [NB * 128, D], BF16)
    xrT_d = dscr("xrT_d", [LW, T], F32)
    yrT_d = dscr("yrT_d", [LW, T], F32)
    gT_d = dscr("gT_d", [2 * D, T], BF16)
    aoT_d = dscr("aoT_d", [D, T], BF16)
    hyT_d = dscr("hyT_d", [LW, T], BF16)

    WSP = {"ffn1_wg": (ffn1_wg, D, FF, 256), "ffn1_wu": (ffn1_wu, D, FF, 256), "ffn1_wd": (ffn1_wd, FF, D, 256),
           "w_in": (w_in, D, NIN, 512), "w_ba": (w_ba, D, D, 256), "w_bl": (w_bl, LW, D, 256),
           "w_o": (w_o, D, D, 256),
           "ffn2_wg": (ffn2_wg, D, FF, 256), "ffn2_wu": (ffn2_wu, D, FF, 256), "ffn2_wd": (ffn2_wd, FF, D, 256)}
    wbuf = {}
    for l_ in range(DEPTH):
        for nm, (_, K_, N_, tc_) in WSP.items():
            wbuf[(nm, l_)] = nc.dram_tensor(f"wb_{nm}_{l_}", [N_ // tc_, 128, (K_ // 128) * tc_], BF16).ap()

    with S.stack:
        es = S.stack

        uid = [0]

        def sb(name, shape, dt, st=None):
            uid[0] += 1
            return (st or es).enter_context(nc.sbuf_tensor(f"{name}_{uid[0]}", list(shape), dt))

        def ps(name, shape, dt, st):
            uid[0] += 1
            return st.enter_context(nc.psum_tensor(f"{name}_{uid[0]}", list(shape), dt))

        ones_bf = sb("ones_bf", [128, 128], BF16)
        ident = sb("ident", [128, 128], BF16)
        maskt = sb("maskt", [128, 128], BF16)
        eps_t = sb("eps_t", [128, 1], F32)
        one_t = sb("one_t", [128, 1], F32)
        cos_t = sb("cos_t", [128, NB, 64], F32)
        sin_t = sb("sin_t", [128, NB, 64], F32)
        R_const = S.R("const")
        S.op("dve", lambda e: e.memset(ones_bf[:], 1.0), writes=[R_const])
        S.op("dve", lambda e: e.memset(eps_t[:], EPS), writes=[R_const])
        S.op("dve", lambda e: e.memset(one_t[:], 1.0), writes=[R_const])
        S.dma("sp", lambda e: e.dma_start(out=ident[:], in_=c_ident[:, :]), "ld_c", writes=[R_const],
              accumulate_writers=True)
        S.dma("sp", lambda e: e.dma_start(out=maskt[:], in_=c_mask[:, :]), "ld_c", writes=[R_const],
              accumulate_writers=True)
        S.dma("sp", lambda e: e.dma_start(out=cos_t[:], in_=c_cos.rearrange("(b p) c -> p b c", p=128)),
              "ld_c", writes=[R_const], accumulate_writers=True)
        S.dma("sp", lambda e: e.dma_start(out=sin_t[:], in_=c_sin.rearrange("(b p) c -> p b c", p=128)),
              "ld_c", writes=[R_const], accumulate_writers=True)

        def Rh(kc, ts):
            return S.R("h", kc, ts)

        def Rhr(hin, kc, ts):
            return [Rh(kc, ts)] if hin is hT else []

        def load_cols(dst, src_row, n, sem, res, q="sp"):
            S.dma(q, lambda e: e.dma_start(out=dst, in_=src_row.rearrange("(c p) -> p c", p=128),
                                           allow_slow_non_contiguous=True), sem, writes=[res])

        def wload(dst, W2d, kcn, col0, ncols, sem, res):
            Wv = W2d.rearrange("(kc p) n -> p kc n", p=128)
            first = True
            for k0 in range(0, kcn, 4):
                k1 = min(kcn, k0 + 4)
                S.dma("pool", lambda e, k0=k0, k1=k1: e.dma_start(
                    out=dst[:, k0:k1, 0:ncols], in_=Wv[:, k0:k1, col0:col0 + ncols]), sem,
                    writes=[res], accumulate_writers=not first)
                first = False

        def precast_all():
            for l_ in range(DEPTH):
                for nm, (W3, K_, N_, tc_) in WSP.items():
                    kcn = K_ // 128
                    Wv = W3[l_].rearrange("(kc p) n -> p kc n", p=128)
                    for t in range(N_ // tc_):
                        dv = wbuf[(nm, l_)][t].rearrange("p (kc c) -> p kc c", c=tc_)
                        for k0 in range(0, kcn, 4):
                            k1 = min(kcn, k0 + 4)
                            S.dma("pool", lambda e, dv=dv, Wv=Wv, k0=k0, k1=k1, t=t, tc_=tc_: e.dma_start(
                                out=dv[:, k0:k1, :], in_=Wv[:, k0:k1, t * tc_:(t + 1) * tc_]),
                                f"pc_{nm}_{l_}", writes=[S.R("wb", nm, l_)], accumulate_writers=True)

        def wtile(dst, nm, l_, t, sem, res, acc=False):
            tc_ = WSP[nm][3]
            S.dma("sp", lambda e: e.dma_start(
                out=dst[:, :, :], in_=wbuf[(nm, l_)][t].rearrange("p (kc c) -> p kc c", c=tc_)),
                sem, reads=[S.R("wb", nm, l_)], writes=[res], accumulate_writers=acc)

        def rms_to_uT(st, stoks, gain_row, uT, R_uT, tag, out_dram=None, hin=None):
            if hin is None:
                hin = hT
            with ExitStack() as ls:
                gcol = sb(f"gcol_{tag}", [128, KD], F32, ls)
                hb = [sb(f"hb{i}_{tag}", [128, 512], F32, ls) for i in range(3)]
                sq = [sb(f"sq{i}_{tag}", [128, 512], BF16, ls) for i in range(2)]
                rt = sb(f"rt_{tag}", [128, 512], F32, ls)
                rstd = sb(f"rstd_{tag}", [128, 512], F32, ls)
                ob = [sb(f"ob{i}_{tag}", [128, 512], F32, ls) for i in range(2)] if out_dram is not None else None
                pss = ps(f"pss_{tag}", [128, 512], F32, ls)
                R_g = S.R("gcol", tag)
                load_cols(gcol[:, :], gain_row, KD, "ld_small", R_g)
                R_hb = [S.R("hb", tag, i) for i in range(3)]
                R_sq = [S.R("sq", tag, i) for i in range(2)]
                R_pss = S.R("pss", tag)
                R_rt = S.R("rt", tag)
                R_rstd = S.R("rstd", tag)
                R_ob = [S.R("ob", tag, i) for i in range(2)]
                n_hb = 0
                off = 0
                for (ts, tn) in stoks:
                    for kc in range(KD):
                        b = n_hb % 3
                        n_hb += 1
                        S.dma("sp", lambda e, b=b, kc=kc, ts=ts, tn=tn: e.dma_start(
                            out=hb[b][:, :tn], in_=hin[kc * 128:(kc + 1) * 128, ts:ts + tn]),
                            f"ld_hb{b}", reads=Rhr(hin, kc, ts), writes=[R_hb[b]])
                        s2 = kc % 2
                        S.op("act", lambda e, b=b, s2=s2, tn=tn: e.activation(
                            out=sq[s2][:, :tn], in_=hb[b][:, :tn], func=AF.Square),
                            reads=[R_hb[b]], writes=[R_sq[s2]])
                        S.op("pe", lambda e, s2=s2, tn=tn, kc=kc: e.matmul(
                            pss[:, :tn], ones_bf[:, :], sq[s2][:, :tn], start=(kc == 0), stop=(kc == KD - 1)),
                            reads=[R_sq[s2], R_const], writes=[R_pss])
                    S.op("act", lambda e, tn=tn: e.activation(
                        out=rt[:, :tn], in_=pss[:, :tn], func=AF.Sqrt, bias=eps_t[:, :], scale=1.0 / D),
                        reads=[R_pss, R_const], writes=[R_rt])
                    S.op("dve", lambda e, tn=tn: e.reciprocal(out=rstd[:, :tn], in_=rt[:, :tn]),
                         reads=[R_rt], writes=[R_rstd])
                    for kc in range(KD):
                        b = n_hb % 3
                        n_hb += 1
                        S.dma("sp", lambda e, b=b, kc=kc, ts=ts, tn=tn: e.dma_start(
                            out=hb[b][:, :tn], in_=hin[kc * 128:(kc + 1) * 128, ts:ts + tn]),
                            f"ld_hb{b}", reads=Rhr(hin, kc, ts), writes=[R_hb[b]])
                        if out_dram is None:
                            S.op("dve", lambda e, b=b, kc=kc, tn=tn, off=off: e.scalar_tensor_tensor(
                                out=uT[:, kc, off:off + tn], in0=hb[b][:, :tn], scalar=gcol[:, kc:kc + 1],
                                in1=rstd[:, :tn], op0=ALU.mult, op1=ALU.mult),
                                reads=[R_hb[b], R_g, R_rstd], writes=[R_uT])
                        else:
                            o2 = kc % 2
                            S.op("dve", lambda e, b=b, kc=kc, tn=tn, o2=o2: e.scalar_tensor_tensor(
                                out=ob[o2][:, :tn], in0=hb[b][:, :tn], scalar=gcol[:, kc:kc + 1],
                                in1=rstd[:, :tn], op0=ALU.mult, op1=ALU.mult),
                                reads=[R_hb[b], R_g, R_rstd], writes=[R_ob[o2]])
                            S.dma("sp", lambda e, kc=kc, ts=ts, tn=tn, o2=o2: e.dma_start(
                                out=out_dram[kc * 128:(kc + 1) * 128, ts:ts + tn], in_=ob[o2][:, :tn]),
                                f"st_ob{o2}", reads=[R_ob[o2]], writes=[S.R("out", kc, ts)])
                    off += tn
                S.barrier()

        def ffn(l, gain, pfx, tag, hin=None):
            if hin is None:
                hin = hT
            for si, stoks in enumerate(ST):
                ffn_body(l, gain, pfx, tag, hin, si, stoks)

        def ffn_body(l, gain, pfx, tag, hin, si, stoks):
            if True:
                ntok = sum(n for _, n in stoks)
                with ExitStack() as ls:
                    aT = sb(f"aT_{tag}", [128, KF, STMAX], BF16, ls)
                    R_aT = S.R("aT", tag, si)
                    with ExitStack() as l1:
                        uT = sb(f"uT_{tag}", [128, KD, STMAX], BF16, l1)
                        R_uT = S.R("uT", tag, si)
                        rms_to_uT(l1, stoks, gain[l, :], uT, R_uT, f"{tag}n", hin=hin)
                        wgt = [sb(f"wg{i}_{tag}", [128, KD, 256], BF16, l1) for i in range(2)]
                        wut = [sb(f"wu{i}_{tag}", [128, KD, 256], BF16, l1) for i in range(2)]
                        sg = [sb(f"sg{i}_{tag}", [128, 512], BF16, l1) for i in range(2)]
                        psg = [ps(f"psg{i}_{tag}", [128, 512], F32, l1) for i in range(2)]
                        psu = [ps(f"psu{i}_{tag}", [128, 512], F32, l1) for i in range(2)]
                        R_w = [S.R("wgu", tag, i) for i in range(2)]
                        R_sg = [S.R("sg", tag, i) for i in range(2)]
                        R_pg = [S.R("psg", tag, i) for i in range(2)]
                        R_pu = [S.R("psu", tag, i) for i in range(2)]

                        def ldw(m2):
                            s = m2 % 2
                            wtile(wgt[s], pfx + "_wg", l, m2, f"ld_w{s}", R_w[s])
                            wtile(wut[s], pfx + "_wu", l, m2, f"ld_w{s}", R_w[s], acc=True)

                        ldw(0)
                        ldw(1)
                        it = 0
                        for m2 in range(KF // 2):
                            s = m2 % 2
                            off = 0
                            for (ts, tn) in stoks:
                                for mm in range(2):
                                    m = 2 * m2 + mm
                                    p = it % 2
                                    it += 1
                                    for kc in range(KD):
                                        S.op("pe", lambda e, p=p, s=s, kc=kc, mm=mm, off=off, tn=tn: e.matmul(
                                            psg[p][:, :tn], wgt[s][:, kc, mm * 128:(mm + 1) * 128],
                                            uT[:, kc, off:off + tn], start=(kc == 0), stop=(kc == KD - 1)),
                                            reads=[R_w[s], R_uT], writes=[R_pg[p]])
                                    for kc in range(KD):
                                        S.op("pe", lambda e, p=p, s=s, kc=kc, mm=mm, off=off, tn=tn: e.matmul(
                                            psu[p][:, :tn], wut[s][:, kc, mm * 128:(mm + 1) * 128],
                                            uT[:, kc, off:off + tn], start=(kc == 0), stop=(kc == KD - 1)),
                                            reads=[R_w[s], R_uT], writes=[R_pu[p]])
                                    S.op("act", lambda e, p=p, tn=tn: e.activation(
                                        out=sg[p][:, :tn], in_=psg[p][:, :tn], func=AF.Silu),
                                        reads=[R_pg[p]], writes=[R_sg[p]])
                                    S.op("dve", lambda e, p=p, m=m, off=off, tn=tn: e.tensor_tensor(
                                        out=aT[:, m, off:off + tn], in0=psu[p][:, :tn], in1=sg[p][:, :tn],
                                        op=ALU.mult), reads=[R_pu[p], R_sg[p]], writes=[R_aT])
                                off += tn
                            if m2 + 2 < KF // 2:
                                ldw(m2 + 2)
                        S.barrier()
                    with ExitStack() as l2:
                        wdt = [sb(f"wd{i}_{tag}", [128, KF, 256], BF16, l2) for i in range(2)]
                        hr = [sb(f"hr{i}_{tag}", [128, 512], F32, l2) for i in range(2)]
                        hn = [sb(f"hn{i}_{tag}", [128, 512], F32, l2) for i in range(2)]
                        pso = [ps(f"pso{i}_{tag}", [128, 512], F32, l2) for i in range(2)]
                        R_wd = [S.R("wd", tag, i) for i in range(2)]
                        R_hr = [S.R("hr", tag, i) for i in range(2)]
                        R_hn = [S.R("hn", tag, i) for i in range(2)]
                        R_po = [S.R("pso", tag, i) for i in range(2)]

                        def ldd(d2):
                            s = d2 % 2
                            wtile(wdt[s], pfx + "_wd", l, d2, f"ld_w{s}", R_wd[s])

                        ldd(0)
                        ldd(1)
                        it = 0
                        for d2 in range(KD // 2):
                            s = d2 % 2
                            off = 0
                            for (ts, tn) in stoks:
                                for dd in range(2):
                                    d = 2 * d2 + dd
                                    p = it % 2
                                    it += 1
                                    S.dma("sp", lambda e, p=p, d=d, ts=ts, tn=tn: e.dma_start(
                                        out=hr[p][:, :tn], in_=hin[d * 128:(d + 1) * 128, ts:ts + tn]),
                                        f"ld_hr{p}", reads=Rhr(hin, d, ts), writes=[R_hr[p]])
                                    for m in range(KF):
                                        S.op("pe", lambda e, p=p, s=s, m=m, dd=dd, off=off, tn=tn: e.matmul(
                                            pso[p][:, :tn], wdt[s][:, m, dd * 128:(dd + 1) * 128],
                                            aT[:, m, off:off + tn], start=(m == 0), stop=(m == KF - 1)),
                                            reads=[R_wd[s], R_aT], writes=[R_po[p]])
                                    S.op("dve", lambda e, p=p, tn=tn: e.scalar_tensor_tensor(
                                        out=hn[p][:, :tn], in0=pso[p][:, :tn], scalar=0.5, in1=hr[p][:, :tn],
                                        op0=ALU.mult, op1=ALU.add), reads=[R_po[p], R_hr[p]], writes=[R_hn[p]])
                                    S.dma("sp", lambda e, p=p, d=d, ts=ts, tn=tn: e.dma_start(
                                        out=hT[d * 128:(d + 1) * 128, ts:ts + tn], in_=hn[p][:, :tn]),
                                        f"st_hn{p}", reads=[R_hn[p]], writes=[Rh(d, ts)])
                                off += tn
                            if d2 + 2 < KD // 2:
                                ldd(d2 + 2)
                        S.barrier()

        def mixer_proj(l):
            for si, stoks in enumerate(ST):
                mixer_proj_body(l, si, stoks)

        def mixer_proj_body(l, si, stoks):
            if True:
                t0 = stoks[0][0]
                ntok = sum(n for _, n in stoks)
                blocks = tiles_of(ntok, 128)
                with ExitStack() as ls:
                    uT = sb("uT_m", [128, KD, STMAX], BF16, ls)
                    R_uT = S.R("uT_m", si)
                    rms_to_uT(ls, stoks, norm_mix[l, :], uT, R_uT, "mn")
                    wt = [sb(f"win{i}", [128, KD, 512], BF16, ls) for i in range(2)]
                    R_w = [S.R("win", i) for i in range(2)]
                    bg = sb("bgcol", [128, 32], F32, ls)
                    R_bg = S.R("bgcol")
                    load_cols(bg[:, :], b_gate[l, :], 32, "ld_small", R_bg)
                    qk_sb = [sb(f"qk_sb{i}", [128, 512], BF16, ls) for i in range(2)]
                    t1 = [sb(f"rt1_{i}", [128, 4, 16], F32, ls) for i in range(2)]
                    t2 = [sb(f"rt2_{i}", [128, 4, 16], F32, ls) for i in range(2)]
                    x12 = [sb(f"x12_{i}", [128, 4, 32], F32, ls) for i in range(2)]
                    stage = sb("qk_stage", [128, 4, STMAX], BF16, ls)
                    vst = [sb(f"vst{i}", [128, 512], BF16, ls) for i in range(2)]
                    fst = [sb(f"fst{i}", [128, 512], F32, ls) for i in range(2)]
                    gst = [sb(f"gst{i}", [128, 512], BF16, ls) for i in range(2)]
                    pm = [ps(f"pm{i}", [128, 512], F32, ls) for i in range(3)]
                    pt = [ps(f"ptr{i}", [128, 512], BF16, ls) for i in range(2)]
                    R_qk = [S.R("qk_sb", i) for i in range(2)]
                    R_t = [S.R("ropet", i) for i in range(2)]
                    R_stage = S.R("qk_stage")
                    R_vst = [S.R("vst", i) for i in range(2)]
                    R_fst = [S.R("fst", i) for i in range(2)]
                    R_gst = [S.R("gst", i) for i in range(2)]
                    R_pm = [S.R("pm", i) for i in range(3)]
                    R_pt = [S.R("ptr", i) for i in range(2)]
                    NG = NIN // 512

                    def ldw(g):
                        wload(wt[g % 2], w_in[l], KD, g * 512, 512, f"ld_w{g % 2}", R_w[g % 2])

                    import os
                    GL = [int(x) for x in os.environ.get("K_GL", ",".join(str(i) for i in range(NG))).split(",")]
                    norope = bool(os.environ.get("K_NOROPE"))
                    notr = bool(os.environ.get("K_NOTR"))
                    for gi_ in range(min(2, len(GL))):
                        wtile(wt[gi_ % 2], "w_in", l, GL[gi_], f"ld_w{gi_ % 2}", R_w[gi_ % 2])
                    it = 0
                    itq = 0
                    pending_tr = [None]
                    for gi_, g in enumerate(GL):
                        s = gi_ % 2
                        if g < 12:
                            for (bs, bn) in blocks:
                                p = it % 3
                                it += 1
                                gb = (t0 + bs) // 128
                                for kc in range(KD):
                                    S.op("pe", lambda e, p=p, s=s, kc=kc, bs=bs, bn=bn: e.matmul(
                                        pm[p][:bn, :], uT[:, kc, bs:bs + bn], wt[s][:, kc, :],
                                        start=(kc == 0), stop=(kc == KD - 1)),
                                        reads=[R_w[s], R_uT], writes=[R_pm[p]])
                                if g < 8:
                                    q2 = itq % 2
                                    itq += 1
                                    S.op("act", lambda e, p=p, q2=q2, bn=bn: e.activation(
                                        out=qk_sb[q2][:bn, :], in_=pm[p][:bn, :], func=AF.Copy),
                                        reads=[R_pm[p]], writes=[R_qk[q2]])
                                    pv = pm[p][:, :].rearrange("p (h d) -> p h d", h=4)
                                    qv = qk_sb[q2][:, :].rearrange("p (h d) -> p h d", h=4)
                                    cv = cos_t[:, gb, :].rearrange("p (h d) -> p h d", h=4)
                                    sv = sin_t[:, gb, :].rearrange("p (h d) -> p h d", h=4)
                                    if not norope:
                                        xr32 = x12[q2]
                                        S.op("act", lambda e, bn=bn, pv=pv, xr32=xr32: e.activation(
                                            out=xr32[:bn], in_=pv[:bn, :, 0:32], func=AF.Copy),
                                            reads=[R_pm[p]], writes=[R_t[q2]])
                                        S.op("dve", lambda e, bn=bn, xr32=xr32, cv=cv, q2=q2: e.tensor_tensor(
                                            out=t1[q2][:bn], in0=xr32[:bn, :, 0:16], in1=cv[:bn], op=ALU.mult),
                                            reads=[R_t[q2], R_const], writes=[R_t[q2]])
                                        S.op("dve", lambda e, bn=bn, xr32=xr32, sv=sv, q2=q2: e.tensor_tensor(
                                            out=t2[q2][:bn], in0=xr32[:bn, :, 16:32], in1=sv[:bn], op=ALU.mult),
                                            reads=[R_t[q2], R_const], writes=[R_t[q2]])
                                        S.op("dve", lambda e, bn=bn, qv=qv, q2=q2: e.tensor_tensor(
                                            out=qv[:bn, :, 0:16], in0=t1[q2][:bn], in1=t2[q2][:bn], op=ALU.subtract),
                                            reads=[R_t[q2]], writes=[R_qk[q2]])
                                        S.op("dve", lambda e, bn=bn, xr32=xr32, cv=cv, q2=q2: e.tensor_tensor(
                                            out=t1[q2][:bn], in0=xr32[:bn, :, 16:32], in1=cv[:bn], op=ALU.mult),
                                            reads=[R_t[q2], R_const, R_qk[q2]], writes=[R_t[q2]])
                                        S.op("dve", lambda e, bn=bn, xr32=xr32, sv=sv, q2=q2: e.tensor_tensor(
                                            out=t2[q2][:bn], in0=xr32[:bn, :, 0:16], in1=sv[:bn], op=ALU.mult),
                                            reads=[R_t[q2], R_const], writes=[R_t[q2]])
                                        S.op("dve", lambda e, bn=bn, qv=qv, q2=q2: e.tensor_tensor(
                                            out=qv[:bn, :, 16:32], in0=t1[q2][:bn], in1=t2[q2][:bn], op=ALU.add),
                                            reads=[R_t[q2]], writes=[R_qk[q2]])
                                    def do_tr(q2=q2, bs=bs, bn=bn):
                                        for hc in range(4):
                                            S.op("pe", lambda e, q2=q2, hc=hc, bn=bn: e.transpose(
                                                pt[q2][:, hc * 128:hc * 128 + bn], qk_sb[q2][:bn, hc * 128:(hc + 1) * 128],
                                                ident[:bn, :bn]), reads=[R_qk[q2], R_const], writes=[R_pt[q2]])
                                        ptv = pt[q2][:, :].rearrange("p (h t) -> p h t", h=4)
                                        S.op("act", lambda e, ptv=ptv, bs=bs, bn=bn: e.activation(
                                            out=stage[:, :, bs:bs + bn], in_=ptv[:, :, :bn], func=AF.Copy),
                                            reads=[R_pt[q2]], writes=[R_stage])

                                    if pending_tr[0] is not None:
                                        pending_tr[0]()
                                    pending_tr[0] = do_tr
                                else:
                                    v2 = it % 2
                                    S.op("act", lambda e, p=p, v2=v2, bn=bn: e.activation(
                                        out=vst[v2][:bn, :], in_=pm[p][:bn, :], func=AF.Copy),
                                        reads=[R_pm[p]], writes=[R_vst[v2]])
                                    S.dma("sp", lambda e, v2=v2, bs=bs, bn=bn, g=g: e.dma_start(
                                        out=v_d[t0 + bs:t0 + bs + bn, (g - 8) * 512:(g - 7) * 512], in_=vst[v2][:bn, :]),
                                        f"st_v{v2}", reads=[R_vst[v2]], writes=[S.R("v_d", g, t0 + bs)])
                            if pending_tr[0] is not None:
                                pending_tr[0]()
                                pending_tr[0] = None
                            if g < 8:
                                dst = qT_d if g < 4 else kT_d
                                i0 = (g % 4) * 4
                                S.dma("sp", lambda e, dst=dst, i0=i0: e.dma_start(
                                    out=dst[i0:i0 + 4, :, t0:t0 + ntok].rearrange("h p t -> p h t"),
                                    in_=stage[:, :, 0:ntok]), "st_stage", reads=[R_stage],
                                    writes=[S.R("qkT_d", g, si)])
                        else:
                            off = 0
                            for (ts, tn) in stoks:
                                for mm in range(4):
                                    col = g * 512 + mm * 128
                                    p = it % 3
                                    it += 1
                                    for kc in range(KD):
                                        S.op("pe", lambda e, p=p, s=s, kc=kc, mm=mm, off=off, tn=tn: e.matmul(
                                            pm[p][:, :tn], wt[s][:, kc, mm * 128:(mm + 1) * 128],
                                            uT[:, kc, off:off + tn], start=(kc == 0), stop=(kc == KD - 1)),
                                            reads=[R_w[s], R_uT], writes=[R_pm[p]])
                                    f2 = it % 2
                                    if col < 6144 + 2 * LW:
                                        c0 = col - 6144
                                        dstT = xrT_d if c0 < LW else yrT_d
                                        r0 = c0 % LW
                                        S.op("act", lambda e, p=p, f2=f2, tn=tn: e.activation(
                                            out=fst[f2][:, :tn], in_=pm[p][:, :tn], func=AF.Copy),
                                            reads=[R_pm[p]], writes=[R_fst[f2]])
                                        S.dma("sp", lambda e, f2=f2, dstT=dstT, r0=r0, ts=ts, tn=tn: e.dma_start(
                                            out=dstT[r0:r0 + 128, ts:ts + tn], in_=fst[f2][:, :tn]),
                                            f"st_f{f2}", reads=[R_fst[f2]], writes=[S.R("xy_d", col, ts)])
                                    else:
                                        c0 = col - (6144 + 2 * LW)
                                        S.op("act", lambda e, p=p, f2=f2, tn=tn, c0=c0: e.activation(
                                            out=gst[f2][:, :tn], in_=pm[p][:, :tn], func=AF.Sigmoid,
                                            bias=bg[:, c0 // 128:c0 // 128 + 1]),
                                            reads=[R_pm[p], R_bg], writes=[R_gst[f2]])
                                        S.dma("sp", lambda e, f2=f2, c0=c0, ts=ts, tn=tn: e.dma_start(
                                            out=gT_d[c0:c0 + 128, ts:ts + tn], in_=gst[f2][:, :tn]),
                                            f"st_g{f2}", reads=[R_gst[f2]], writes=[S.R("g_d", col, ts)])
                                off += tn
                        if gi_ + 2 < len(GL):
                            wtile(wt[gi_ % 2], "w_in", l, GL[gi_ + 2], f"ld_w{gi_ % 2}", R_w[gi_ % 2])
                    S.barrier()

        def attention(l):
            with ExitStack() as ls:
                qT2 = [sb(f"qT2_{i}", [128, 2, T], BF16, ls) for i in range(2)]
                kT2 = [sb(f"kT2_{i}", [128, 2, T], BF16, ls) for i in range(2)]
                vx = [sb(f"vx_{i}", [128, NB, VD + 1], BF16, ls) for i in range(2)]
                pT = [sb(f"pT_{i}", [128, 512], BF16, ls) for i in range(3)]
                oc = [sb(f"oc_{i}", [128, 4, VD], F32, ls) for i in range(2)]
                rl = sb("rl", [128, 8], F32, ls)
                osb = sb("osb", [128, VD], F32, ls)
                junk = sb("junk", [128, VD], F32, ls)
                ss = sb("ss", [128, 4], F32, ls)
                on = sb("on", [128, VD], BF16, ls)
                ostage = [sb(f"ostage{i}", [128, 2, 512], BF16, ls) for i in range(2)]
                lamv = sb("lamv", [128, 8], F32, ls)
                lamc = sb("lamc", [128, 2], F32, ls)
                subl = sb("subl", [128, VD], F32, ls)
                pS = [ps(f"pS{i}", [128, 512], F32, ls) for i in range(2)]
                acc = [ps(f"acc{i}", [128, 512], F32, ls) for i in range(4)]
                ptr = ps("ptr_a", [128, 1024], BF16, ls)
                plam = ps("plam", [128, 512], F32, ls)
                R_q = [S.R("qT2", i) for i in range(2)]
                R_v = [S.R("vx", i) for i in range(2)]
                R_pT = [S.R("pT", i) for i in range(3)]
                R_oc = [S.R("oc", i) for i in range(2)]
                R_pS = [S.R("pS", i) for i in range(2)]
                R_acc = [S.R("acc", i) for i in range(4)]
                R_fin = S.R("fin")
                R_on = S.R("on")
                R_ptr = S.R("ptr_a")
                R_ost = [S.R("ostage", i) for i in range(2)]
                R_lam = S.R("lam")
                R_subl = S.R("subl")
                for j, src in enumerate((lam_q1, lam_k1, lam_q2, lam_k2)):
                    S.dma("sp", lambda e, j=j, src=src: e.dma_start(
                        out=lamv[:, j:j + 1], in_=src[l, :].rearrange("(p o) -> p o", o=1)),
                        "ld_small", writes=[R_lam], accumulate_writers=(j > 0))
                S.dma("sp", lambda e: e.dma_start(out=lamc[:, :], in_=c_laminit[l, :].partition_broadcast(128)),
                      "ld_small", writes=[R_lam], accumulate_writers=True)
                S.dma("sp", lambda e: e.dma_start(out=subl[:, :], in_=attn_subln[l, :].partition_broadcast(128)),
                      "ld_small", writes=[R_subl])
                S.op("dve", lambda e: e.tensor_tensor(out=lamv[:, 4:5], in0=lamv[:, 0:1], in1=lamv[:, 1:2],
                                                      op=ALU.mult), reads=[R_lam], writes=[R_lam])
                S.op("dve", lambda e: e.tensor_tensor(out=lamv[:, 5:6], in0=lamv[:, 2:3], in1=lamv[:, 3:4],
                                                      op=ALU.mult), reads=[R_lam], writes=[R_lam])
                onesf = sb("onesf", [128, 128], F32, ls)
                S.op("dve", lambda e: e.memset(onesf[:], 1.0), writes=[R_lam])
                S.op("pe", lambda e: e.matmul(plam[:, 0:2], onesf[:, :], lamv[:, 4:6], start=True, stop=True),
                     reads=[R_lam], writes=[S.R("plam")])
                S.op("act", lambda e: e.activation(out=lamv[:, 6:8], in_=plam[:, 0:2], func=AF.Exp),
                     reads=[S.R("plam")], writes=[R_lam])
                S.op("dve", lambda e: e.tensor_tensor(out=lamv[:, 4:5], in0=lamv[:, 7:8], in1=lamv[:, 6:7],
                                                      op=ALU.subtract), reads=[R_lam], writes=[R_lam])
                S.op("dve", lambda e: e.tensor_tensor(out=lamv[:, 4:5], in0=lamv[:, 4:5], in1=lamc[:, 0:1],
                                                      op=ALU.subtract), reads=[R_lam], writes=[R_lam])
                S.op("dve", lambda e: e.tensor_scalar(out=subl[:, :], in0=subl[:, :], scalar1=lamc[:, 1:2],
                                                      scalar2=None, op0=ALU.mult),
                     reads=[R_lam, R_subl], writes=[R_subl])
                for i in range(2):
                    S.op("dve", lambda e, i=i: e.memset(vx[i][:, :, VD:VD + 1], 1.0), writes=[R_v[i]])

                def load_head(h):
                    b = h % 2
                    S.dma("sp", lambda e, b=b, h=h: e.dma_start(
                        out=qT2[b][:, :, :], in_=qT_d[2 * h:2 * h + 2, :, :].rearrange("c p t -> p c t")),
                        f"ld_q{b}", writes=[R_q[b]])
                    S.dma("sp", lambda e, b=b, h=h: e.dma_start(
                        out=kT2[b][:, :, :], in_=kT_d[2 * h:2 * h + 2, :, :].rearrange("c p t -> p c t")),
                        f"ld_q{b}", writes=[R_q[b]], accumulate_writers=True)
                    for j0 in range(0, NB, 8):
                        j1 = min(NB, j0 + 8)
                        S.dma("sp", lambda e, b=b, h=h, j0=j0, j1=j1: e.dma_start(
                            out=vx[b][:, j0:j1, 0:VD],
                            in_=v_d[j0 * 128:j1 * 128, h * VD:(h + 1) * VD].rearrange("(j p) e -> p j e", p=128)),
                            f"ld_v{b}", reads=[], writes=[R_v[b]], accumulate_writers=True)

                load_head(0)
                ip = 0
                io = 0
                scale = HD ** -0.5
                for h in range(NH):
                    b = h % 2
                    if h + 1 < NH:
                        load_head(h + 1)
                    for (qs, qn) in TT:
                        nsb = (qn + 127) // 128
                        last_blk = (qs + qn - 1) // 128
                        for c in range(2):
                            units = []
                            for j in range(last_blk + 1):
                                ks, kn = TB[j]
                                q_lo = max(qs, ks) - qs
                                units.append((j, ks, kn, q_lo, ip % 2, ip % 3))
                                ip += 1

                            def emit_S(u, b=b, c=c, qs=qs, qn=qn):
                                (j, ks, kn, q_lo, p2, p3) = u
                                S.op("pe", lambda e: e.matmul(
                                    pS[p2][:kn, q_lo:qn], kT2[b][:, c, ks:ks + kn], qT2[b][:, c, qs + q_lo:qs + qn],
                                    start=True, stop=True), reads=[R_q[b]], writes=[R_pS[p2]])

                            emit_S(units[0])
                            for ui, u in enumerate(units):
                                (j, ks, kn, q_lo, p2, p3) = u
                                w = qn - q_lo
                                if ui + 1 < len(units):
                                    emit_S(units[ui + 1])
                                S.op("act", lambda e, kn=kn, q_lo=q_lo, qn=qn, p2=p2, p3=p3: e.activation(
                                    out=pT[p3][:kn, q_lo:qn], in_=pS[p2][:kn, q_lo:qn], func=AF.Exp, scale=scale),
                                    reads=[R_pS[p2]], writes=[R_pT[p3]])
                                if ks >= qs:
                                    mw = min(128, w)
                                    S.op("dve", lambda e, kn=kn, q_lo=q_lo, mw=mw, p3=p3: e.tensor_tensor(
                                        out=pT[p3][:kn, q_lo:q_lo + mw], in0=pT[p3][:kn, q_lo:q_lo + mw],
                                        in1=maskt[:kn, :mw], op=ALU.mult),
                                        reads=[R_pT[p3], R_const], writes=[R_pT[p3]])
                                for sbi in range(nsb):
                                    s0 = sbi * 128
                                    if s0 < q_lo:
                                        continue
                                    sn = min(128, qn - s0)
                                    diag = (qs + s0) // 128
                                    S.op("pe", lambda e, b=b, kn=kn, s0=s0, sn=sn, j=j, sbi=sbi, p3=p3, diag=diag: e.matmul(
                                        acc[sbi][:sn, 0:VD + 1], pT[p3][:kn, s0:s0 + sn], vx[b][:kn, j, :],
                                        start=(j == 0), stop=(j == diag)),
                                        reads=[R_pT[p3], R_v[b]], writes=[R_acc[sbi]])
                            for sbi in range(nsb):
                                sn = min(128, qn - sbi * 128)
                                S.op("dve", lambda e, sbi=sbi, sn=sn, c=c: e.reciprocal(
                                    out=rl[:sn, c * 4 + sbi:c * 4 + sbi + 1], in_=acc[sbi][:sn, VD:VD + 1]),
                                    reads=[R_acc[sbi]], writes=[R_fin])
                                S.op("dve", lambda e, sbi=sbi, sn=sn, c=c: e.tensor_scalar(
                                    out=oc[c][:sn, sbi, :], in0=acc[sbi][:sn, 0:VD],
                                    scalar1=rl[:sn, c * 4 + sbi:c * 4 + sbi + 1], scalar2=None, op0=ALU.mult),
                                    reads=[R_acc[sbi], R_fin], writes=[R_oc[c]])
                        o2 = io % 2
                        io += 1
                        for sbi in range(nsb):
                            s0 = sbi * 128
                            sn = min(128, qn - s0)
                            S.op("dve", lambda e, sbi=sbi, sn=sn: e.scalar_tensor_tensor(
                                out=osb[:sn, :], in0=oc[1][:sn, sbi, :], scalar=lamv[:sn, 4:5], in1=oc[0][:sn, sbi, :],
                                op0=ALU.mult, op1=ALU.add), reads=[R_oc[0], R_oc[1], R_lam], writes=[R_fin])
                            S.op("dve", lambda e: e.memset(ss[:, 0:1], 0.0), writes=[S.R("ss")])
                            S.op("act", lambda e, sn=sn, sbi=sbi: e.activation(
                                out=junk[:sn, :], in_=osb[:sn, :], func=AF.Square, accum_out=ss[:sn, 0:1]),
                                reads=[R_fin], writes=[S.R("ss")])
                            S.op("act", lambda e, sn=sn: e.activation(
                                out=ss[:sn, 1:2], in_=ss[:sn, 0:1], func=AF.Sqrt, bias=eps_t[:sn, :], scale=1.0 / VD),
                                reads=[S.R("ss"), R_const], writes=[S.R("ss")])
                            S.op("dve", lambda e, sn=sn: e.reciprocal(out=ss[:sn, 2:3], in_=ss[:sn, 1:2]),
                                 reads=[S.R("ss")], writes=[S.R("ss")])
                            S.op("dve", lambda e, sn=sn: e.scalar_tensor_tensor(
                                out=on[:sn, :], in0=osb[:sn, :], scalar=ss[:sn, 2:3], in1=subl[:sn, :],
                                op0=ALU.mult, op1=ALU.mult), reads=[R_fin, S.R("ss"), R_subl], writes=[R_on])
                            for e2 in range(2):
                                S.op("pe", lambda e, e2=e2, sn=sn: e.transpose(
                                    ptr[:, e2 * 128:e2 * 128 + sn], on[:sn, e2 * 128:(e2 + 1) * 128], ident[:sn, :sn]),
                                    reads=[R_on, R_const], writes=[R_ptr])
                            ptv = ptr[:, 0:256].rearrange("p (a t) -> p a t", a=2)
                            S.op("act", lambda e, ptv=ptv, s0=s0, sn=sn, o2=o2: e.activation(
                                out=ostage[o2][:, :, s0:s0 + sn], in_=ptv[:, :, :sn], func=AF.Copy),
                                reads=[R_ptr], writes=[R_ost[o2]])
                        S.dma("sp", lambda e, h=h, qs=qs, qn=qn, o2=o2: e.dma_start(
                            out=aoT_d[h * VD:(h + 1) * VD, qs:qs + qn].rearrange("(a p) t -> p a t", p=128),
                            in_=ostage[o2][:, :, 0:qn]), f"st_o{o2}", reads=[R_ost[o2]],
                            writes=[S.R("aoT_d", h, qs)])
                S.barrier()

        def lru(l):
            with ExitStack() as ls:
                NP = len(GATE_PAIRS)
                wexp = sb("wexp", [128, 2 * 4 * NP, 128], BF16, ls)
                wstg = sb("wstg", [128, NP, 128], F32, ls)
                cw = sb("cw", [128, 4, KL], F32, ls)
                cb = sb("cb", [128, KL], F32, ls)
                gab = sb("gab", [128, KL], F32, ls)
                gxb = sb("gxb", [128, KL], F32, ls)
                ca = sb("ca", [128, KL], F32, ls)
                carry = sb("carry", [128, KL], F32, ls)
                xin = [sb(f"xin{i}", [128, 5, 3 + 512], F32, ls) for i in range(2)]
                xc = sb("xc", [128, 5, 512], F32, ls)
                xcb = sb("xcb", [128, 5, 512], BF16, ls)
                tr = sb("tr", [128, 5, 512], F32, ls)
                ti = sb("ti", [128, 5, 512], F32, ls)
                ta = sb("ta", [128, 5, 512], F32, ls)
                th = sb("th", [128, 5, 512], F32, ls)
                ty = sb("ty", [128, 5, 512], F32, ls)
                tg = sb("tg", [128, 5, 512], F32, ls)
                thy = sb("thy", [128, 5, 512], BF16, ls)
                pg = [ps(f"pgate{i}", [128, 512], F32, ls) for i in range(4)]
                R_w = S.R("wexp")
                R_ws = S.R("wstg")
                R_par = S.R("lrupar")
                R_carry = S.R("carry")
                R_xin = [S.R("xin", i) for i in range(2)]
                R_xc = S.R("xc")
                R_xcb = S.R("xcb")
                R_pg = [S.R("pgate", i) for i in range(4)]
                R_tr, R_ti, R_ta, R_th, R_ty, R_tg, R_thy = (S.R("lt", n) for n in ("r", "i", "a", "h", "y", "g", "hy"))
                for j in range(4):
                    S.dma("sp", lambda e, j=j: e.dma_start(
                        out=cw[:, j, :], in_=conv_w[l, j, :].rearrange("(c p) -> p c", p=128),
                        allow_slow_non_contiguous=True), "ld_small", writes=[R_par], accumulate_writers=(j > 0))
                for dst, src in ((cb, conv_b), (gab, gate_a_b), (gxb, gate_x_b), (ca, lru_a)):
                    S.dma("sp", lambda e, dst=dst, src=src: e.dma_start(
                        out=dst[:, :], in_=src[l, :].rearrange("(c p) -> p c", p=128),
                        allow_slow_non_contiguous=True), "ld_small", writes=[R_par], accumulate_writers=True)
                S.op("act", lambda e: e.activation(out=ca[:, :], in_=ca[:, :], func=AF.Exp, scale=-1.0),
                     reads=[R_par], writes=[R_par])
                S.op("act", lambda e: e.activation(out=ca[:, :], in_=ca[:, :], func=AF.Ln, bias=one_t[:, :], scale=1.0),
                     reads=[R_par, R_const], writes=[R_par])
                S.op("dve", lambda e: e.tensor_scalar(out=ca[:, :], in0=ca[:, :], scalar1=-8.0, scalar2=None,
                                                      op0=ALU.mult), reads=[R_par], writes=[R_par])
                S.op("dve", lambda e: e.memset(carry[:, :], 0.0), writes=[R_carry])
                for gi, gw in enumerate((gate_a_w, gate_x_w)):
                    for grp in range(4):
                        S.op("dve", lambda e: e.memset(wstg[:], 0.0), writes=[R_ws])
                        for bl in range(4):
                            n = grp * 4 + bl
                            base = 160 * bl
                            for (ci, r0, r1) in ((bl, base, 128 * (bl + 1)), (bl + 1, 128 * (bl + 1), base + 160)):
                                for (co, c0, c1) in ((bl, base, 128 * (bl + 1)), (bl + 1, 128 * (bl + 1), base + 160)):
                                    idx = GATE_PAIRS.index((ci, co))
                                    S.dma("sp", lambda e, idx=idx, gw=gw, n=n, r0=r0, r1=r1, c0=c0, c1=c1, ci=ci, co=co, base=base: e.dma_start(
                                        out=wstg[r0 - 128 * ci:r1 - 128 * ci, idx, c0 - 128 * co:c1 - 128 * co],
                                        in_=gw[l, n, r0 - base:r1 - base, c0 - base:c1 - base]),
                                        "ld_gw", reads=[R_ws], writes=[R_ws], accumulate_writers=True)
                        i0 = (gi * 4 + grp) * NP
                        S.op("dve", lambda e, i0=i0: e.tensor_copy(out=wexp[:, i0:i0 + NP, :], in_=wstg[:]),
                             reads=[R_ws], writes=[R_w, R_ws])
                for i in range(2):
                    S.op("dve", lambda e, i=i: e.memset(xin[i][:, :, 0:3], 0.0), writes=[R_xin[i]])
                it = 0
                ic = 0
                for (ts, tn) in TT:
                    for grp in range(4):
                        x2 = it % 2
                        it += 1
                        g5 = grp * 5
                        rows = xrT_d[grp * 640:(grp + 1) * 640, :].rearrange("(c p) t -> p c t", p=128)
                        yrows = yrT_d[grp * 640:(grp + 1) * 640, :].rearrange("(c p) t -> p c t", p=128)
                        hrows = hyT_d[grp * 640:(grp + 1) * 640, :].rearrange("(c p) t -> p c t", p=128)
                        if ts == 0:
                            S.dma("sp", lambda e, x2=x2, rows=rows, tn=tn: e.dma_start(
                                out=xin[x2][:, :, 3:3 + tn], in_=rows[:, :, 0:tn]), f"ld_xin{x2}", writes=[R_xin[x2]])
                        else:
                            S.dma("sp", lambda e, x2=x2, rows=rows, ts=ts, tn=tn: e.dma_start(
                                out=xin[x2][:, :, 0:3 + tn], in_=rows[:, :, ts - 3:ts + tn]), f"ld_xin{x2}",
                                writes=[R_xin[x2]])
                        S.dma("sp", lambda e, yrows=yrows, ts=ts, tn=tn: e.dma_start(
                            out=ty[:, :, :tn], in_=yrows[:, :, ts:ts + tn]), "ld_ty0", writes=[R_ty])
                        for ci in range(5):
                            ch = g5 + ci
                            S.op("dve", lambda e, x2=x2, ci=ci, ch=ch, tn=tn: e.tensor_scalar(
                                out=xc[:, ci, :tn], in0=xin[x2][:, ci, 3:3 + tn], scalar1=cw[:, 0, ch:ch + 1],
                                scalar2=cb[:, ch:ch + 1], op0=ALU.mult, op1=ALU.add),
                                reads=[R_xin[x2], R_par], writes=[R_xc])
                            for j in range(1, 4):
                                S.op("dve", lambda e, x2=x2, ci=ci, ch=ch, tn=tn, j=j: e.scalar_tensor_tensor(
                                    out=xc[:, ci, :tn], in0=xin[x2][:, ci, 3 - j:3 - j + tn], scalar=cw[:, j, ch:ch + 1],
                                    in1=xc[:, ci, :tn], op0=ALU.mult, op1=ALU.add),
                                    reads=[R_xin[x2], R_par, R_xc], writes=[R_xc])
                        S.op("act", lambda e, tn=tn: e.activation(out=xcb[:, :, :tn], in_=xc[:, :, :tn], func=AF.Copy),
                             reads=[R_xc], writes=[R_xcb])
                        S.op("dve", lambda e, tn=tn: e.tensor_tensor(
                            out=tg[:, :, :tn], in0=ty[:, :, :tn], in1=ty[:, :, :tn], op=ALU.mult),
                            reads=[R_ty], writes=[R_tg])
                        S.op("dve", lambda e, tn=tn: e.tensor_scalar(
                            out=tg[:, :, :tn], in0=tg[:, :, :tn], scalar1=0.044715, scalar2=1.0,
                            op0=ALU.mult, op1=ALU.add), reads=[R_tg], writes=[R_tg])
                        S.op("dve", lambda e, tn=tn: e.tensor_tensor(
                            out=tg[:, :, :tn], in0=tg[:, :, :tn], in1=ty[:, :, :tn], op=ALU.mult),
                            reads=[R_tg, R_ty], writes=[R_tg])
                        for co in range(5):
                            ch = g5 + co
                            pa = (2 * ic) % 4
                            px = (2 * ic + 1) % 4
                            ic += 1
                            nb = GATE_NBRS[co]
                            for gi, pp in ((0, pa), (1, px)):
                                for n_i, ci in enumerate(nb):
                                    idx = (gi * 4 + grp) * NP + GATE_PAIRS.index((ci, co))
                                    S.op("pe", lambda e, pp=pp, idx=idx, ci=ci, tn=tn, n_i=n_i, nb=nb: e.matmul(
                                        pg[pp][:, :tn], wexp[:, idx, :], xcb[:, ci, :tn],
                                        start=(n_i == 0), stop=(n_i == len(nb) - 1)),
                                        reads=[R_w, R_xcb], writes=[R_pg[pp]])
                            S.op("act", lambda e, co=co, pa=pa, ch=ch, tn=tn: e.activation(
                                out=tr[:, co, :tn], in_=pg[pa][:, :tn], func=AF.Sigmoid, bias=gab[:, ch:ch + 1]),
                                reads=[R_pg[pa], R_par], writes=[R_tr])
                            S.op("act", lambda e, co=co, px=px, ch=ch, tn=tn: e.activation(
                                out=ti[:, co, :tn], in_=pg[px][:, :tn], func=AF.Sigmoid, bias=gxb[:, ch:ch + 1]),
                                reads=[R_pg[px], R_par], writes=[R_ti])
                            S.op("act", lambda e, co=co, ch=ch, tn=tn: e.activation(
                                out=ta[:, co, :tn], in_=tr[:, co, :tn], func=AF.Exp, scale=ca[:, ch:ch + 1]),
                                reads=[R_tr, R_par], writes=[R_ta])
                        S.op("act", lambda e, tn=tn: e.activation(
                            out=tg[:, :, :tn], in_=tg[:, :, :tn], func=AF.Sigmoid, scale=1.5957691216057308),
                            reads=[R_tg], writes=[R_tg])
                        S.op("dve", lambda e, tn=tn: e.tensor_tensor(
                            out=ti[:, :, :tn], in0=ti[:, :, :tn], in1=xc[:, :, :tn], op=ALU.mult),
                            reads=[R_ti, R_xc], writes=[R_ti])
                        S.op("dve", lambda e, tn=tn: e.tensor_tensor(
                            out=tr[:, :, :tn], in0=ta[:, :, :tn], in1=ta[:, :, :tn], op=ALU.mult),
                            reads=[R_ta, R_tr], writes=[R_tr])
                        S.op("dve", lambda e, tn=tn: e.tensor_scalar(
                            out=tr[:, :, :tn], in0=tr[:, :, :tn], scalar1=-1.0, scalar2=1.0,
                            op0=ALU.mult, op1=ALU.add), reads=[R_tr], writes=[R_tr])
                        S.op("act", lambda e, tn=tn: e.activation(
                            out=tr[:, :, :tn], in_=tr[:, :, :tn], func=AF.Sqrt), reads=[R_tr], writes=[R_tr])
                        S.op("dve", lambda e, tn=tn: e.tensor_tensor(
                            out=tg[:, :, :tn], in0=tg[:, :, :tn], in1=ty[:, :, :tn], op=ALU.mult),
                            reads=[R_tg, R_ty], writes=[R_tg])
                        S.op("dve", lambda e, tn=tn: e.tensor_tensor(
                            out=ti[:, :, :tn], in0=ti[:, :, :tn], in1=tr[:, :, :tn], op=ALU.mult),
                            reads=[R_ti, R_tr], writes=[R_ti])
                        for co in range(5):
                            ch = g5 + co
                            S.op("dve", lambda e, co=co, ch=ch, tn=tn: e.tensor_tensor_scan(
                                out=th[:, co, :tn], data0=ta[:, co, :tn], data1=ti[:, co, :tn],
                                initial=carry[:, ch:ch + 1], op0=ALU.mult, op1=ALU.add),
                                reads=[R_ta, R_ti, R_carry], writes=[R_th])
                        for co in range(5):
                            ch = g5 + co
                            S.op("dve", lambda e, co=co, ch=ch, tn=tn: e.tensor_copy(
                                out=carry[:, ch:ch + 1], in_=th[:, co, tn - 1:tn]),
                                reads=[R_th], writes=[R_carry])
                        S.op("dve", lambda e, tn=tn: e.tensor_tensor(
                            out=thy[:, :, :tn], in0=tg[:, :, :tn], in1=th[:, :, :tn], op=ALU.mult),
                            reads=[R_tg, R_th], writes=[R_thy])
                        S.dma("sp", lambda e, hrows=hrows, ts=ts, tn=tn: e.dma_start(
                            out=hrows[:, :, ts:ts + tn], in_=thy[:, :, :tn]),
                            "st_hy0", reads=[R_thy], writes=[S.R("hyT_d", grp, ts)])
                S.barrier()

        def mixer_out(l):
            for si, stoks in enumerate(ST):
                mixer_out_body(l, si, stoks)

        def mixer_out_body(l, si, stoks):
            if True:
                t0 = stoks[0][0]
                ntok = sum(n for _, n in stoks)
                with ExitStack() as ls:
                    aoT = sb("aoT", [128, KD, STMAX], BF16, ls)
                    hyT = sb("hyT", [128, KL, STMAX], BF16, ls)
                    mT = sb("mT", [128, KD, STMAX], BF16, ls)
                    wa = [sb(f"wba{i}", [128, KD, 256], BF16, ls) for i in range(2)]
                    wl = [sb(f"wbl{i}", [128, KL, 256], BF16, ls) for i in range(2)]
                    gA = [sb(f"gA{i}", [128, 512], BF16, ls) for i in range(2)]
                    gL = [sb(f"gL{i}", [128, 512], BF16, ls) for i in range(2)]
                    m1 = [sb(f"m1_{i}", [128, 512], F32, ls) for i in range(2)]
                    m2 = [sb(f"m2_{i}", [128, 512], F32, ls) for i in range(2)]
                    hr = [sb(f"hrm{i}", [128, 512], F32, ls) for i in range(2)]
                    hn = [sb(f"hnm{i}", [128, 512], F32, ls) for i in range(2)]
                    pa = [ps(f"pba{i}", [128, 512], F32, ls) for i in range(2)]
                    pl = [ps(f"pbl{i}", [128, 512], F32, ls) for i in range(2)]
                    po = [ps(f"pwo{i}", [128, 512], F32, ls) for i in range(2)]
                    R_in = S.R("m4in")
                    R_mT = S.R("mT")
                    R_w = [S.R("wbr", i) for i in range(2)]
                    R_g = [S.R("gAL", i) for i in range(2)]
                    R_m = [S.R("m12", i) for i in range(2)]
                    R_hr = [S.R("hrm", i) for i in range(2)]
                    R_hn = [S.R("hnm", i) for i in range(2)]
                    R_pa = [S.R("pba", i) for i in range(2)]
                    R_pl = [S.R("pbl", i) for i in range(2)]
                    R_po = [S.R("pwo", i) for i in range(2)]
                    for c0 in range(0, KD, 4):
                        S.dma("sp", lambda e, c0=c0: e.dma_start(
                            out=aoT[:, c0:c0 + 4, 0:ntok],
                            in_=aoT_d[c0 * 128:(c0 + 4) * 128, t0:t0 + ntok].rearrange("(c p) t -> p c t", p=128)),
                            "ld_m4", writes=[R_in], accumulate_writers=(c0 > 0))
                    for c0 in range(0, KL, 4):
                        S.dma("sp", lambda e, c0=c0: e.dma_start(
                            out=hyT[:, c0:c0 + 4, 0:ntok],
                            in_=hyT_d[c0 * 128:(c0 + 4) * 128, t0:t0 + ntok].rearrange("(c p) t -> p c t", p=128)),
                            "ld_m4", writes=[R_in], accumulate_writers=True)

                    def ldw(dg):
                        s = dg % 2
                        wtile(wa[s], "w_ba", l, dg, f"ld_w{s}", R_w[s])
                        wtile(wl[s], "w_bl", l, dg, f"ld_w{s}", R_w[s], acc=True)

                    ldw(0)
                    ldw(1)
                    it = 0
                    for dg in range(KD // 2):
                        s = dg % 2
                        off = 0
                        for (ts, tn) in stoks:
                            for dd in range(2):
                                d = 2 * dg + dd
                                p = it % 2
                                it += 1
                                S.dma("sp", lambda e, p=p, d=d, ts=ts, tn=tn: e.dma_start(
                                    out=gA[p][:, :tn], in_=gT_d[d * 128:(d + 1) * 128, ts:ts + tn]),
                                    f"ld_g{p}", writes=[R_g[p]])
                                S.dma("sp", lambda e, p=p, d=d, ts=ts, tn=tn: e.dma_start(
                                    out=gL[p][:, :tn], in_=gT_d[D + d * 128:D + (d + 1) * 128, ts:ts + tn]),
                                    f"ld_g{p}", writes=[R_g[p]], accumulate_writers=True)
                                for kc in range(KD):
                                    S.op("pe", lambda e, p=p, s=s, kc=kc, dd=dd, off=off, tn=tn: e.matmul(
                                        pa[p][:, :tn], wa[s][:, kc, dd * 128:(dd + 1) * 128], aoT[:, kc, off:off + tn],
                                        start=(kc == 0), stop=(kc == KD - 1)), reads=[R_w[s], R_in], writes=[R_pa[p]])
                                for kc in range(KL):
                                    S.op("pe", lambda e, p=p, s=s, kc=kc, dd=dd, off=off, tn=tn: e.matmul(
                                        pl[p][:, :tn], wl[s][:, kc, dd * 128:(dd + 1) * 128], hyT[:, kc, off:off + tn],
                                        start=(kc == 0), stop=(kc == KL - 1)), reads=[R_w[s], R_in], writes=[R_pl[p]])
                                S.op("dve", lambda e, p=p, tn=tn: e.tensor_tensor(
                                    out=m1[p][:, :tn], in0=pa[p][:, :tn], in1=gA[p][:, :tn], op=ALU.mult),
                                    reads=[R_pa[p], R_g[p]], writes=[R_m[p]])
                                S.op("dve", lambda e, p=p, tn=tn: e.tensor_tensor(
                                    out=m2[p][:, :tn], in0=pl[p][:, :tn], in1=gL[p][:, :tn], op=ALU.mult),
                                    reads=[R_pl[p], R_g[p]], writes=[R_m[p]])
                                S.op("dve", lambda e, p=p, d=d, off=off, tn=tn: e.tensor_tensor(
                                    out=mT[:, d, off:off + tn], in0=m1[p][:, :tn], in1=m2[p][:, :tn], op=ALU.add),
                                    reads=[R_m[p]], writes=[R_mT])
                            off += tn
                        if dg + 2 < KD // 2:
                            ldw(dg + 2)
                    def ldo(dg):
                        s = dg % 2
                        wtile(wa[s], "w_o", l, dg, f"ld_w{s}", R_w[s])

                    ldo(0)
                    ldo(1)
                    it = 0
                    for dg in range(KD // 2):
                        s = dg % 2
                        off = 0
                        for (ts, tn) in stoks:
                            for dd in range(2):
                                d = 2 * dg + dd
                                p = it % 2
                                it += 1
                                S.dma("sp", lambda e, p=p, d=d, ts=ts, tn=tn: e.dma_start(
                                    out=hr[p][:, :tn], in_=hT[d * 128:(d + 1) * 128, ts:ts + tn]),
                                    f"ld_hr{p}", reads=[Rh(d, ts)], writes=[R_hr[p]])
                                for kc in range(KD):
                                    S.op("pe", lambda e, p=p, s=s, kc=kc, dd=dd, off=off, tn=tn: e.matmul(
                                        po[p][:, :tn], wa[s][:, kc, dd * 128:(dd + 1) * 128], mT[:, kc, off:off + tn],
                                        start=(kc == 0), stop=(kc == KD - 1)), reads=[R_w[s], R_mT], writes=[R_po[p]])
                                S.op("dve", lambda e, p=p, tn=tn: e.tensor_tensor(
                                    out=hn[p][:, :tn], in0=po[p][:, :tn], in1=hr[p][:, :tn], op=ALU.add),
                                    reads=[R_po[p], R_hr[p]], writes=[R_hn[p]])
                                S.dma("sp", lambda e, p=p, d=d, ts=ts, tn=tn: e.dma_start(
                                    out=hT[d * 128:(d + 1) * 128, ts:ts + tn], in_=hn[p][:, :tn]),
                                    f"st_hn{p}", reads=[R_hn[p]], writes=[Rh(d, ts)])
                            off += tn
                        if dg + 2 < KD // 2:
                            ldo(dg + 2)
                    S.barrier()

        precast_all()
        import os
        PH = os.environ.get("K_PHASES", "f1,proj,attn,lru,out,f2").split(",")
        wrote_h = False
        for l in range(DEPTH):
            if "f1" in PH:
                ffn(l, norm_ffn1, "ffn1", "f1", hin=(h0T if l == 0 else hT))
                wrote_h = True
            if "proj" in PH:
                mixer_proj(l)
            if "attn" in PH:
                attention(l)
            if "lru" in PH:
                lru(l)
            if "out" in PH:
                mixer_out(l)
            if "f2" in PH:
                ffn(l, norm_ffn2, "ffn2", "f2")
        for si, stoks in enumerate(ST):
            with ExitStack() as ls:
                rms_to_uT(ls, stoks, final_norm[0, :], None, None, "fin", out_dram=outT,
                          hin=(hT if wrote_h else h0T))
        S.barrier()
        S.emit()
    return nc


def const_tables(T):
    NB = (T + 127) // 128
    inv_freq = (THETA ** (-np.arange(0, ROT, 2, dtype=np.float32) / ROT)).astype(np.float32)
    ang = np.arange(NB * 128, dtype=np.float32)[:, None] * inv_freq[None, :]
    cos = np.tile(np.cos(ang).astype(np.float32), (1, 4))
    sin = np.tile(np.sin(ang).astype(np.float32), (1, 4))
    mask = np.triu(np.ones((128, 128), np.float32))
    ident = np.eye(128, dtype=np.float32)
    return cos, sin, mask, ident


_NC_CACHE = {}


def run_layers(h0T_list, inputs, T, DEPTH, debug=False):
    key = (T, DEPTH, debug)
    if key not in _NC_CACHE:
        _NC_CACHE[key] = build(T, DEPTH, debug)
    nc = _NC_CACHE[key]
    cos, sin, mask, ident = const_tables(T)
    import ml_dtypes
    lam = np.array([[lam_init_of(l), 1.0 - lam_init_of(l)] for l in range(DEPTH)], np.float32)
    shared = {k: np.ascontiguousarray(v) for k, v in inputs.items() if k not in ("x", "meta_tokens")}
    shared["final_norm"] = np.ascontiguousarray(inputs["final_norm"]).reshape(1, D)
    shared.update(c_cos=cos, c_sin=sin, c_mask=mask.astype(ml_dtypes.bfloat16),
                  c_ident=ident.astype(ml_dtypes.bfloat16), c_laminit=lam)
    in_maps = []
    for h0T in h0T_list:
        m = dict(shared)
        m["h0T"] = h0T
        in_maps.append(m)
    res = run_bass_kernel_spmd(nc, in_maps, core_ids=list(range(len(in_maps))))
    return res


def kernel(**inputs):
    x = np.asarray(inputs["x"])
    meta = np.asarray(inputs["meta_tokens"])
    B, SEQ, _ = x.shape
    T = SEQ + NMETA
    DEPTH = np.asarray(inputs["w_in"]).shape[0]
    h0T_list = [np.ascontiguousarray(np.concatenate([meta, x[b]], axis=0).T) for b in range(B)]
    res = run_layers(h0T_list, {k: np.asarray(v) for k, v in inputs.items()}, T, DEPTH)
    out = np.stack([np.ascontiguousarray(r["outT"].T[NMETA:]) for r in res.results], axis=0)
    return out.astype(np.float32)
```
